# Optimizing a Trainium2 kernel written in Bass

```python
import math
import jax, jax.numpy as jnp
from jax import lax
import numpy as np

D_MODEL = 1024
BATCH = 1
SEQ = 16384
DEPTH = 4

N_MIXERS = 4
RMS_EPS = 1e-6
D_FF = ((8 * D_MODEL // 3 + 255) // 256) * 256
DA_HEAD_DIM = 64
DA_HEADS = D_MODEL // (2 * DA_HEAD_DIM)
DA_QBLOCK = 128
RT_HEADS = 4
RT_KEY_DIM = D_MODEL // RT_HEADS
RT_VAL_DIM = 2 * RT_KEY_DIM
RT_CHUNK = 128
SG_WIDTH = 3 * D_MODEL
SG_GROUPS = 8
SG_CHUNK = 128
LRU_WIDTH = ((4 * D_MODEL // 3 + 255) // 256) * 256
LRU_BLOCK = 128
LRU_BLOCKS = LRU_WIDTH // LRU_BLOCK
CONV_WIDTH = 4
RG_C = 8.0

kernel_name = "hybrid_interleaved_macaron_trunk"


def _rms(x, gain):
    x32 = x.astype(jnp.float32)
    y = x32 * lax.rsqrt(jnp.mean(x32 * x32, axis=-1, keepdims=True) + RMS_EPS)
    return (y * gain.astype(jnp.float32)).astype(x.dtype)


def _swiglu(xn, w_in, w_out):
    a, b = jnp.split(xn @ w_in, 2, axis=-1)
    return (jax.nn.silu(a) * b) @ w_out


def _lambda_init(layer_idx):
    return 0.8 - 0.6 * math.exp(-0.3 * layer_idx)


def _diff_attention(xn, w_in, q_gain, k_gain, lam, subln, w_out, lam_init):
    B, T, _ = xn.shape
    H, Dh = DA_HEADS, DA_HEAD_DIM
    q, k, v = jnp.split(xn @ w_in, 3, axis=-1)
    q = _rms(q.reshape(B, T, H, 2, Dh), q_gain) * (Dh ** -0.5)
    k = _rms(k.reshape(B, T, H, 2, Dh), k_gain)
    q = q.transpose(0, 2, 3, 1, 4)
    k = k.transpose(0, 2, 3, 1, 4)
    v = v.reshape(B, T, H, 2 * Dh).transpose(0, 2, 1, 3)
    lam32 = lam.astype(jnp.float32)
    lmbda = (jnp.exp(jnp.sum(lam32[0] * lam32[1])) - jnp.exp(jnp.sum(lam32[2] * lam32[3]))
             + lam_init)
    slopes = 2.0 ** (-8.0 * jnp.arange(1, H + 1, dtype=jnp.float32) / H)
    nb = T // DA_QBLOCK
    q_blocks = q.reshape(B, H, 2, nb, DA_QBLOCK, Dh).transpose(3, 0, 1, 2, 4, 5)
    key_pos = jnp.arange(T)

    def block(args):
        qb, n = args
        q_pos = n * DA_QBLOCK + jnp.arange(DA_QBLOCK)
        s = jnp.einsum('bhmqd,bhmkd->bhmqk', qb, k, preferred_element_type=jnp.float32)
        dist = (q_pos[:, None] - key_pos[None, :]).astype(jnp.float32)
        bias = jnp.where(dist >= 0, -slopes[:, None, None] * dist, -jnp.inf)
        p = jax.nn.softmax(s + bias[None, :, None], axis=-1)
        a = p[:, :, 0] - lmbda * p[:, :, 1]
        return jnp.einsum('bhqk,bhkd->bhqd', a.astype(v.dtype), v)

    o = lax.map(block, (q_blocks, jnp.arange(nb)))
    o = o.transpose(1, 0, 3, 2, 4).reshape(B, T, H, 2 * Dh)
    o = _rms(o, subln) * (1.0 - lam_init)
    return o.reshape(B, T, H * 2 * Dh).astype(xn.dtype) @ w_out


def _retention(xn, w_in, subln, w_out):
    B, T, _ = xn.shape
    H, Dk, Dv, C = RT_HEADS, RT_KEY_DIM, RT_VAL_DIM, RT_CHUNK
    nc = T // C
    proj = xn @ w_in
    q, k, v, g = jnp.split(proj, [H * Dk, 2 * H * Dk, 2 * H * Dk + H * Dv], axis=-1)

    def to_chunks(t, d):
        t = t.astype(jnp.float32).reshape(B, T, H, d).transpose(0, 2, 1, 3)
        return t.reshape(B, H, nc, C, d).transpose(2, 0, 1, 3, 4)

    qc = to_chunks(q, Dk)
    kc = to_chunks(k, Dk) * (Dk ** -0.5)
    vc = to_chunks(v, Dv)
    gamma = 1.0 - 2.0 ** (-5.0 - jnp.arange(H, dtype=jnp.float32))
    log_g = jnp.log(gamma)
    idx = jnp.arange(C, dtype=jnp.float32)
    diff = idx[:, None] - idx[None, :]
    decay_intra = jnp.where(diff >= 0, jnp.exp(jnp.maximum(diff, 0.0) * log_g[:, None, None]), 0.0)
    xi = jnp.exp((idx + 1.0) * log_g[:, None])
    zeta = jnp.exp((C - 1.0 - idx) * log_g[:, None])
    g_chunk = jnp.exp(C * log_g)

    def step(R, inp):
        qq, kk, vv = inp
        att = jnp.einsum('bhqd,bhkd->bhqk', qq, kk) * decay_intra
        o = (jnp.einsum('bhqk,bhkv->bhqv', att, vv)
             + jnp.einsum('bhqd,bhdv->bhqv', qq, R) * xi[:, :, None])
        R = g_chunk[:, None, None] * R + jnp.einsum('bhkd,bhkv->bhdv', kk * zeta[:, :, None], vv)
        return R, o

    R0 = jnp.zeros((B, H, Dk, Dv), jnp.float32)
    _, o = lax.scan(step, R0, (qc, kc, vc))
    o = o.transpose(1, 0, 3, 2, 4).reshape(B, T, H, Dv)
    o = _rms(o, subln).reshape(B, T, H * Dv)
    return (jax.nn.silu(g.astype(jnp.float32)) * o).astype(xn.dtype) @ w_out


def _spatial_gating(xn, w_in, v_gain, w_s, b_s, w_out):
    B, T, _ = xn.shape
    C, G = SG_CHUNK, SG_GROUPS
    nc = T // C
    z = jax.nn.gelu(xn @ w_in)
    u, v = jnp.split(z, 2, axis=-1)
    v = _rms(v, v_gain).reshape(B, nc, C, G, SG_WIDTH // G)
    ws = w_s * jnp.tril(jnp.ones((C, C), w_s.dtype))
    mixed = jnp.einsum('gts,bnsgc->bntgc', ws, v) + jnp.transpose(b_s)[None, None, :, :, None]
    return (u * mixed.reshape(B, T, SG_WIDTH)) @ w_out


def _rglru_block(xn, w_in, conv_w, conv_b, w_a, b_a, w_x, b_x, lam, w_out):
    B, T, _ = xn.shape
    gate_br, xb = jnp.split(xn @ w_in, 2, axis=-1)
    gate = jax.nn.gelu(gate_br)
    xb = lax.conv_general_dilated(
        xb, conv_w[:, None, :].astype(xb.dtype), window_strides=(1,),
        padding=[(CONV_WIDTH - 1, 0)], dimension_numbers=('NWC', 'WIO', 'NWC'),
        feature_group_count=LRU_WIDTH) + conv_b
    xb = xb.astype(jnp.float32)
    xblk = xb.reshape(B, T, LRU_BLOCKS, LRU_BLOCK)
    r = jax.nn.sigmoid(jnp.einsum('btnc,ncd->btnd', xblk, w_a.astype(jnp.float32))
                       .reshape(B, T, LRU_WIDTH) + b_a)
    i = jax.nn.sigmoid(jnp.einsum('btnc,ncd->btnd', xblk, w_x.astype(jnp.float32))
                       .reshape(B, T, LRU_WIDTH) + b_x)
    log_a = -RG_C * r * jax.nn.softplus(-lam.astype(jnp.float32))
    a = jnp.exp(log_a)
    b = jnp.sqrt(jnp.maximum(-jnp.expm1(2.0 * log_a), 1e-12)) * (i * xb)

    def combine(c1, c2):
        a1, b1 = c1
        a2, b2 = c2
        return a1 * a2, a2 * b1 + b2

    _, h = lax.associative_scan(combine, (a, b), axis=1)
    return (h * gate.astype(jnp.float32)).astype(xn.dtype) @ w_out


def setup_inputs(seed: int = 0) -> dict:
    key = jax.random.key(seed)
    keys = iter(jax.random.split(key, 40))
    f32 = jnp.float32

    def nrm(shape, scale):
        return scale * jax.random.normal(next(keys), shape, f32)

    def gain(shape):
        return 1.0 + 0.02 * jax.random.normal(next(keys), shape, f32)

    na, nr, ns, nl = [len(range(m, DEPTH, N_MIXERS)) for m in range(N_MIXERS)]
    D = D_MODEL
    da_width = DA_HEADS * 2 * DA_HEAD_DIM
    rt_in = 2 * RT_HEADS * RT_KEY_DIM + 2 * RT_HEADS * RT_VAL_DIM
    inp = {}
    inp['x'] = jax.random.normal(next(keys), (BATCH, SEQ, D), f32)
    inp['ffn1_norm'] = gain((DEPTH, D))
    inp['ffn1_w_in'] = nrm((DEPTH, D, 2 * D_FF), D ** -0.5)
    inp['ffn1_w_out'] = nrm((DEPTH, D_FF, D), D_FF ** -0.5)
    inp['mix_norm'] = gain((DEPTH, D))
    inp['ffn2_norm'] = gain((DEPTH, D))
    inp['ffn2_w_in'] = nrm((DEPTH, D, 2 * D_FF), D ** -0.5)
    inp['ffn2_w_out'] = nrm((DEPTH, D_FF, D), D_FF ** -0.5)
    inp['da_w_in'] = nrm((na, D, 3 * da_width), D ** -0.5)
    inp['da_q_gain'] = gain((na, DA_HEAD_DIM))
    inp['da_k_gain'] = gain((na, DA_HEAD_DIM))
    inp['da_lambda'] = nrm((na, 4, DA_HEAD_DIM), 0.1)
    inp['da_subln'] = gain((na, 2 * DA_HEAD_DIM))
    inp['da_w_out'] = nrm((na, da_width, D), da_width ** -0.5)
    inp['rt_w_in'] = nrm((nr, D, rt_in), D ** -0.5)
    inp['rt_subln'] = gain((nr, RT_HEADS, RT_VAL_DIM))
    inp['rt_w_out'] = nrm((nr, RT_HEADS * RT_VAL_DIM, D), (RT_HEADS * RT_VAL_DIM) ** -0.5)
    inp['sg_w_in'] = nrm((ns, D, 2 * SG_WIDTH), D ** -0.5)
    inp['sg_v_gain'] = gain((ns, SG_WIDTH))
    inp['sg_w_s'] = nrm((ns, SG_GROUPS, SG_CHUNK, SG_CHUNK), SG_CHUNK ** -0.5)
    inp['sg_b_s'] = 1.0 + nrm((ns, SG_GROUPS, SG_CHUNK), 0.1)
    inp['sg_w_out'] = nrm((ns, SG_WIDTH, D), SG_WIDTH ** -0.5)
    inp['lr_w_in'] = nrm((nl, D, 2 * LRU_WIDTH), D ** -0.5)
    inp['lr_conv_w'] = nrm((nl, CONV_WIDTH, LRU_WIDTH), CONV_WIDTH ** -0.5)
    inp['lr_conv_b'] = nrm((nl, LRU_WIDTH), 0.01)
    inp['lr_w_a'] = nrm((nl, LRU_BLOCKS, LRU_BLOCK, LRU_BLOCK), LRU_BLOCK ** -0.5)
    inp['lr_b_a'] = nrm((nl, LRU_WIDTH), 0.01)
    inp['lr_w_x'] = nrm((nl, LRU_BLOCKS, LRU_BLOCK, LRU_BLOCK), LRU_BLOCK ** -0.5)
    inp['lr_b_x'] = nrm((nl, LRU_WIDTH), 0.01)
    a0 = jax.random.uniform(next(keys), (nl, LRU_WIDTH), f32, 0.9, 0.999)
    s = a0 ** (1.0 / RG_C)
    inp['lr_lambda'] = jnp.log(s) - jnp.log1p(-s)
    inp['lr_w_out'] = nrm((nl, LRU_WIDTH, D), LRU_WIDTH ** -0.5)
    return inp


def reference(x, ffn1_norm, ffn1_w_in, ffn1_w_out, mix_norm, ffn2_norm, ffn2_w_in, ffn2_w_out,
              da_w_in, da_q_gain, da_k_gain, da_lambda, da_subln, da_w_out,
              rt_w_in, rt_subln, rt_w_out,
              sg_w_in, sg_v_gain, sg_w_s, sg_b_s, sg_w_out,
              lr_w_in, lr_conv_w, lr_conv_b, lr_w_a, lr_b_a, lr_w_x, lr_b_x, lr_lambda, lr_w_out):
    for i in range(DEPTH):
        kind = i % N_MIXERS
        j = i // N_MIXERS
        h = x + 0.5 * _swiglu(_rms(x, ffn1_norm[i]), ffn1_w_in[i], ffn1_w_out[i])
        xn = _rms(h, mix_norm[i])
        if kind == 0:
            m = _diff_attention(xn, da_w_in[j], da_q_gain[j], da_k_gain[j], da_lambda[j],
                                da_subln[j], da_w_out[j], _lambda_init(i))
        elif kind == 1:
            m = _retention(xn, rt_w_in[j], rt_subln[j], rt_w_out[j])
        elif kind == 2:
            m = _spatial_gating(xn, sg_w_in[j], sg_v_gain[j], sg_w_s[j], sg_b_s[j], sg_w_out[j])
        else:
            m = _rglru_block(xn, lr_w_in[j], lr_conv_w[j], lr_conv_b[j], lr_w_a[j], lr_b_a[j],
                             lr_w_x[j], lr_b_x[j], lr_lambda[j], lr_w_out[j])
        h = h + m.astype(h.dtype)
        x = h + 0.5 * _swiglu(_rms(h, ffn2_norm[i]), ffn2_w_in[i], ffn2_w_out[i])
    return x
```

```python
import numpy as np
import concourse.bass as bass
import concourse.mybir as mybir
from concourse.bass_utils import run_bass_kernel_spmd

F32 = mybir.dt.float32
BF16 = mybir.dt.bfloat16
AF = mybir.ActivationFunctionType
ALU = mybir.AluOpType

NCORES = 8
D = 1024
SEQ = 16384
TPC = SEQ // NCORES
DFF = 2816
NJ = DFF // 128
KC = D // 128
EPS = 1e-6
SAME_ENG_SYNC = True


class TT:
    __slots__ = ("ap", "lw", "rd", "dsem", "dcnt", "name")

    def __init__(self, ap, name=""):
        self.ap = ap
        self.lw = None
        self.rd = {}
        self.dsem = None
        self.dcnt = 0
        self.name = name


class Prog:
    ENG = ("pe", "act", "dve", "pool", "sp")

    def __init__(self):
        self.nc = bass.Bass("TRN2", target_bir_lowering=False)
        self.ops = {e: [] for e in self.ENG}
        self.cnt = {e: 0 for e in self.ENG}
        self.waited = {e: {} for e in self.ENG}
        self.sems = {}
        self.stack = []
        self.dma_tiles = []
        self.dma_final = {}
        self.all_eng_sems = []
        self.nsem = 0
        for e in self.ENG:
            self.sems[e] = self._sem("e_" + e)
        self.psum = []

    def _enter(self, cm):
        v = cm.__enter__()
        self.stack.append(cm)
        return v

    def _sem(self, name):
        self.nsem += 1
        cm = self.nc.semaphore(name + "_%d" % self.nsem)
        v = cm.__enter__()
        try:
            cm._is_sem = True
            self.stack.append(cm)
        except Exception:
            self.semstack = getattr(self, "semstack", [])
            self.semstack.append(cm)
        return v

    def sbuf(self, name, shape, dt):
        self.nsb = getattr(self, "nsb", 0) + 1
        return self._enter(self.nc.sbuf_tensor("sb%d_%s" % (self.nsb, name), list(shape), dt))

    def psum_tiles(self):
        if not self.psum:
            for i in range(8):
                h = self._enter(self.nc.psum_tensor("ps%d" % i, [128, 512], F32))
                self.psum.append(TT(h, "ps%d" % i))
            self.psi = 0
        return self.psum

    def ps(self):
        self.psum_tiles()
        pool = getattr(self, "ps_pool", list(range(8)))
        t = self.psum[pool[self.psi % len(pool)]]
        self.psi += 1
        return t

    def barrier(self):
        evs = [(self.sems[e], self.cnt[e], e) for e in self.ENG if self.cnt[e] > 0 and e != "sp"]
        devs = list(self.dma_final.values())
        for eng in self.ENG:
            wl = []
            wd = self.waited[eng]
            for sm, v, e_ in evs:
                if e_ != eng and wd.get(id(sm), 0) < v:
                    wd[id(sm)] = v
                    wl.append((sm, v))
            for sm, v in devs:
                if wd.get(id(sm), 0) < v:
                    wd[id(sm)] = v
                    wl.append((sm, v))

            def emit(e, wl=wl):
                for s_, v in wl:
                    e.wait_ge(s_, v)
            self.ops[eng].append(emit)

    def mark(self):
        return len(self.stack)

    def release(self, m):
        self.barrier()
        keep = []
        while len(self.stack) > m:
            cm = self.stack.pop()
            if getattr(cm, "_is_sem", False):
                keep.append(cm)
            else:
                cm.__exit__(None, None, None)
        self.stack.extend(reversed(keep))

    def dram(self, name, shape, dt, kind):
        return self.nc.dram_tensor(name, list(shape), dt, kind=kind).ap()

    EPOCH = 1500

    def _deps(self, eng, reads, writes):
        need = {}

        def add(ev):
            if ev is None:
                return
            sm, v, e = ev
            if e == eng and (eng == "pe" or not SAME_ENG_SYNC):
                return
            k = id(sm)
            if k not in need or need[k][1] < v:
                need[k] = (sm, v)
        for t in reads:
            add(t.lw)
        for t in writes:
            add(t.lw)
            for ev in t.rd.values():
                add(ev)
        out = []
        wd = self.waited[eng]
        for k, (sm, v) in need.items():
            if wd.get(k, 0) < v:
                wd[k] = v
                out.append((sm, v))
        return out

    def op(self, eng, fn, reads=(), writes=()):
        wl = self._deps(eng, reads, writes)
        if self.cnt[eng] >= self.EPOCH:
            self.sems[eng] = self._sem("e_" + eng)
            self.cnt[eng] = 0
        self.cnt[eng] += 1
        val = self.cnt[eng]
        sem = self.sems[eng]

        def emit(e):
            for s, v in wl:
                e.wait_ge(s, v)
            fn(e).then_inc(sem, 1)
        self.ops[eng].append(emit)
        for t in reads:
            t.rd[eng] = (sem, val, eng)
        for t in writes:
            t.lw = (sem, val, eng)
            t.rd = {}

    def dma(self, q, out_ap, in_ap, tile, is_load):
        if tile.dsem is None or tile.dcnt >= 16 * 100:
            tile.dsem = self._sem("d")
            tile.dcnt = 0
            self.dma_tiles.append((tile, tile.dsem))
        wl = self._deps(q, () if is_load else (tile,), (tile,) if is_load else ())
        tile.dcnt += 16
        val = tile.dcnt
        sem = tile.dsem
        self.dma_final[id(sem)] = (sem, val)

        def emit(e):
            for s, v in wl:
                e.wait_ge(s, v)
            e.dma_start(out=out_ap, in_=in_ap).then_inc(sem, 16)
        self.ops[q].append(emit)
        if is_load:
            tile.lw = (sem, val, None)
            tile.rd = {}
        else:
            tile.rd[id(sem)] = (sem, val, None)

    def finish(self):
        finals = list(self.dma_final.values())

        def emit(e):
            for s, v in finals:
                e.wait_ge(s, v)
        self.ops["sp"].append(emit)
        nc = self.nc
        ops = self.ops
        with nc.Block() as block:
            @block.tensor
            def _(e):
                for f in ops["pe"]:
                    f(e)

            @block.scalar
            def _(e):
                for f in ops["act"]:
                    f(e)

            @block.vector
            def _(e):
                for f in ops["dve"]:
                    f(e)

            @block.gpsimd
            def _(e):
                for f in ops["pool"]:
                    f(e)

            @block.sync
            def _(e):
                for f in ops["sp"]:
                    f(e)
        while self.stack:
            self.stack.pop().__exit__(None, None, None)
        return nc


class Ctx:
    pass


def setup_common(P):
    c = Ctx()
    c.ones_f = TT(P.sbuf("ones_f", [128, 128], F32), "ones_f")
    P.op("dve", lambda e: e.memset(c.ones_f.ap[:], 1.0), writes=[c.ones_f])
    return c


def load_resid(P, c, x_dram):
    c.xT_h = P.sbuf("xT", [128, KC, TPC], F32)
    c.xT_all = TT(c.xT_h, "xT")
    P.dma("sp", c.xT_h[:], x_dram[:, :, :], c.xT_all, True)
    c.xT = []
    for tb in range(TPC // 512):
        t = TT(c.xT_h[:, :, tb * 512:(tb + 1) * 512], "xT%d" % tb)
        t.lw = c.xT_all.lw
        c.xT.append(t)


def store_resid(P, c, y_dram):
    tall = c.xT_all
    for t in c.xT:
        pass
    for tb, t in enumerate(c.xT):
        P.dma("sp", y_dram[:, :, tb * 512:(tb + 1) * 512], t.ap, t, False)


def rms_norm(P, c, src_blk, gain_ap_fn, dst, n_feat_chunks, tb_cols, tmp_sq, rstd, src_ap_fn, inv_n):
    raise NotImplementedError


def ffn(P, c, pre, w_in_d, w_out_d, gain_t, gidx):
    if not hasattr(c, "ffn_bufs"):
        b = Ctx()
        b.xn = [TT(P.sbuf("xn%d" % i, [128, KC, 512], BF16), "xn%d" % i) for i in range(2)]
        b.gT = [TT(P.sbuf("gT%d" % i, [128, NJ, 512], BF16), "gT%d" % i) for i in range(2)]
        b.win = [TT(P.sbuf("win%d" % i, [128, KC, 256], BF16), "win%d" % i) for i in range(3)]
        b.wout = [TT(P.sbuf("wout%d" % i, [128, NJ, 128], BF16), "wout%d" % i) for i in range(2)]
        b.sq = [TT(P.sbuf("sq%d" % i, [128, 512], F32), "sq%d" % i) for i in range(2)]
        b.rstd = [TT(P.sbuf("rstd%d" % i, [128, 512], F32), "rstd%d" % i) for i in range(2)]
        b.sa = [TT(P.sbuf("sa%d" % i, [128, 512], F32), "sa%d" % i) for i in range(2)]
        b.i = 0
        c.ffn_bufs = b
    b = c.ffn_bufs
    nblk = TPC // 512
    for half in range(nblk // 2):
        tbs = [2 * half, 2 * half + 1]
        for ii, tb in enumerate(tbs):
            x = c.xT[tb]
            xn = b.xn[ii]
            rstd = b.rstd[ii]
            pss = P.ps()
            for kc in range(KC):
                sq = b.sq[kc % 2]
                P.op("act", lambda e, sq=sq, x=x, kc=kc: e.activation(out=sq.ap[:], in_=x.ap[:, kc, :], func=AF.Square),
                     reads=[x], writes=[sq])
                P.op("pe", lambda e, sq=sq, kc=kc, pss=pss: e.matmul(pss.ap[:], c.ones_f.ap[:], sq.ap[:], start=(kc == 0), stop=(kc == KC - 1)),
                     reads=[sq, c.ones_f], writes=[pss])
            P.op("act", lambda e, rstd=rstd, pss=pss: e.activation(out=rstd.ap[:], in_=pss.ap[:], func=AF.Sqrt, scale=1.0 / D, bias=c.eps_t.ap[:, 0:1]),
                 reads=[pss, c.eps_t], writes=[rstd])
            P.op("dve", lambda e, rstd=rstd: e.reciprocal(out=rstd.ap[:], in_=rstd.ap[:]), reads=[rstd], writes=[rstd])
            for kc in range(KC):
                P.op("dve", lambda e, xn=xn, x=x, kc=kc, rstd=rstd: e.scalar_tensor_tensor(
                    out=xn.ap[:, kc, :], in0=x.ap[:, kc, :], scalar=gain_t.ap[:, gidx, kc:kc + 1], in1=rstd.ap[:],
                    op0=ALU.mult, op1=ALU.mult), reads=[x, rstd, gain_t], writes=[xn])
        for j in range(NJ):
            w = b.win[b.i % 3]
            b.i += 1
            P.dma("pool", w.ap[:], w_in_d[j], w, True)
            for ii, tb in enumerate(tbs):
                xn = b.xn[ii]
                gT = b.gT[ii]
                pa = P.ps()
                pb = P.ps()
                for kc in range(KC):
                    P.op("pe", lambda e, pa=pa, w=w, xn=xn, kc=kc: e.matmul(pa.ap[:], w.ap[:, kc, 0:128], xn.ap[:, kc, :], start=(kc == 0), stop=(kc == KC - 1)),
                         reads=[w, xn], writes=[pa])
                for kc in range(KC):
                    P.op("pe", lambda e, pb=pb, w=w, xn=xn, kc=kc: e.matmul(pb.ap[:], w.ap[:, kc, 128:256], xn.ap[:, kc, :], start=(kc == 0), stop=(kc == KC - 1)),
                         reads=[w, xn], writes=[pb])
                sa = b.sa[(j * 2 + ii) % 2]
                P.op("act", lambda e, sa=sa, pa=pa: e.activation(out=sa.ap[:], in_=pa.ap[:], func=AF.Silu), reads=[pa], writes=[sa])
                P.op("dve", lambda e, gT=gT, sa=sa, pb=pb, j=j: e.tensor_tensor(out=gT.ap[:, j, :], in0=sa.ap[:], in1=pb.ap[:], op=ALU.mult),
                     reads=[sa, pb], writes=[gT])
        for n in range(KC):
            w = b.wout[n % 2]
            P.dma("pool", w.ap[:], w_out_d[n], w, True)
            for ii, tb in enumerate(tbs):
                x = c.xT[tb]
                gT = b.gT[ii]
                po = P.ps()
                for j in range(NJ):
                    P.op("pe", lambda e, po=po, w=w, gT=gT, j=j: e.matmul(po.ap[:], w.ap[:, j, :], gT.ap[:, j, :], start=(j == 0), stop=(j == NJ - 1)),
                         reads=[w, gT], writes=[po])
                P.op("dve", lambda e, x=x, po=po, n=n: e.scalar_tensor_tensor(
                    out=x.ap[:, n, :], in0=po.ap[:], scalar=0.5, in1=x.ap[:, n, :], op0=ALU.mult, op1=ALU.add),
                    reads=[po, x], writes=[x])


def load_small(P, c, name, dram_ap, shape):
    t = TT(P.sbuf(name, shape, F32), name)
    P.dma("sp", t.ap[:], dram_ap, t, True)
    return t


def lay_resid(x_tok):
    out = []
    for c in range(NCORES):
        xs = x_tok[c * TPC:(c + 1) * TPC]
        out.append(np.ascontiguousarray(xs.T.reshape(KC, 128, TPC).transpose(1, 0, 2)))
    return out


def unlay_resid(shards):
    outs = []
    for s in shards:
        outs.append(s.transpose(1, 0, 2).reshape(D, TPC).T)
    return np.ascontiguousarray(np.concatenate(outs, axis=0))


def lay_w_in(w):
    a = w[:, :DFF].reshape(KC, 128, NJ, 128)
    b = w[:, DFF:].reshape(KC, 128, NJ, 128)
    ab = np.concatenate([a, b], axis=3)
    return np.ascontiguousarray(ab.transpose(2, 1, 0, 3))


def lay_w_out(w):
    return np.ascontiguousarray(w.reshape(NJ, 128, KC, 128).transpose(2, 1, 0, 3))


def lay_gain(g):
    n = g.shape[0]
    return np.ascontiguousarray(g.reshape(n, KC, 128).transpose(2, 0, 1))


def V(t, ap=None):
    return (t, t.ap[:] if ap is None else ap)


def _tl(*ops):
    return [o[0] for o in ops if o is not None and isinstance(o, tuple)]


def act(P, out, in_, func, bias=None, scale=None, accum=None):
    kw = {}
    rd = [in_[0]]
    if bias is not None:
        if isinstance(bias, tuple):
            kw["bias"] = bias[1]
            rd.append(bias[0])
        else:
            kw["bias"] = bias
    if scale is not None:
        if isinstance(scale, tuple):
            kw["scale"] = scale[1]
            rd.append(scale[0])
        else:
            kw["scale"] = scale
    wr = [out[0]]
    if accum is not None:
        kw["accum_out"] = accum[1]
        wr.append(accum[0])
    P.op("act", lambda e: e.activation(out=out[1], in_=in_[1], func=func, **kw), reads=rd, writes=wr)


def mm(P, out, lhsT, rhs, start, stop):
    P.op("pe", lambda e: e.matmul(out[1], lhsT[1], rhs[1], start=start, stop=stop), reads=[lhsT[0], rhs[0]], writes=[out[0]])


def tt(P, out, a, b, op, eng="dve"):
    P.op(eng, lambda e: e.tensor_tensor(out=out[1], in0=a[1], in1=b[1], op=op), reads=[a[0], b[0]], writes=[out[0]])


def stt(P, out, a, scalar, b, op0, op1, eng="dve"):
    rd = [a[0], b[0]]
    sc = scalar
    if isinstance(scalar, tuple):
        rd.append(scalar[0])
        sc = scalar[1]
    P.op(eng, lambda e: e.scalar_tensor_tensor(out=out[1], in0=a[1], scalar=sc, in1=b[1], op0=op0, op1=op1), reads=rd, writes=[out[0]])


def ts(P, out, a, s1, s2, op0, op1=None, eng="dve", accum=None):
    rd = [a[0]]
    v1 = s1
    if isinstance(s1, tuple):
        rd.append(s1[0])
        v1 = s1[1]
    v2 = s2
    if isinstance(s2, tuple):
        rd.append(s2[0])
        v2 = s2[1]
    kw = {}
    if op1 is not None:
        kw["op1"] = op1
    wr = [out[0]]
    if accum is not None:
        kw["accum_out"] = accum[1]
        wr.append(accum[0])
    P.op(eng, lambda e: e.tensor_scalar(out=out[1], in0=a[1], scalar1=v1, scalar2=v2, op0=op0, **kw), reads=rd, writes=wr)


def cp(P, out, a, eng="dve"):
    if eng == "act":
        P.op("act", lambda e: e.activation(out=out[1], in_=a[1], func=AF.Copy), reads=[a[0]], writes=[out[0]])
    else:
        P.op(eng, lambda e: e.tensor_copy(out=out[1], in_=a[1]), reads=[a[0]], writes=[out[0]])


def recip(P, out, a):
    P.op("dve", lambda e: e.reciprocal(out=out[1], in_=a[1]), reads=[a[0]], writes=[out[0]])


def const_tile(P, c, name, val, shape=(128, 1), dt=F32):
    t = TT(P.sbuf(name, list(shape), dt), name)
    P.op("dve", lambda e: e.memset(t.ap[:], val), writes=[t])
    return t


def newt(P, name, shape, dt):
    return TT(P.sbuf(name, list(shape), dt), name)


def sumsq_rstd(P, c, srcs, lhsT, scale, bias_t, rstd, ncols):
    pss = P.ps()
    n = len(srcs)
    for i, s in enumerate(srcs):
        sq = c.sq[c.sqi % 2]
        c.sqi += 1
        act(P, V(sq, sq.ap[:, :ncols]), s, AF.Square)
        mm(P, V(pss, pss.ap[:, :ncols]), V(lhsT), V(sq, sq.ap[:, :ncols]), i == 0, i == n - 1)
    act(P, V(rstd, rstd.ap[:, :ncols]), V(pss, pss.ap[:, :ncols]), AF.Sqrt, bias=V(bias_t, bias_t.ap[:, 0:1]), scale=scale)
    recip(P, V(rstd, rstd.ap[:, :ncols]), V(rstd, rstd.ap[:, :ncols]))


def norm_block(P, c, x, gain_t, gidx, xn, rstd):
    sumsq_rstd(P, c, [V(x, x.ap[:, kc, :]) for kc in range(KC)], c.ones_f, 1.0 / D, c.eps_t, rstd, 512)
    for kc in range(KC):
        stt(P, V(xn, xn.ap[:, kc, :]), V(x, x.ap[:, kc, :]), V(gain_t, gain_t.ap[:, gidx, kc:kc + 1]), V(rstd), ALU.mult, ALU.mult)


def common_bufs(P, c):
    c.sq = [newt(P, "sqc%d" % i, [128, 512], F32) for i in range(2)]
    c.sqi = 0
    c.eps_t = const_tile(P, c, "eps", EPS)


def ffn_scoped(P, c, w_in_d, w_out_d, gain_t, gidx):
    m = P.mark()
    ffn(P, c, None, w_in_d, w_out_d, gain_t, gidx)
    del c.ffn_bufs
    P.release(m)


def linear_resid(P, c, w_d, nk, src_fn, scale, wname):
    wt = [newt(P, "%s%d" % (wname, i), [128, nk, 128], BF16) for i in range(2)]
    for n in range(KC):
        w = wt[n % 2]
        P.dma("pool", w.ap[:], w_d[n], w, True)
        for tb in range(TPC // 512):
            x = c.xT[tb]
            po = P.ps()
            for k in range(nk):
                mm(P, V(po), V(w, w.ap[:, k, :]), src_fn(tb, k), k == 0, k == nk - 1)
            stt(P, V(x, x.ap[:, n, :]), V(po), scale, V(x, x.ap[:, n, :]), ALU.mult, ALU.add)


def lay_w_cols(w, cb):
    K_, N_ = w.shape
    return np.ascontiguousarray(w.reshape(K_ // 128, 128, N_ // cb, cb).transpose(2, 1, 0, 3))


def lay_w_rows(w):
    K_ = w.shape[0]
    return np.ascontiguousarray(w.reshape(K_ // 128, 128, KC, 128).transpose(2, 1, 0, 3))


def build_A():
    P = Prog()
    c = setup_common(P)
    common_bufs(P, c)
    x_d = P.dram("x", [128, KC, TPC], F32, "ExternalInput")
    win_d = P.dram("w_in", [NJ, 128, KC, 256], F32, "ExternalInput")
    wout_d = P.dram("w_out", [KC, 128, NJ, 128], F32, "ExternalInput")
    g_d = P.dram("gain", [128, 2, KC], F32, "ExternalInput")
    wqk_d = P.dram("wqk", [4, 128, KC, 512], F32, "ExternalInput")
    wv_d = P.dram("wv", [2, 128, KC, 512], F32, "ExternalInput")
    qkg_d = P.dram("qkg", [128, 2], F32, "ExternalInput")
    bd_d = P.dram("bd", [128, 128], F32, "ExternalInput")
    h_o = P.dram("h_out", [128, KC, TPC], F32, "ExternalOutput")
    qk_o = P.dram("qk_out", [16, 128, TPC], BF16, "ExternalOutput")
    v_o = P.dram("v_out", [TPC // 128, 128, 1024], BF16, "ExternalOutput")
    gain_t = load_small(P, c, "gain", g_d[:, :, :], [128, 2, KC])
    qkg = load_small(P, c, "qkg", qkg_d[:, :], [128, 2])
    bd = load_small(P, c, "bd", bd_d[:, :], [128, 128])
    eps64 = const_tile(P, c, "eps64", 64 * EPS)
    load_resid(P, c, x_d)
    ffn_scoped(P, c, win_d, wout_d, gain_t, 0)
    store_resid(P, c, h_o)
    xn = [newt(P, "xnA%d" % i, [128, KC, 512], BF16) for i in range(2)]
    rstd = [newt(P, "rsA%d" % i, [128, 512], F32) for i in range(2)]
    wv = [newt(P, "wv%d" % i, [128, KC, 512], BF16) for i in range(2)]
    wqk = [newt(P, "wqk%d" % i, [128, KC, 512], BF16) for i in range(2)]
    qst = [newt(P, "qst%d" % i, [128, 512], BF16) for i in range(3)]
    vst = [newt(P, "vst%d" % i, [128, 1024], BF16) for i in range(2)]
    rq = [newt(P, "rq%d" % i, [128, 512], F32) for i in range(2)]
    for i in range(2):
        P.dma("pool", wv[i].ap[:], wv_d[i], wv[i], True)
    wi = 0
    si = 0
    for tb in range(TPC // 512):
        x = c.xT[tb]
        xb = xn[tb % 2]
        norm_block(P, c, x, gain_t, 1, xb, rstd[tb % 2])
        for blk in range(4):
            w = wqk[wi % 2]
            wi += 1
            P.dma("pool", w.ap[:], wqk_d[blk], w, True)
            for cc in range(4):
                ch = blk * 4 + cc
                isq = ch < 8
                pq = P.ps()
                for kc in range(KC):
                    mm(P, V(pq), V(w, w.ap[:, kc, cc * 128:(cc + 1) * 128]), V(xb, xb.ap[:, kc, :]), kc == 0, kc == KC - 1)
                r = rq[si % 2]
                if isq:
                    sumsq_rstd(P, c, [V(pq)], bd, 1.0, eps64, r, 512)
                else:
                    sumsq_rstd(P, c, [V(pq)], bd, 1.0 / 64, c.eps_t, r, 512)
                st = qst[si % 3]
                si += 1
                stt(P, V(st), V(pq), V(qkg, qkg.ap[:, (0 if isq else 1):(1 if isq else 2)]), V(r), ALU.mult, ALU.mult)
                P.dma("sp", qk_o[ch][:, tb * 512:(tb + 1) * 512], st.ap[:], st, False)
        for sub in range(4):
            vs = vst[sub % 2]
            for cb in range(2):
                pv = P.ps()
                for kc in range(KC):
                    mm(P, V(pv), V(xb, xb.ap[:, kc, sub * 128:(sub + 1) * 128]), V(wv[cb], wv[cb].ap[:, kc, :]), kc == 0, kc == KC - 1)
                cp(P, V(vs, vs.ap[:, cb * 512:(cb + 1) * 512]), V(pv), eng="act")
            P.dma("sp", v_o[tb * 4 + sub], vs.ap[:], vs, False)
    return P.finish()


def host_A(inp, li=0):
    x = inp["x"][0]
    xs = lay_resid(x)
    w = inp["da_w_in"][0]
    wq, wk, wv = w[:, :1024], w[:, 1024:2048], w[:, 2048:]
    common = {
        "w_in": lay_w_in(inp["ffn1_w_in"][li]), "w_out": lay_w_out(inp["ffn1_w_out"][li]),
        "gain": lay_gain(np.stack([inp["ffn1_norm"][li], inp["mix_norm"][li]])),
        "wqk": lay_w_cols(np.concatenate([wq, wk], axis=1), 512),
        "wv": lay_w_cols(wv, 512),
        "qkg": np.ascontiguousarray(np.stack([np.tile(inp["da_q_gain"][0], 2), np.tile(inp["da_k_gain"][0], 2)], axis=1)),
        "bd": np.kron(np.eye(2, dtype=np.float32), np.ones((64, 64), np.float32)),
    }
    return [dict(common, x=xs[c]) for c in range(NCORES)]


NQB = SEQ // 512


def build_B(lam_init, nqb=NQB):
    P = Prog()
    c = setup_common(P)
    common_bufs(P, c)
    qa_d = P.dram("qa", [2, 68, SEQ], BF16, "ExternalInput")
    ka_d = P.dram("ka", [2, 68, SEQ], BF16, "ExternalInput")
    v_d = P.dram("v", [128, SEQ // 128, 128], BF16, "ExternalInput")
    lam_d = P.dram("lam", [128, 4, 64], F32, "ExternalInput")
    sg_d = P.dram("subg", [128, 1], F32, "ExternalInput")
    tri_d = P.dram("tri", [128, 128], F32, "ExternalInput")
    o_o = P.dram("o_out", [128, SEQ], BF16, "ExternalOutput")
    sp_t = [TT(P._enter(P.nc.psum_tensor("psS%d" % i, [128, 2, 512], F32)), "psS%d" % i) for i in range(3)]
    acc = [TT(P._enter(P.nc.psum_tensor("psO%d" % i, [128, 512], F32)), "psO%d" % i) for i in range(2)]
    lam = load_small(P, c, "lam", lam_d[:, :, :], [128, 4, 64])
    subg = load_small(P, c, "subg", sg_d[:, :], [128, 1])
    trif = load_small(P, c, "trif", tri_d[:, :], [128, 128])
    tri2 = newt(P, "tri2", [128, 2, 128], BF16)
    cp(P, V(tri2, tri2.ap[:, 0, :]), V(trif))
    cp(P, V(tri2, tri2.ap[:, 1, :]), V(trif))
    pr = newt(P, "lpr", [128, 64], F32)
    s1 = newt(P, "ls1", [128, 1], F32)
    s2 = newt(P, "ls2", [128, 1], F32)
    nlam = newt(P, "nlam", [128, 1], F32)
    tt(P, V(pr), V(lam, lam.ap[:, 0, :]), V(lam, lam.ap[:, 1, :]), ALU.mult)
    P.op("dve", lambda e: e.reduce_sum(out=s1.ap[:], in_=pr.ap[:], axis=mybir.AxisListType.X), reads=[pr], writes=[s1])
    tt(P, V(pr), V(lam, lam.ap[:, 2, :]), V(lam, lam.ap[:, 3, :]), ALU.mult)
    P.op("dve", lambda e: e.reduce_sum(out=s2.ap[:], in_=pr.ap[:], axis=mybir.AxisListType.X), reads=[pr], writes=[s2])
    act(P, V(s1), V(s1), AF.Exp)
    act(P, V(s2), V(s2), AF.Exp)
    tt(P, V(nlam), V(s1), V(s2), ALU.subtract)
    ts(P, V(nlam), V(nlam), lam_init, -1.0, ALU.add, ALU.mult)
    one_m = 1.0 - lam_init
    epsb = const_tile(P, c, "epsb", EPS / (one_m * one_m))
    ka = [newt(P, "ka%d" % m, [68, SEQ], BF16) for m in range(2)]
    for m in range(2):
        for part in range(4):
            sl = slice(part * (SEQ // 4), (part + 1) * (SEQ // 4))
            P.dma("sp", ka[m].ap[:, sl], ka_d[m][:, sl], ka[m], True)
    vt = newt(P, "vt", [128, SEQ // 128, 128], BF16)
    for part in range(4):
        sl = slice(part * 32, (part + 1) * 32)
        P.dma("sp", vt.ap[:, sl, :], v_d[:, sl, :], vt, True)
    qb = [[newt(P, "qb%d_%d" % (i, m), [68, 512], BF16) for m in range(2)] for i in range(2)]
    pts = [newt(P, "pt%d" % i, [128, 2, 512], BF16) for i in range(4)]
    accL = {"dve": newt(P, "accLd", [128, 2, 512], F32), "pool": newt(P, "accLp", [128, 2, 512], F32)}
    rl = [newt(P, "rl%d" % i, [128, 512], F32) for i in range(2)]
    t0 = newt(P, "t0", [128, 512], F32)
    t1 = newt(P, "t1", [128, 512], F32)
    osum = newt(P, "osum", [128, 512], F32)
    sqB = newt(P, "sqB", [128, 512], F32)
    rstd = newt(P, "rstdB", [128, 512], F32)
    ost = [newt(P, "ost%d" % i, [128, 512], BF16) for i in range(2)]
    pti = 0
    spi = 0
    for Qi in range(nqb):
        qq = qb[Qi % 2]
        for m in range(2):
            P.dma("sp", qq[m].ap[:], qa_d[m][:, Qi * 512:(Qi + 1) * 512], qq[m], True)
        nkb = 4 * Qi + 4
        pend = []
        first = {"dve": True, "pool": True}

        def do_pv(item, nkb=nkb):
            kb, c0, pt = item
            for m in range(2):
                mm(P, V(acc[m], acc[m].ap[:, c0:512]), V(vt, vt.ap[:, kb, :]), V(pt, pt.ap[:, m, c0:512]), kb == 0, kb == nkb - 1)
        for kb in range(nkb):
            j = kb - 4 * Qi
            c0 = 128 * j if j > 0 else 0
            S = sp_t[spi % 3]
            spi += 1
            for m in range(2):
                mm(P, V(S, S.ap[:, m, c0:512]), V(ka[m], ka[m].ap[:, kb * 128:(kb + 1) * 128]), V(qq[m], qq[m].ap[:, c0:512]), True, True)
            pt = pts[pti % 4]
            pti += 1
            act(P, V(pt, pt.ap[:, :, c0:512]), V(S, S.ap[:, :, c0:512]), AF.Exp)
            if j >= 0:
                tt(P, V(pt, pt.ap[:, :, 128 * j:128 * j + 128]), V(pt, pt.ap[:, :, 128 * j:128 * j + 128]), V(tri2), ALU.mult)
            eng = "dve" if (kb % 2 == 0 or j >= 0) else "pool"
            aL = accL[eng]
            if first[eng]:
                first[eng] = False
                cp(P, V(aL), V(pt), eng=eng)
            else:
                tt(P, V(aL, aL.ap[:, :, c0:512]), V(aL, aL.ap[:, :, c0:512]), V(pt, pt.ap[:, :, c0:512]), ALU.add, eng=eng)
            pend.append((kb, c0, pt))
            if len(pend) > 1:
                do_pv(pend.pop(0))
        while pend:
            do_pv(pend.pop(0))
        engs = [e_ for e_ in ("dve", "pool") if not first[e_]]
        Sl = sp_t[spi % 3]
        spi += 1
        for m in range(2):
            for i_, e_ in enumerate(engs):
                mm(P, V(Sl, Sl.ap[:, m, :]), V(c.ones_f), V(accL[e_], accL[e_].ap[:, m, :]), i_ == 0, i_ == len(engs) - 1)
        recip(P, V(rl[0]), V(Sl, Sl.ap[:, 0, :]))
        tt(P, V(t0), V(acc[0]), V(rl[0]), ALU.mult)
        recip(P, V(rl[1]), V(Sl, Sl.ap[:, 1, :]))
        tt(P, V(t1), V(acc[1]), V(rl[1]), ALU.mult)
        stt(P, V(osum), V(t1), V(nlam, nlam.ap[:, 0:1]), V(t0), ALU.mult, ALU.add)
        Sq = sp_t[spi % 3]
        spi += 1
        act(P, V(sqB), V(osum), AF.Square)
        mm(P, V(Sq, Sq.ap[:, 0, :]), V(c.ones_f), V(sqB), True, True)
        act(P, V(rstd), V(Sq, Sq.ap[:, 0, :]), AF.Sqrt, bias=V(epsb, epsb.ap[:, 0:1]), scale=1.0 / (128 * one_m * one_m))
        recip(P, V(rstd), V(rstd))
        os_ = ost[Qi % 2]
        stt(P, V(os_), V(osum), V(subg, subg.ap[:, 0:1]), V(rstd), ALU.mult, ALU.mult)
        P.dma("sp", o_o[:, Qi * 512:(Qi + 1) * 512], os_.ap[:], os_, False)
    return P.finish()


def host_B(inp, A_outs):
    import ml_dtypes
    bf = ml_dtypes.bfloat16
    qk = np.concatenate([o["qk_out"] for o in A_outs], axis=2)
    v = np.concatenate([o["v_out"].reshape(TPC, 1024) for o in A_outs], axis=0)
    pos = np.arange(SEQ)
    pb = (pos // 128).astype(np.float32)
    pr = (pos % 128).astype(np.float32)
    ones = np.ones(SEQ, np.float32)
    tri = (np.arange(128)[:, None] <= np.arange(128)[None, :]).astype(np.float32)
    lam = np.ascontiguousarray(np.broadcast_to(inp["da_lambda"][0][None], (128, 4, 64))).astype(np.float32)
    subg = np.ascontiguousarray(inp["da_subln"][0].reshape(128, 1))
    maps = []
    for h in range(NCORES):
        slope = 2.0 ** (-8.0 * (h + 1) / 8)
        qrows = np.stack([ones, ones, -slope * 128 * pb, -slope * pr]).astype(bf)
        krows = np.stack([slope * 128 * pb, slope * pr, ones, ones]).astype(bf)
        qh = qk[h].reshape(2, 64, SEQ)
        kh = qk[8 + h].reshape(2, 64, SEQ)
        qa = np.concatenate([qh, np.broadcast_to(qrows[None], (2, 4, SEQ))], axis=1)
        ka = np.concatenate([kh, np.broadcast_to(krows[None], (2, 4, SEQ))], axis=1)
        vh = v[:, h * 128:(h + 1) * 128].reshape(SEQ // 128, 128, 128).transpose(1, 0, 2)
        maps.append({"qa": np.ascontiguousarray(qa), "ka": np.ascontiguousarray(ka), "v": np.ascontiguousarray(vh),
                     "lam": lam, "subg": subg, "tri": tri})
    return maps


RT_G = [1.0 - 2.0 ** (-5.0 - h) for h in range(4)]
RT_S = 256 ** -0.5


def reload_resid(P, c, x_dram):
    P.dma("sp", c.xT_h[:], x_dram[:, :, :], c.xT_all, True)
    for t in c.xT:
        t.lw = c.xT_all.lw
        t.rd = {}


def gelu_tanh(P, c, out, z, g):
    n = out[1].shape[-1] if False else None
    if "sets" in g:
        g["i"] = g.get("i", 0) + 1
        gg_ = g["sets"][g["i"] % len(g["sets"])]
        t1, t2 = gg_["t1"], gg_["t2"]
    else:
        t1, t2 = g["t1"], g["t2"]
    w = z[1].shape[1]
    act(P, V(t1, t1.ap[:, :w]), z, AF.Square)
    ts(P, V(t1, t1.ap[:, :w]), V(t1, t1.ap[:, :w]), 0.044715, 1.0, ALU.mult, ALU.add)
    tt(P, V(t2, t2.ap[:, :w]), V(t1, t1.ap[:, :w]), z, ALU.mult)
    act(P, V(t2, t2.ap[:, :w]), V(t2, t2.ap[:, :w]), AF.Sigmoid, scale=1.5957691216057308)
    tt(P, out, V(t2, t2.ap[:, :w]), z, ALU.mult)


def build_C():
    P = Prog()
    c = setup_common(P)
    common_bufs(P, c)
    h_d = P.dram("h_in", [128, KC, TPC], F32, "ExternalInput")
    on_d = P.dram("on", [128, 8, TPC], BF16, "ExternalInput")
    wo_d = P.dram("wo", [KC, 128, 8, 128], F32, "ExternalInput")
    f2i = P.dram("f2_in", [NJ, 128, KC, 256], F32, "ExternalInput")
    f2o = P.dram("f2_out", [KC, 128, NJ, 128], F32, "ExternalInput")
    f1i = P.dram("f1_in", [NJ, 128, KC, 256], F32, "ExternalInput")
    f1o = P.dram("f1_out", [KC, 128, NJ, 128], F32, "ExternalInput")
    g_d = P.dram("gain", [128, 3, KC], F32, "ExternalInput")
    wfm_d = P.dram("wfm", [8, 128, KC, 512], F32, "ExternalInput")
    wtm_d = P.dram("wtm", [6, 128, KC, 512], F32, "ExternalInput")
    zc_d = P.dram("zc", [128, 4], F32, "ExternalInput")
    c2_d = P.dram("c2", [128, 16, 4], F32, "ExternalInput")
    h_o = P.dram("h_out", [128, KC, TPC], F32, "ExternalOutput")
    qk_o = P.dram("qkT_out", [16, 128, TPC], BF16, "ExternalOutput")
    gs_o = P.dram("gs_out", [16, 128, TPC], BF16, "ExternalOutput")
    kz_o = P.dram("kz_out", [16, 128, 1024], BF16, "ExternalOutput")
    v_o = P.dram("v_out", [16, 128, 2048], BF16, "ExternalOutput")
    R_o = P.dram("R_out", [128, 8, 512], F32, "ExternalOutput")
    gain_t = load_small(P, c, "gain", g_d[:, :, :], [128, 3, KC])
    zc = load_small(P, c, "zc", zc_d[:, :], [128, 4])
    c2 = load_small(P, c, "c2", c2_d[:, :, :], [128, 16, 4])
    load_resid(P, c, h_d)
    m0 = P.mark()
    on = newt(P, "on", [128, 8, TPC], BF16)
    P.dma("sp", on.ap[:], on_d[:, :, :], on, True)
    linear_resid(P, c, wo_d, 8, lambda tb, k: V(on, on.ap[:, k, tb * 512:(tb + 1) * 512]), 1.0, "woC")
    P.release(m0)
    ffn_scoped(P, c, f2i, f2o, gain_t, 0)
    ffn_scoped(P, c, f1i, f1o, gain_t, 1)
    store_resid(P, c, h_o)
    xn = [newt(P, "xnC%d" % i, [128, KC, 512], BF16) for i in range(2)]
    rstd = [newt(P, "rsC%d" % i, [128, 512], F32) for i in range(2)]
    wb = [newt(P, "wC%d" % i, [128, KC, 512], BF16) for i in range(3)]
    st = [newt(P, "stC%d" % i, [128, 512], BF16) for i in range(3)]
    kzst = [newt(P, "kzst%d" % i, [128, 1024], BF16) for i in range(4)]
    k2 = newt(P, "k2", [128, 4, 1024], BF16)
    vb = newt(P, "vb", [128, 4, 2048], BF16)
    Rl = newt(P, "Rl", [128, 8, 512], F32)
    wi = 0
    si = 0
    for tb in range(TPC // 512):
        x = c.xT[tb]
        xb = xn[tb % 2]
        norm_block(P, c, x, gain_t, 2, xb, rstd[tb % 2])
        for blk in range(8):
            w = wb[wi % 3]
            wi += 1
            P.dma("pool", w.ap[:], wfm_d[blk], w, True)
            for cc in range(4):
                ch = blk * 4 + cc
                pq = P.ps()
                for kc in range(KC):
                    mm(P, V(pq), V(w, w.ap[:, kc, cc * 128:(cc + 1) * 128]), V(xb, xb.ap[:, kc, :]), kc == 0, kc == KC - 1)
                s_ = st[si % 3]
                si += 1
                if ch < 16:
                    cp(P, V(s_), V(pq), eng="act")
                    P.dma("sp", qk_o[ch][:, tb * 512:(tb + 1) * 512], s_.ap[:], s_, False)
                else:
                    act(P, V(s_), V(pq), AF.Silu)
                    P.dma("sp", gs_o[ch - 16][:, tb * 512:(tb + 1) * 512], s_.ap[:], s_, False)
        for cb in range(6):
            w = wb[wi % 3]
            wi += 1
            P.dma("pool", w.ap[:], wtm_d[cb], w, True)
            for sub in range(4):
                n = tb * 4 + sub
                pv = P.ps()
                for kc in range(KC):
                    mm(P, V(pv), V(xb, xb.ap[:, kc, sub * 128:(sub + 1) * 128]), V(w, w.ap[:, kc, :]), kc == 0, kc == KC - 1)
                if cb < 2:
                    kzs = kzst[sub]
                    for hh in range(2):
                        h = cb * 2 + hh
                        ts(P, V(kzs, kzs.ap[:, cb * 512 + hh * 256:cb * 512 + (hh + 1) * 256]), V(pv, pv.ap[:, hh * 256:(hh + 1) * 256]),
                           V(zc, zc.ap[:, h:h + 1]), None, ALU.mult)
                        ts(P, V(k2, k2.ap[:, sub, cb * 512 + hh * 256:cb * 512 + (hh + 1) * 256]), V(pv, pv.ap[:, hh * 256:(hh + 1) * 256]),
                           V(c2, c2.ap[:, n, h:h + 1]), None, ALU.mult)
                    if cb == 1:
                        P.dma("sp", kz_o[n], kzs.ap[:], kzs, False)
                else:
                    cp(P, V(vb, vb.ap[:, sub, (cb - 2) * 512:(cb - 1) * 512]), V(pv), eng="act")
        for sub in range(4):
            P.dma("sp", v_o[tb * 4 + sub], vb.ap[:, sub, :], vb, False)
        for h in range(4):
            for dc in range(2):
                pr_ = P.ps()
                for sub in range(4):
                    mm(P, V(pr_), V(k2, k2.ap[:, sub, h * 256 + dc * 128:h * 256 + (dc + 1) * 128]), V(vb, vb.ap[:, sub, h * 512:(h + 1) * 512]), sub == 0, sub == 3)
                if tb == 0:
                    cp(P, V(Rl, Rl.ap[:, h * 2 + dc, :]), V(pr_))
                else:
                    tt(P, V(Rl, Rl.ap[:, h * 2 + dc, :]), V(Rl, Rl.ap[:, h * 2 + dc, :]), V(pr_), ALU.add)
    P.dma("sp", R_o[:, :, :], Rl.ap[:], Rl, False)
    return P.finish()


def host_C(inp, A_outs, B_outs):
    li = 0
    w = inp["rt_w_in"][0]
    wq, wk, wv, wg = w[:, :1024], w[:, 1024:2048], w[:, 2048:4096], w[:, 4096:]
    p = np.arange(128)
    zc = np.stack([RT_S * np.float64(RT_G[h]) ** (127 - p) for h in range(4)], axis=1).astype(np.float32)
    t = (np.arange(16)[None, :] * 128 + p[:, None])
    c2 = np.stack([RT_S * np.float64(RT_G[h]) ** (2047 - t) for h in range(4)], axis=2).astype(np.float32)
    common = {
        "wo": lay_w_rows(inp["da_w_out"][0]),
        "f2_in": lay_w_in(inp["ffn2_w_in"][0]), "f2_out": lay_w_out(inp["ffn2_w_out"][0]),
        "f1_in": lay_w_in(inp["ffn1_w_in"][1]), "f1_out": lay_w_out(inp["ffn1_w_out"][1]),
        "gain": lay_gain(np.stack([inp["ffn2_norm"][0], inp["ffn1_norm"][1], inp["mix_norm"][1]])),
        "wfm": lay_w_cols(np.concatenate([wq, wk, wg], axis=1), 512),
        "wtm": lay_w_cols(np.concatenate([wk, wv], axis=1), 512),
        "zc": zc, "c2": np.ascontiguousarray(c2),
    }
    maps = []
    for cix in range(NCORES):
        on = np.stack([B_outs[h]["o_out"][:, cix * TPC:(cix + 1) * TPC] for h in range(8)], axis=1)
        maps.append(dict(common, h_in=A_outs[cix]["h_out"], on=np.ascontiguousarray(on)))
    return maps


def build_D():
    P = Prog()
    c = setup_common(P)
    common_bufs(P, c)
    I = lambda n, s, d=F32: P.dram(n, s, d, "ExternalInput")
    O = lambda n, s, d=F32: P.dram(n, s, d, "ExternalOutput")
    h_d = I("h_in", [128, KC, TPC])
    qk_d = I("qkT", [16, 128, TPC], BF16)
    gs_d = I("gs", [16, 128, TPC], BF16)
    kz_d = I("kz", [16, 128, 1024], BF16)
    v_d = I("v", [16, 128, 2048], BF16)
    Rall_d = I("Rall", [8, 128, 8, 512])
    coef_d = I("coef", [128, 8, 4])
    decT_d = I("decT", [128, 4, 128])
    xi2_d = I("xi2", [128, 4, 2, 128])
    gch_d = I("gch", [128, 4])
    rsub_d = I("rsub", [128, 16])
    wo_rt_d = I("wo_rt", [KC, 128, 16, 128])
    fA_i, fA_o = I("fA_in", [NJ, 128, KC, 256]), I("fA_out", [KC, 128, NJ, 128])
    fB_i, fB_o = I("fB_in", [NJ, 128, KC, 256]), I("fB_out", [KC, 128, NJ, 128])
    fC_i, fC_o = I("fC_in", [NJ, 128, KC, 256]), I("fC_out", [KC, 128, NJ, 128])
    fD_i, fD_o = I("fD_in", [NJ, 128, KC, 256]), I("fD_out", [KC, 128, NJ, 128])
    g_d = I("gain", [128, 6, KC])
    wu_d = I("wu", [12, 128, KC, 256])
    wv_d = I("wv", [6, 128, KC, 512])
    vgain_d = I("vgain", [128, 3072])
    wsT_d = I("wsT", [128, 8, 128])
    bsb_d = I("bsb", [128, 8, 128])
    tri_d = I("tri", [128, 128])
    wo_sg_d = I("wo_sg", [KC, 128, 24, 128])
    wl_d = I("wl", [12, 128, KC, 256])
    y_o = O("y_rt", [16, 128, TPC], BF16)
    hscr_o = O("hscr", [128, KC, TPC])
    prod_o = O("prod", [24, 128, TPC], BF16)
    h_o = O("h_out", [128, KC, TPC])
    gg_o = O("gg_out", [12, 128, TPC], BF16)
    xb_o = O("xb_out", [12, 128, TPC])
    x2_o = O("x2_dbg", [128, KC, TPC])
    gain_t = load_small(P, c, "gain", g_d[:, :, :], [128, 6, KC])
    m0 = P.mark()
    coef = load_small(P, c, "coef", coef_d[:, :, :], [128, 8, 4])
    decT = load_small(P, c, "decT", decT_d[:, :, :], [128, 4, 128])
    xi2 = load_small(P, c, "xi2", xi2_d[:, :, :, :], [128, 4, 2, 128])
    gch = load_small(P, c, "gch", gch_d[:, :], [128, 4])
    rsub = load_small(P, c, "rsub", rsub_d[:, :], [128, 16])
    R = [newt(P, "R%d" % h, [128, 2, 512], F32) for h in range(4)]
    Rb = [newt(P, "Rb%d" % h, [128, 2, 512], BF16) for h in range(4)]
    rtmp = [newt(P, "rtmp%d" % i, [128, 2, 512], F32) for i in range(2)]
    for h in range(4):
        for j in range(8):
            t = rtmp[(h * 8 + j) % 2]
            P.dma("sp", t.ap[:], Rall_d[j][:, 2 * h:2 * h + 2, :], t, True)
            if j == 0:
                ts(P, V(R[h]), V(t), V(coef, coef.ap[:, j, h:h + 1]), None, ALU.mult)
            else:
                stt(P, V(R[h]), V(t), V(coef, coef.ap[:, j, h:h + 1]), V(R[h]), ALU.mult, ALU.add)
        cp(P, V(Rb[h]), V(R[h]), eng="act")
    qkb = newt(P, "qkb", [128, 16, 512], BF16)
    gsb = newt(P, "gsb", [128, 16, 512], BF16)
    kzb = newt(P, "kzb", [128, 4, 1024], BF16)
    vbk = newt(P, "vbk", [128, 4, 2048], BF16)
    ob = newt(P, "ob", [128, 16, 512], F32)
    atts = [newt(P, "att%d" % i, [128, 128], BF16) for i in range(2)]
    qxs = [newt(P, "qx%d" % i, [128, 2, 128], BF16) for i in range(2)]
    rstd = newt(P, "rsD", [128, 512], F32)
    yt = [newt(P, "yt%d" % i, [128, 512], F32) for i in range(2)]
    yst = [newt(P, "yst%d" % i, [128, 512], BF16) for i in range(3)]
    it = 0
    yi = 0
    for tb in range(TPC // 512):
        tsl = slice(tb * 512, (tb + 1) * 512)
        P.dma("sp", qkb.ap[:], qk_d[:, :, tsl].rearrange("c p t -> p c t"), qkb, True)
        P.dma("sp", gsb.ap[:], gs_d[:, :, tsl].rearrange("c p t -> p c t"), gsb, True)
        P.dma("sp", kzb.ap[:], kz_d[tb * 4:(tb + 1) * 4].rearrange("n p d -> p n d"), kzb, True)
        P.dma("sp", vbk.ap[:], v_d[tb * 4:(tb + 1) * 4].rearrange("n p d -> p n d"), vbk, True)
        for n in range(4):
            cols = slice(n * 128, (n + 1) * 128)
            for h in range(4):
                pa = P.ps()
                for dc in range(2):
                    mm(P, V(pa, pa.ap[:, 0:128]), V(qkb, qkb.ap[:, 8 + h * 2 + dc, cols]), V(qkb, qkb.ap[:, h * 2 + dc, cols]), dc == 0, dc == 1)
                attm = atts[it % 2]
                qx = qxs[it % 2]
                it += 1
                tt(P, V(attm), V(pa, pa.ap[:, 0:128]), V(decT, decT.ap[:, h, :]), ALU.mult)
                tt(P, V(qx), V(qkb, qkb.ap[:, h * 2:h * 2 + 2, cols]), V(xi2, xi2.ap[:, h, :, :]), ALU.mult)
                po = P.ps()
                for vc in range(4):
                    oc = V(po, po.ap[:, vc * 128:(vc + 1) * 128])
                    mm(P, oc, V(vbk, vbk.ap[:, n, h * 512 + vc * 128:h * 512 + (vc + 1) * 128]), V(attm), True, False)
                    mm(P, oc, V(Rb[h], Rb[h].ap[:, 0, vc * 128:(vc + 1) * 128]), V(qx, qx.ap[:, 0, :]), False, False)
                    mm(P, oc, V(Rb[h], Rb[h].ap[:, 1, vc * 128:(vc + 1) * 128]), V(qx, qx.ap[:, 1, :]), False, True)
                for vc in range(4):
                    cp(P, V(ob, ob.ap[:, h * 4 + vc, cols]), V(po, po.ap[:, vc * 128:(vc + 1) * 128]), eng="act")
                for dc in range(2):
                    pr_ = P.ps()
                    mm(P, V(pr_), V(kzb, kzb.ap[:, n, h * 256 + dc * 128:h * 256 + (dc + 1) * 128]), V(vbk, vbk.ap[:, n, h * 512:(h + 1) * 512]), True, True)
                    stt(P, V(R[h], R[h].ap[:, dc, :]), V(R[h], R[h].ap[:, dc, :]), V(gch, gch.ap[:, h:h + 1]), V(pr_), ALU.mult, ALU.add)
                cp(P, V(Rb[h]), V(R[h]), eng="act")
        for h in range(4):
            sumsq_rstd(P, c, [V(ob, ob.ap[:, h * 4 + vc, :]) for vc in range(4)], c.ones_f, 1.0 / 512, c.eps_t, rstd, 512)
            for vc in range(4):
                ch = h * 4 + vc
                y1 = yt[yi % 2]
                y2 = yst[yi % 3]
                yi += 1
                stt(P, V(y1), V(ob, ob.ap[:, ch, :]), V(rsub, rsub.ap[:, ch:ch + 1]), V(rstd), ALU.mult, ALU.mult)
                tt(P, V(y2), V(y1), V(gsb, gsb.ap[:, ch, :]), ALU.mult)
                P.dma("sp", y_o[ch][:, tsl], y2.ap[:], y2, False)
    P.release(m0)
    mB = P.mark()
    load_resid(P, c, h_d)
    m1 = P.mark()
    yall = newt(P, "yall", [128, 16, TPC], BF16)
    P.dma("sp", yall.ap[:], y_o[:, :, :].rearrange("c p t -> p c t"), yall, True)
    linear_resid(P, c, wo_rt_d, 16, lambda tb, k: V(yall, yall.ap[:, k, tb * 512:(tb + 1) * 512]), 1.0, "woR")
    P.release(m1)
    ffn_scoped(P, c, fA_i, fA_o, gain_t, 0)
    store_resid(P, c, x2_o)
    ffn_scoped(P, c, fB_i, fB_o, gain_t, 1)
    store_resid(P, c, hscr_o)
    P.release(mB)
    mC = P.mark()
    vgain = load_small(P, c, "vgain", vgain_d[:, :], [128, 3072])
    wsT = load_small(P, c, "wsT", wsT_d[:, :, :], [128, 8, 128])
    bsb = load_small(P, c, "bsb", bsb_d[:, :, :], [128, 8, 128])
    trif = load_small(P, c, "trif", tri_d[:, :], [128, 128])
    wsTb = newt(P, "wsTb", [128, 8, 128], BF16)
    bsb4 = newt(P, "bsb4", [128, 8, 512], F32)
    for g in range(8):
        tt(P, V(wsTb, wsTb.ap[:, g, :]), V(wsT, wsT.ap[:, g, :]), V(trif), ALU.mult)
        for s_ in range(4):
            cp(P, V(bsb4, bsb4.ap[:, g, s_ * 128:(s_ + 1) * 128]), V(bsb, bsb.ap[:, g, :]))
    xblk = newt(P, "xblk", [128, KC, 512], F32)
    xn = newt(P, "xnG", [128, KC, 512], BF16)
    rs2 = newt(P, "rsG", [128, 512], F32)
    gt = {"sets": [{"t1": newt(P, "gt1_%d" % i, [128, 512], F32), "t2": newt(P, "gt2_%d" % i, [128, 512], F32)} for i in range(3)]}
    wvb = [newt(P, "wvG%d" % i, [128, KC, 512], BF16) for i in range(2)]
    wub = [newt(P, "wuG%d" % i, [128, KC, 256], BF16) for i in range(2)]
    vg = newt(P, "vg", [128, 4, 3072], BF16)
    vtok = newt(P, "vtok", [128, 4, 3072], BF16)
    ss = newt(P, "ssG", [128, 24], F32)
    ssum = newt(P, "ssumG", [128, 4], F32)
    sqt = newt(P, "sqtG", [128, 512], F32)
    ut = [newt(P, "ut%d" % i, [128, 512], F32) for i in range(2)]
    t3 = [newt(P, "t3_%d" % i, [128, 512], F32) for i in range(2)]
    pst = [newt(P, "pst%d" % i, [128, 512], BF16) for i in range(3)]
    wi = 0
    for tb in range(TPC // 512):
        tsl = slice(tb * 512, (tb + 1) * 512)
        P.dma("sp", xblk.ap[:], hscr_o[:, :, tsl], xblk, True)
        norm_block(P, c, xblk, gain_t, 2, xn, rs2)
        for cb in range(6):
            w = wvb[wi % 2]
            wi += 1
            P.dma("pool", w.ap[:], wv_d[cb], w, True)
            for sub in range(4):
                pv = P.ps()
                for kc in range(KC):
                    mm(P, V(pv), V(xn, xn.ap[:, kc, sub * 128:(sub + 1) * 128]), V(w, w.ap[:, kc, :]), kc == 0, kc == KC - 1)
                vsl = V(vg, vg.ap[:, sub, cb * 512:(cb + 1) * 512])
                gelu_tanh(P, c, vsl, V(pv), gt)
                act(P, V(sqt), vsl, AF.Square)
                col = sub * 6 + cb
                P.op("dve", lambda e, col=col: e.reduce_sum(out=ss.ap[:, col:col + 1], in_=sqt.ap[:], axis=mybir.AxisListType.X), reads=[sqt], writes=[ss])
        for sub in range(4):
            P.op("dve", lambda e, sub=sub: e.reduce_sum(out=ssum.ap[:, sub:sub + 1], in_=ss.ap[:, sub * 6:(sub + 1) * 6], axis=mybir.AxisListType.X), reads=[ss], writes=[ssum])
        act(P, V(ssum), V(ssum), AF.Sqrt, bias=V(c.eps_t, c.eps_t.ap[:, 0:1]), scale=1.0 / 3072)
        recip(P, V(ssum), V(ssum))
        for sub in range(4):
            stt(P, V(vtok, vtok.ap[:, sub, :]), V(vg, vg.ap[:, sub, :]), V(ssum, ssum.ap[:, sub:sub + 1]), V(vgain), ALU.mult, ALU.mult)
        for cc in range(24):
            if cc % 2 == 0:
                wu = wub[(cc // 2) % 2]
                P.dma("pool", wu.ap[:], wu_d[cc // 2], wu, True)
            pu = P.ps()
            for kc in range(KC):
                mm(P, V(pu), V(wu, wu.ap[:, kc, (cc % 2) * 128:(cc % 2 + 1) * 128]), V(xn, xn.ap[:, kc, :]), kc == 0, kc == KC - 1)
            u = ut[cc % 2]
            gelu_tanh(P, c, V(u), V(pu), gt)
            pm = P.ps()
            for sub in range(4):
                mm(P, V(pm, pm.ap[:, sub * 128:(sub + 1) * 128]), V(vtok, vtok.ap[:, sub, cc * 128:(cc + 1) * 128]), V(wsTb, wsTb.ap[:, cc // 3, :]), True, True)
            t3_ = t3[cc % 2]
            tt(P, V(t3_), V(pm), V(bsb4, bsb4.ap[:, cc // 3, :]), ALU.add)
            ps_ = pst[cc % 3]
            tt(P, V(ps_), V(t3_), V(u), ALU.mult)
            P.dma("sp", prod_o[cc][:, tsl], ps_.ap[:], ps_, False)
    P.release(mC)
    load_resid(P, c, hscr_o)
    m2 = P.mark()
    pall = newt(P, "pall", [128, 24, TPC], BF16)
    P.dma("sp", pall.ap[:], prod_o[:, :, :].rearrange("c p t -> p c t"), pall, True)
    linear_resid(P, c, wo_sg_d, 24, lambda tb, k: V(pall, pall.ap[:, k, tb * 512:(tb + 1) * 512]), 1.0, "woG")
    P.release(m2)
    ffn_scoped(P, c, fC_i, fC_o, gain_t, 3)
    ffn_scoped(P, c, fD_i, fD_o, gain_t, 4)
    store_resid(P, c, h_o)
    xn2 = [newt(P, "xnL%d" % i, [128, KC, 512], BF16) for i in range(2)]
    rs3 = [newt(P, "rsL%d" % i, [128, 512], F32) for i in range(2)]
    gt2 = {"sets": [{"t1": newt(P, "gl1_%d" % i, [128, 512], F32), "t2": newt(P, "gl2_%d" % i, [128, 512], F32)} for i in range(3)]}
    wlb = [newt(P, "wlb%d" % i, [128, KC, 256], BF16) for i in range(2)]
    gst = [newt(P, "gst%d" % i, [128, 512], BF16) for i in range(2)]
    xst = [newt(P, "xst%d" % i, [128, 512], F32) for i in range(2)]
    for tb in range(TPC // 512):
        tsl = slice(tb * 512, (tb + 1) * 512)
        xb_ = xn2[tb % 2]
        norm_block(P, c, c.xT[tb], gain_t, 5, xb_, rs3[tb % 2])
        for blk in range(12):
            w = wlb[blk % 2]
            P.dma("pool", w.ap[:], wl_d[blk], w, True)
            for c2_ in range(2):
                ch = blk * 2 + c2_
                pl = P.ps()
                for kc in range(KC):
                    mm(P, V(pl), V(w, w.ap[:, kc, c2_ * 128:(c2_ + 1) * 128]), V(xb_, xb_.ap[:, kc, :]), kc == 0, kc == KC - 1)
                if ch < 12:
                    s_ = gst[ch % 2]
                    gelu_tanh(P, c, V(s_), V(pl), gt2)
                    P.dma("sp", gg_o[ch][:, tsl], s_.ap[:], s_, False)
                else:
                    s_ = xst[ch % 2]
                    cp(P, V(s_), V(pl), eng="act")
                    P.dma("sp", xb_o[ch - 12][:, tsl], s_.ap[:], s_, False)
    return P.finish()


def host_D(inp, C_outs):
    p = np.arange(128)
    G = [np.float64(g) for g in RT_G]
    decT = np.zeros((128, 4, 128), np.float32)
    xi2 = np.zeros((128, 4, 2, 128), np.float32)
    gch = np.zeros((128, 4), np.float32)
    for h in range(4):
        d = p[None, :] - p[:, None]
        decT[:, h, :] = np.where(d >= 0, RT_S * G[h] ** np.maximum(d, 0), 0.0)
        xi2[:, h, :, :] = (G[h] ** (p + 1.0))[None, None, :]
        gch[:, h] = G[h] ** 128
    tri = (p[:, None] <= p[None, :]).astype(np.float32)
    Rall = np.stack([o["R_out"] for o in C_outs])
    w = inp["sg_w_in"][0]
    wl = inp["lr_w_in"][0]
    common = {
        "Rall": Rall, "decT": decT, "xi2": xi2, "gch": gch,
        "rsub": np.ascontiguousarray(inp["rt_subln"][0].reshape(16, 128).T),
        "wo_rt": lay_w_rows(inp["rt_w_out"][0]),
        "fA_in": lay_w_in(inp["ffn2_w_in"][1]), "fA_out": lay_w_out(inp["ffn2_w_out"][1]),
        "fB_in": lay_w_in(inp["ffn1_w_in"][2]), "fB_out": lay_w_out(inp["ffn1_w_out"][2]),
        "fC_in": lay_w_in(inp["ffn2_w_in"][2]), "fC_out": lay_w_out(inp["ffn2_w_out"][2]),
        "fD_in": lay_w_in(inp["ffn1_w_in"][3]), "fD_out": lay_w_out(inp["ffn1_w_out"][3]),
        "gain": lay_gain(np.stack([inp["ffn2_norm"][1], inp["ffn1_norm"][2], inp["mix_norm"][2], inp["ffn2_norm"][2],
                                   inp["ffn1_norm"][3], inp["mix_norm"][3]])),
        "wu": lay_w_cols(w[:, :3072], 256), "wv": lay_w_cols(w[:, 3072:], 512),
        "vgain": np.ascontiguousarray(np.broadcast_to(inp["sg_v_gain"][0][None], (128, 3072))),
        "wsT": np.ascontiguousarray(inp["sg_w_s"][0].transpose(2, 0, 1)),
        "bsb": np.ascontiguousarray(np.broadcast_to(inp["sg_b_s"][0][None], (128, 8, 128))),
        "tri": tri,
        "wo_sg": lay_w_rows(inp["sg_w_out"][0]),
        "wl": lay_w_cols(wl, 256),
    }
    maps = []
    for cix in range(NCORES):
        coef = np.zeros((128, 8, 4), np.float32)
        for j in range(cix):
            for h in range(4):
                coef[:, j, h] = G[h] ** (2048.0 * (cix - 1 - j))
        o = C_outs[cix]
        maps.append(dict(common, h_in=o["h_out"], qkT=o["qkT_out"], gs=o["gs_out"], kz=o["kz_out"], v=o["v_out"], coef=coef))
    return maps


def build_E():
    P = Prog()
    c = setup_common(P)
    common_bufs(P, c)
    I = lambda n, s, d=F32: P.dram(n, s, d, "ExternalInput")
    O = lambda n, s, d=F32: P.dram(n, s, d, "ExternalOutput")
    xb_d = I("xb", [12, 128, TPC])
    halo_d = I("halo", [12, 128, 3])
    cw_d = I("cw", [128, 12, 4])
    cb_d = I("cb", [128, 12])
    wa_d = I("wa", [12, 128, 128])
    wx_d = I("wx", [12, 128, 128])
    ba_d = I("ba", [128, 12])
    bx_d = I("bx", [128, 12])
    lam_d = I("lam", [128, 12])
    hl_o = O("hloc", [12, 128, TPC])
    pc_o = O("pcum", [12, 128, TPC])
    ah_o = O("ah", [128, 12, 2])
    cw = load_small(P, c, "cw", cw_d[:, :, :], [128, 12, 4])
    cb = load_small(P, c, "cb", cb_d[:, :], [128, 12])
    ba = load_small(P, c, "ba", ba_d[:, :], [128, 12])
    bx = load_small(P, c, "bx", bx_d[:, :], [128, 12])
    lam = load_small(P, c, "lam", lam_d[:, :], [128, 12])
    sc = newt(P, "sc", [128, 12], F32)
    act(P, V(sc), V(lam), AF.Exp, scale=-1.0)
    ts(P, V(sc), V(sc), 1.0, None, ALU.add)
    act(P, V(sc), V(sc), AF.Ln)
    ts(P, V(sc), V(sc), -8.0, None, ALU.mult)
    zeros = const_tile(P, c, "zeros", 0.0, shape=(128, TPC))
    ah = newt(P, "ah", [128, 12, 2], F32)
    xpad = [newt(P, "xpad%d" % i, [128, TPC + 3], F32) for i in range(2)]
    xc = newt(P, "xc", [128, TPC], F32)
    xcb = newt(P, "xcb", [128, TPC], BF16)
    a_t = newt(P, "a_t", [128, TPC], F32)
    b_t = newt(P, "b_t", [128, TPC], F32)
    r_t = [newt(P, "r_t%d" % i, [128, 512], F32) for i in range(2)]
    i_t = [newt(P, "i_t%d" % i, [128, 512], F32) for i in range(2)]
    hl = [newt(P, "hl%d" % i, [128, TPC], F32) for i in range(2)]
    pc = [newt(P, "pc%d" % i, [128, TPC], F32) for i in range(2)]
    wab = [newt(P, "wab%d" % i, [128, 128], BF16) for i in range(2)]
    wxb = [newt(P, "wxb%d" % i, [128, 128], BF16) for i in range(2)]
    for n in range(12):
        xp = xpad[n % 2]
        P.dma("sp", xp.ap[:, 3:], xb_d[n], xp, True)
        P.dma("sp", xp.ap[:, 0:3], halo_d[n], xp, True)
        wa = wab[n % 2]
        wx = wxb[n % 2]
        P.dma("pool", wa.ap[:], wa_d[n], wa, True)
        P.dma("pool", wx.ap[:], wx_d[n], wx, True)
        ts(P, V(xc), V(xp, xp.ap[:, 0:TPC]), V(cw, cw.ap[:, n, 0:1]), V(cb, cb.ap[:, n:n + 1]), ALU.mult, ALU.add)
        for j in range(1, 4):
            stt(P, V(xc), V(xp, xp.ap[:, j:j + TPC]), V(cw, cw.ap[:, n, j:j + 1]), V(xc), ALU.mult, ALU.add)
        cp(P, V(xcb), V(xc), eng="act")
        for blk in range(4):
            sl = slice(blk * 512, (blk + 1) * 512)
            pr_ = P.ps()
            mm(P, V(pr_), V(wa), V(xcb, xcb.ap[:, sl]), True, True)
            pi_ = P.ps()
            mm(P, V(pi_), V(wx), V(xcb, xcb.ap[:, sl]), True, True)
            r = r_t[blk % 2]
            ii = i_t[blk % 2]
            act(P, V(r), V(pr_), AF.Sigmoid, bias=V(ba, ba.ap[:, n:n + 1]))
            act(P, V(ii), V(pi_), AF.Sigmoid, bias=V(bx, bx.ap[:, n:n + 1]))
            act(P, V(a_t, a_t.ap[:, sl]), V(r), AF.Exp, scale=V(sc, sc.ap[:, n:n + 1]))
            tt(P, V(r), V(a_t, a_t.ap[:, sl]), V(a_t, a_t.ap[:, sl]), ALU.mult)
            ts(P, V(r), V(r), -1.0, 1.0, ALU.mult, ALU.add)
            ts(P, V(r), V(r), 1e-12, None, ALU.max)
            act(P, V(r), V(r), AF.Sqrt)
            tt(P, V(ii), V(ii), V(xc, xc.ap[:, sl]), ALU.mult)
            tt(P, V(b_t, b_t.ap[:, sl]), V(r), V(ii), ALU.mult)
        h_ = hl[n % 2]
        p_ = pc[n % 2]
        P.op("dve", lambda e, h_=h_: e.tensor_tensor_scan(out=h_.ap[:], data0=a_t.ap[:], data1=b_t.ap[:], initial=0.0, op0=ALU.mult, op1=ALU.add),
             reads=[a_t, b_t], writes=[h_])
        P.op("dve", lambda e, p_=p_: e.tensor_tensor_scan(out=p_.ap[:], data0=a_t.ap[:], data1=zeros.ap[:], initial=1.0, op0=ALU.mult, op1=ALU.add),
             reads=[a_t, zeros], writes=[p_])
        cp(P, V(ah, ah.ap[:, n, 0:1]), V(p_, p_.ap[:, TPC - 1:TPC]))
        cp(P, V(ah, ah.ap[:, n, 1:2]), V(h_, h_.ap[:, TPC - 1:TPC]))
        P.dma("sp", hl_o[n], h_.ap[:], h_, False)
        P.dma("sp", pc_o[n], p_.ap[:], p_, False)
    P.dma("sp", ah_o[:, :, :], ah.ap[:], ah, False)
    return P.finish()


def lay_vec12(v):
    return np.ascontiguousarray(v.reshape(12, 128).T)


def host_E(inp, D_outs):
    common = {
        "cw": np.ascontiguousarray(inp["lr_conv_w"][0].reshape(4, 12, 128).transpose(2, 1, 0)),
        "cb": lay_vec12(inp["lr_conv_b"][0]),
        "wa": np.ascontiguousarray(inp["lr_w_a"][0]), "wx": np.ascontiguousarray(inp["lr_w_x"][0]),
        "ba": lay_vec12(inp["lr_b_a"][0]), "bx": lay_vec12(inp["lr_b_x"][0]), "lam": lay_vec12(inp["lr_lambda"][0]),
    }
    maps = []
    for cix in range(NCORES):
        if cix == 0:
            halo = np.zeros((12, 128, 3), np.float32)
        else:
            halo = np.ascontiguousarray(D_outs[cix - 1]["xb_out"][:, :, TPC - 3:])
        maps.append(dict(common, xb=D_outs[cix]["xb_out"], halo=halo))
    return maps


def build_F():
    P = Prog()
    c = setup_common(P)
    common_bufs(P, c)
    I = lambda n, s, d=F32: P.dram(n, s, d, "ExternalInput")
    O = lambda n, s, d=F32: P.dram(n, s, d, "ExternalOutput")
    h_d = I("h_in", [128, KC, TPC])
    hl_d = I("hloc", [12, 128, TPC])
    pc_d = I("pcum", [12, 128, TPC])
    gg_d = I("gg", [12, 128, TPC], BF16)
    ahall_d = I("ahall", [128, 8, 12, 2])
    msk_d = I("msk", [128, 8, 2])
    wo_d = I("wo", [KC, 128, 12, 128])
    f_i, f_o = I("f_in", [NJ, 128, KC, 256]), I("f_out", [KC, 128, NJ, 128])
    g_d = I("gain", [128, 1, KC])
    y_o = O("out", [128, KC, TPC])
    gain_t = load_small(P, c, "gain", g_d[:, :, :], [128, 1, KC])
    ahall = load_small(P, c, "ahall", ahall_d[:, :, :, :], [128, 8, 12, 2])
    msk = load_small(P, c, "msk", msk_d[:, :, :], [128, 8, 2])
    hs = const_tile(P, c, "hs", 0.0, shape=(128, 12))
    A_ = newt(P, "A_", [128, 12], F32)
    H_ = newt(P, "H_", [128, 12], F32)
    for j in range(8):
        ts(P, V(A_), V(ahall, ahall.ap[:, j, :, 0]), V(msk, msk.ap[:, j, 0:1]), V(msk, msk.ap[:, j, 1:2]), ALU.mult, ALU.add)
        ts(P, V(H_), V(ahall, ahall.ap[:, j, :, 1]), V(msk, msk.ap[:, j, 0:1]), None, ALU.mult)
        tt(P, V(hs), V(hs), V(A_), ALU.mult)
        tt(P, V(hs), V(hs), V(H_), ALU.add)
    load_resid(P, c, h_d)
    m0 = P.mark()
    yall = newt(P, "yallF", [128, 12, TPC], BF16)
    hlt = [newt(P, "hlt%d" % i, [128, TPC], F32) for i in range(2)]
    pct = [newt(P, "pct%d" % i, [128, TPC], F32) for i in range(2)]
    ggt = [newt(P, "ggt%d" % i, [128, TPC], BF16) for i in range(2)]
    for n in range(12):
        a, b, g = hlt[n % 2], pct[n % 2], ggt[n % 2]
        P.dma("sp", a.ap[:], hl_d[n], a, True)
        P.dma("sp", b.ap[:], pc_d[n], b, True)
        P.dma("sp", g.ap[:], gg_d[n], g, True)
        stt(P, V(a), V(b), V(hs, hs.ap[:, n:n + 1]), V(a), ALU.mult, ALU.add)
        tt(P, V(yall, yall.ap[:, n, :]), V(a), V(g), ALU.mult)
    linear_resid(P, c, wo_d, 12, lambda tb, k: V(yall, yall.ap[:, k, tb * 512:(tb + 1) * 512]), 1.0, "woF")
    P.release(m0)
    ffn_scoped(P, c, f_i, f_o, gain_t, 0)
    store_resid(P, c, y_o)
    return P.finish()


def host_F(inp, D_outs, E_outs):
    ahall = np.ascontiguousarray(np.stack([o["ah"] for o in E_outs], axis=1))
    common = {
        "ahall": ahall, "wo": lay_w_rows(inp["lr_w_out"][0]),
        "f_in": lay_w_in(inp["ffn2_w_in"][3]), "f_out": lay_w_out(inp["ffn2_w_out"][3]),
        "gain": lay_gain(inp["ffn2_norm"][3:4]),
    }
    maps = []
    for cix in range(NCORES):
        msk = np.zeros((128, 8, 2), np.float32)
        msk[:, :, 1] = 1.0
        msk[:, :cix, 0] = 1.0
        msk[:, :cix, 1] = 0.0
        maps.append(dict(common, h_in=D_outs[cix]["h_out"], hloc=E_outs[cix]["hloc"], pcum=E_outs[cix]["pcum"],
                         gg=D_outs[cix]["gg_out"], msk=msk))
    return maps


_CACHE = {}


def _prog(name, fn):
    if name not in _CACHE:
        _CACHE[name] = fn()
    return _CACHE[name]


def _run(nc, maps):
    res = run_bass_kernel_spmd(nc, maps, core_ids=list(range(NCORES)))
    return [dict(r) for r in res.results]


def kernel(**inp):
    inp = {k: np.asarray(v) for k, v in inp.items()}
    A = _run(build_A(), host_A(inp))
    B = _run(build_B(0.8 - 0.6 * 1.0), host_B(inp, A))
    C = _run(build_C(), host_C(inp, A, B))
    del B
    D_ = _run(build_D(), host_D(inp, C))
    del A, C
    E = _run(build_E(), host_E(inp, D_))
    F = _run(build_F(), host_F(inp, D_, E))
    out = unlay_resid([o["out"] for o in F])
    return out.reshape(1, SEQ, D).astype(np.float32)
```

```python
import numpy as np
import concourse.bass as bass
import concourse.mybir as mybir
from concourse.bass_utils import run_bass_kernel_spmd

F32 = mybir.dt.float32
BF16 = mybir.dt.bfloat16
AF = mybir.ActivationFunctionType
ALU = mybir.AluOpType

NCORES = 8
D = 1024
SEQ = 16384
TPC = SEQ // NCORES
DFF = 2816
NJ = DFF // 128
KC = D // 128
EPS = 1e-6
SAME_ENG_SYNC = True


class TT:
    __slots__ = ("ap", "lw", "rd", "dsem", "dcnt", "name")

    def __init__(self, ap, name=""):
        self.ap = ap
        self.lw = None
        self.rd = {}
        self.dsem = None
        self.dcnt = 0
        self.name = name


class Prog:
    ENG = ("pe", "act", "dve", "pool", "sp")

    def __init__(self):
        self.nc = bass.Bass("TRN2", target_bir_lowering=False)
        self.ops = {e: [] for e in self.ENG}
        self.cnt = {e: 0 for e in self.ENG}
        self.waited = {e: {} for e in self.ENG}
        self.sems = {}
        self.stack = []
        self.dma_tiles = []
        self.dma_final = {}
        self.all_eng_sems = []
        self.nsem = 0
        for e in self.ENG:
            self.sems[e] = self._sem("e_" + e)
        self.psum = []

    def _enter(self, cm):
        v = cm.__enter__()
        self.stack.append(cm)
        return v

    def _sem(self, name):
        self.nsem += 1
        cm = self.nc.semaphore(name + "_%d" % self.nsem)
        v = cm.__enter__()
        try:
            cm._is_sem = True
            self.stack.append(cm)
        except Exception:
            self.semstack = getattr(self, "semstack", [])
            self.semstack.append(cm)
        return v

    def sbuf(self, name, shape, dt):
        self.nsb = getattr(self, "nsb", 0) + 1
        return self._enter(self.nc.sbuf_tensor("sb%d_%s" % (self.nsb, name), list(shape), dt))

    def psum_tiles(self):
        if not self.psum:
            for i in range(8):
                h = self._enter(self.nc.psum_tensor("ps%d" % i, [128, 512], F32))
                self.psum.append(TT(h, "ps%d" % i))
            self.psi = 0
        return self.psum

    def ps(self):
        self.psum_tiles()
        pool = getattr(self, "ps_pool", list(range(8)))
        t = self.psum[pool[self.psi % len(pool)]]
        self.psi += 1
        return t

    def barrier(self):
        evs = [(self.sems[e], self.cnt[e], e) for e in self.ENG if self.cnt[e] > 0 and e != "sp"]
        devs = list(self.dma_final.values())
        for eng in self.ENG:
            wl = []
            wd = self.waited[eng]
            for sm, v, e_ in evs:
                if e_ != eng and wd.get(id(sm), 0) < v:
                    wd[id(sm)] = v
                    wl.append((sm, v))
            for sm, v in devs:
                if wd.get(id(sm), 0) < v:
                    wd[id(sm)] = v
                    wl.append((sm, v))

            def emit(e, wl=wl):
                for s_, v in wl:
                    e.wait_ge(s_, v)
            self.ops[eng].append(emit)

    def mark(self):
        return len(self.stack)

    def release(self, m):
        self.barrier()
        keep = []
        while len(self.stack) > m:
            cm = self.stack.pop()
            if getattr(cm, "_is_sem", False):
                keep.append(cm)
            else:
                cm.__exit__(None, None, None)
        self.stack.extend(reversed(keep))

    def dram(self, name, shape, dt, kind):
        return self.nc.dram_tensor(name, list(shape), dt, kind=kind).ap()

    EPOCH = 1500

    def _deps(self, eng, reads, writes):
        need = {}

        def add(ev):
            if ev is None:
                return
            sm, v, e = ev
            if e == eng and (eng == "pe" or not SAME_ENG_SYNC):
                return
            k = id(sm)
            if k not in need or need[k][1] < v:
                need[k] = (sm, v)
        for t in reads:
            add(t.lw)
        for t in writes:
            add(t.lw)
            for ev in t.rd.values():
                add(ev)
        out = []
        wd = self.waited[eng]
        for k, (sm, v) in need.items():
            if wd.get(k, 0) < v:
                wd[k] = v
                out.append((sm, v))
        return out

    def op(self, eng, fn, reads=(), writes=()):
        wl = self._deps(eng, reads, writes)
        if self.cnt[eng] >= self.EPOCH:
            self.sems[eng] = self._sem("e_" + eng)
            self.cnt[eng] = 0
        self.cnt[eng] += 1
        val = self.cnt[eng]
        sem = self.sems[eng]

        def emit(e):
            for s, v in wl:
                e.wait_ge(s, v)
            fn(e).then_inc(sem, 1)
        self.ops[eng].append(emit)
        for t in reads:
            t.rd[eng] = (sem, val, eng)
        for t in writes:
            t.lw = (sem, val, eng)
            t.rd = {}

    def dma(self, q, out_ap, in_ap, tile, is_load):
        if tile.dsem is None or tile.dcnt >= 16 * 100:
            tile.dsem = self._sem("d")
            tile.dcnt = 0
            self.dma_tiles.append((tile, tile.dsem))
        wl = self._deps(q, () if is_load else (tile,), (tile,) if is_load else ())
        tile.dcnt += 16
        val = tile.dcnt
        sem = tile.dsem
        self.dma_final[id(sem)] = (sem, val)

        def emit(e):
            for s, v in wl:
                e.wait_ge(s, v)
            e.dma_start(out=out_ap, in_=in_ap).then_inc(sem, 16)
        self.ops[q].append(emit)
        if is_load:
            tile.lw = (sem, val, None)
            tile.rd = {}
        else:
            tile.rd[id(sem)] = (sem, val, None)

    def finish(self):
        finals = list(self.dma_final.values())

        def emit(e):
            for s, v in finals:
                e.wait_ge(s, v)
        self.ops["sp"].append(emit)
        nc = self.nc
        ops = self.ops
        with nc.Block() as block:
            @block.tensor
            def _(e):
                for f in ops["pe"]:
                    f(e)

            @block.scalar
            def _(e):
                for f in ops["act"]:
                    f(e)

            @block.vector
            def _(e):
                for f in ops["dve"]:
                    f(e)

            @block.gpsimd
            def _(e):
                for f in ops["pool"]:
                    f(e)

            @block.sync
            def _(e):
                for f in ops["sp"]:
                    f(e)
        while self.stack:
            self.stack.pop().__exit__(None, None, None)
        return nc


class Ctx:
    pass


def setup_common(P):
    c = Ctx()
    c.ones_f = TT(P.sbuf("ones_f", [128, 128], F32), "ones_f")
    P.op("dve", lambda e: e.memset(c.ones_f.ap[:], 1.0), writes=[c.ones_f])
    return c


def load_resid(P, c, x_dram):
    c.xT_h = P.sbuf("xT", [128, KC, TPC], F32)
    c.xT_all = TT(c.xT_h, "xT")
    P.dma("sp", c.xT_h[:], x_dram[:, :, :], c.xT_all, True)
    c.xT = []
    for tb in range(TPC // 512):
        t = TT(c.xT_h[:, :, tb * 512:(tb + 1) * 512], "xT%d" % tb)
        t.lw = c.xT_all.lw
        c.xT.append(t)


def store_resid(P, c, y_dram):
    tall = c.xT_all
    for t in c.xT:
        pass
    for tb, t in enumerate(c.xT):
        P.dma("sp", y_dram[:, :, tb * 512:(tb + 1) * 512], t.ap, t, False)


def rms_norm(P, c, src_blk, gain_ap_fn, dst, n_feat_chunks, tb_cols, tmp_sq, rstd, src_ap_fn, inv_n):
    raise NotImplementedError


def ffn(P, c, pre, w_in_d, w_out_d, gain_t, gidx):
    if not hasattr(c, "ffn_bufs"):
        b = Ctx()
        b.xn = [TT(P.sbuf("xn%d" % i, [128, KC, 512], BF16), "xn%d" % i) for i in range(2)]
        b.gT = [TT(P.sbuf("gT%d" % i, [128, NJ, 512], BF16), "gT%d" % i) for i in range(2)]
        b.win = [TT(P.sbuf("win%d" % i, [128, KC, 256], BF16), "win%d" % i) for i in range(3)]
        b.wout = [TT(P.sbuf("wout%d" % i, [128, NJ, 128], BF16), "wout%d" % i) for i in range(2)]
        b.sq = [TT(P.sbuf("sq%d" % i, [128, 512], F32), "sq%d" % i) for i in range(2)]
        b.rstd = [TT(P.sbuf("rstd%d" % i, [128, 512], F32), "rstd%d" % i) for i in range(2)]
        b.sa = [TT(P.sbuf("sa%d" % i, [128, 512], F32), "sa%d" % i) for i in range(2)]
        b.i = 0
        c.ffn_bufs = b
    b = c.ffn_bufs
    nblk = TPC // 512
    for half in range(nblk // 2):
        tbs = [2 * half, 2 * half + 1]
        for ii, tb in enumerate(tbs):
            x = c.xT[tb]
            xn = b.xn[ii]
            rstd = b.rstd[ii]
            pss = P.ps()
            for kc in range(KC):
                sq = b.sq[kc % 2]
                P.op("act", lambda e, sq=sq, x=x, kc=kc: e.activation(out=sq.ap[:], in_=x.ap[:, kc, :], func=AF.Square),
                     reads=[x], writes=[sq])
                P.op("pe", lambda e, sq=sq, kc=kc, pss=pss: e.matmul(pss.ap[:], c.ones_f.ap[:], sq.ap[:], start=(kc == 0), stop=(kc == KC - 1)),
                     reads=[sq, c.ones_f], writes=[pss])
            P.op("act", lambda e, rstd=rstd, pss=pss: e.activation(out=rstd.ap[:], in_=pss.ap[:], func=AF.Sqrt, scale=1.0 / D, bias=c.eps_t.ap[:, 0:1]),
                 reads=[pss, c.eps_t], writes=[rstd])
            P.op("dve", lambda e, rstd=rstd: e.reciprocal(out=rstd.ap[:], in_=rstd.ap[:]), reads=[rstd], writes=[rstd])
            for kc in range(KC):
                P.op("dve", lambda e, xn=xn, x=x, kc=kc, rstd=rstd: e.scalar_tensor_tensor(
                    out=xn.ap[:, kc, :], in0=x.ap[:, kc, :], scalar=gain_t.ap[:, gidx, kc:kc + 1], in1=rstd.ap[:],
                    op0=ALU.mult, op1=ALU.mult), reads=[x, rstd, gain_t], writes=[xn])
        for j in range(NJ):
            w = b.win[b.i % 3]
            b.i += 1
            P.dma("pool", w.ap[:], w_in_d[j], w, True)
            for ii, tb in enumerate(tbs):
                xn = b.xn[ii]
                gT = b.gT[ii]
                pa = P.ps()
                pb = P.ps()
                for kc in range(KC):
                    P.op("pe", lambda e, pa=pa, w=w, xn=xn, kc=kc: e.matmul(pa.ap[:], w.ap[:, kc, 0:128], xn.ap[:, kc, :], start=(kc == 0), stop=(kc == KC - 1)),
                         reads=[w, xn], writes=[pa])
                for kc in range(KC):
                    P.op("pe", lambda e, pb=pb, w=w, xn=xn, kc=kc: e.matmul(pb.ap[:], w.ap[:, kc, 128:256], xn.ap[:, kc, :], start=(kc == 0), stop=(kc == KC - 1)),
                         reads=[w, xn], writes=[pb])
                sa = b.sa[(j * 2 + ii) % 2]
                P.op("act", lambda e, sa=sa, pa=pa: e.activation(out=sa.ap[:], in_=pa.ap[:], func=AF.Silu), reads=[pa], writes=[sa])
                P.op("dve", lambda e, gT=gT, sa=sa, pb=pb, j=j: e.tensor_tensor(out=gT.ap[:, j, :], in0=sa.ap[:], in1=pb.ap[:], op=ALU.mult),
                     reads=[sa, pb], writes=[gT])
        for n in range(KC):
            w = b.wout[n % 2]
            P.dma("pool", w.ap[:], w_out_d[n], w, True)
            for ii, tb in enumerate(tbs):
                x = c.xT[tb]
                gT = b.gT[ii]
                po = P.ps()
                for j in range(NJ):
                    P.op("pe", lambda e, po=po, w=w, gT=gT, j=j: e.matmul(po.ap[:], w.ap[:, j, :], gT.ap[:, j, :], start=(j == 0), stop=(j == NJ - 1)),
                         reads=[w, gT], writes=[po])
                P.op("dve", lambda e, x=x, po=po, n=n: e.scalar_tensor_tensor(
                    out=x.ap[:, n, :], in0=po.ap[:], scalar=0.5, in1=x.ap[:, n, :], op0=ALU.mult, op1=ALU.add),
                    reads=[po, x], writes=[x])


def load_small(P, c, name, dram_ap, shape):
    t = TT(P.sbuf(name, shape, F32), name)
    P.dma("sp", t.ap[:], dram_ap, t, True)
    return t


def lay_resid(x_tok):
    out = []
    for c in range(NCORES):
        xs = x_tok[c * TPC:(c + 1) * TPC]
        out.append(np.ascontiguousarray(xs.T.reshape(KC, 128, TPC).transpose(1, 0, 2)))
    return out


def unlay_resid(shards):
    outs = []
    for s in shards:
        outs.append(s.transpose(1, 0, 2).reshape(D, TPC).T)
    return np.ascontiguousarray(np.concatenate(outs, axis=0))


def lay_w_in(w):
    a = w[:, :DFF].reshape(KC, 128, NJ, 128)
    b = w[:, DFF:].reshape(KC, 128, NJ, 128)
    ab = np.concatenate([a, b], axis=3)
    return np.ascontiguousarray(ab.transpose(2, 1, 0, 3))


def lay_w_out(w):
    return np.ascontiguousarray(w.reshape(NJ, 128, KC, 128).transpose(2, 1, 0, 3))


def lay_gain(g):
    n = g.shape[0]
    return np.ascontiguousarray(g.reshape(n, KC, 128).transpose(2, 0, 1))


def V(t, ap=None):
    return (t, t.ap[:] if ap is None else ap)


def _tl(*ops):
    return [o[0] for o in ops if o is not None and isinstance(o, tuple)]


def act(P, out, in_, func, bias=None, scale=None, accum=None):
    kw = {}
    rd = [in_[0]]
    if bias is not None:
        if isinstance(bias, tuple):
            kw["bias"] = bias[1]
            rd.append(bias[0])
        else:
            kw["bias"] = bias
    if scale is not None:
        if isinstance(scale, tuple):
            kw["scale"] = scale[1]
            rd.append(scale[0])
        else:
            kw["scale"] = scale
    wr = [out[0]]
    if accum is not None:
        kw["accum_out"] = accum[1]
        wr.append(accum[0])
    P.op("act", lambda e: e.activation(out=out[1], in_=in_[1], func=func, **kw), reads=rd, writes=wr)


def mm(P, out, lhsT, rhs, start, stop):
    P.op("pe", lambda e: e.matmul(out[1], lhsT[1], rhs[1], start=start, stop=stop), reads=[lhsT[0], rhs[0]], writes=[out[0]])


def tt(P, out, a, b, op, eng="dve"):
    P.op(eng, lambda e: e.tensor_tensor(out=out[1], in0=a[1], in1=b[1], op=op), reads=[a[0], b[0]], writes=[out[0]])


def stt(P, out, a, scalar, b, op0, op1, eng="dve"):
    rd = [a[0], b[0]]
    sc = scalar
    if isinstance(scalar, tuple):
        rd.append(scalar[0])
        sc = scalar[1]
    P.op(eng, lambda e: e.scalar_tensor_tensor(out=out[1], in0=a[1], scalar=sc, in1=b[1], op0=op0, op1=op1), reads=rd, writes=[out[0]])


def ts(P, out, a, s1, s2, op0, op1=None, eng="dve", accum=None):
    rd = [a[0]]
    v1 = s1
    if isinstance(s1, tuple):
        rd.append(s1[0])
        v1 = s1[1]
    v2 = s2
    if isinstance(s2, tuple):
        rd.append(s2[0])
        v2 = s2[1]
    kw = {}
    if op1 is not None:
        kw["op1"] = op1
    wr = [out[0]]
    if accum is not None:
        kw["accum_out"] = accum[1]
        wr.append(accum[0])
    P.op(eng, lambda e: e.tensor_scalar(out=out[1], in0=a[1], scalar1=v1, scalar2=v2, op0=op0, **kw), reads=rd, writes=wr)


def cp(P, out, a, eng="dve"):
    if eng == "act":
        P.op("act", lambda e: e.activation(out=out[1], in_=a[1], func=AF.Copy), reads=[a[0]], writes=[out[0]])
    else:
        P.op(eng, lambda e: e.tensor_copy(out=out[1], in_=a[1]), reads=[a[0]], writes=[out[0]])


def recip(P, out, a):
    P.op("dve", lambda e: e.reciprocal(out=out[1], in_=a[1]), reads=[a[0]], writes=[out[0]])


def const_tile(P, c, name, val, shape=(128, 1), dt=F32):
    t = TT(P.sbuf(name, list(shape), dt), name)
    P.op("dve", lambda e: e.memset(t.ap[:], val), writes=[t])
    return t


def newt(P, name, shape, dt):
    return TT(P.sbuf(name, list(shape), dt), name)


def sumsq_rstd(P, c, srcs, lhsT, scale, bias_t, rstd, ncols):
    pss = P.ps()
    n = len(srcs)
    for i, s in enumerate(srcs):
        sq = c.sq[c.sqi % 2]
        c.sqi += 1
        act(P, V(sq, sq.ap[:, :ncols]), s, AF.Square)
        mm(P, V(pss, pss.ap[:, :ncols]), V(lhsT), V(sq, sq.ap[:, :ncols]), i == 0, i == n - 1)
    act(P, V(rstd, rstd.ap[:, :ncols]), V(pss, pss.ap[:, :ncols]), AF.Sqrt, bias=V(bias_t, bias_t.ap[:, 0:1]), scale=scale)
    recip(P, V(rstd, rstd.ap[:, :ncols]), V(rstd, rstd.ap[:, :ncols]))


def norm_block(P, c, x, gain_t, gidx, xn, rstd):
    sumsq_rstd(P, c, [V(x, x.ap[:, kc, :]) for kc in range(KC)], c.ones_f, 1.0 / D, c.eps_t, rstd, 512)
    for kc in range(KC):
        stt(P, V(xn, xn.ap[:, kc, :]), V(x, x.ap[:, kc, :]), V(gain_t, gain_t.ap[:, gidx, kc:kc + 1]), V(rstd), ALU.mult, ALU.mult)


def common_bufs(P, c):
    c.sq = [newt(P, "sqc%d" % i, [128, 512], F32) for i in range(2)]
    c.sqi = 0
    c.eps_t = const_tile(P, c, "eps", EPS)


def ffn_scoped(P, c, w_in_d, w_out_d, gain_t, gidx):
    m = P.mark()
    ffn(P, c, None, w_in_d, w_out_d, gain_t, gidx)
    del c.ffn_bufs
    P.release(m)


def linear_resid(P, c, w_d, nk, src_fn, scale, wname):
    wt = [newt(P, "%s%d" % (wname, i), [128, nk, 128], BF16) for i in range(2)]
    for n in range(KC):
        w = wt[n % 2]
        P.dma("pool", w.ap[:], w_d[n], w, True)
        for tb in range(TPC // 512):
            x = c.xT[tb]
            po = P.ps()
            for k in range(nk):
                mm(P, V(po), V(w, w.ap[:, k, :]), src_fn(tb, k), k == 0, k == nk - 1)
            stt(P, V(x, x.ap[:, n, :]), V(po), scale, V(x, x.ap[:, n, :]), ALU.mult, ALU.add)


def lay_w_cols(w, cb):
    K_, N_ = w.shape
    return np.ascontiguousarray(w.reshape(K_ // 128, 128, N_ // cb, cb).transpose(2, 1, 0, 3))


def lay_w_rows(w):
    K_ = w.shape[0]
    return np.ascontiguousarray(w.reshape(K_ // 128, 128, KC, 128).transpose(2, 1, 0, 3))


def build_A():
    P = Prog()
    c = setup_common(P)
    common_bufs(P, c)
    x_d = P.dram("x", [128, KC, TPC], F32, "ExternalInput")
    win_d = P.dram("w_in", [NJ, 128, KC, 256], F32, "ExternalInput")
    wout_d = P.dram("w_out", [KC, 128, NJ, 128], F32, "ExternalInput")
    g_d = P.dram("gain", [128, 2, KC], F32, "ExternalInput")
    wqk_d = P.dram("wqk", [4, 128, KC, 512], F32, "ExternalInput")
    wv_d = P.dram("wv", [2, 128, KC, 512], F32, "ExternalInput")
    qkg_d = P.dram("qkg", [128, 2], F32, "ExternalInput")
    bd_d = P.dram("bd", [128, 128], F32, "ExternalInput")
    h_o = P.dram("h_out", [128, KC, TPC], F32, "ExternalOutput")
    qk_o = P.dram("qk_out", [16, 128, TPC], BF16, "ExternalOutput")
    v_o = P.dram("v_out", [TPC // 128, 128, 1024], BF16, "ExternalOutput")
    gain_t = load_small(P, c, "gain", g_d[:, :, :], [128, 2, KC])
    qkg = load_small(P, c, "qkg", qkg_d[:, :], [128, 2])
    bd = load_small(P, c, "bd", bd_d[:, :], [128, 128])
    eps64 = const_tile(P, c, "eps64", 64 * EPS)
    load_resid(P, c, x_d)
    ffn_scoped(P, c, win_d, wout_d, gain_t, 0)
    store_resid(P, c, h_o)
    xn = [newt(P, "xnA%d" % i, [128, KC, 512], BF16) for i in range(2)]
    rstd = [newt(P, "rsA%d" % i, [128, 512], F32) for i in range(2)]
    wv = [newt(P, "wv%d" % i, [128, KC, 512], BF16) for i in range(2)]
    wqk = [newt(P, "wqk%d" % i, [128, KC, 512], BF16) for i in range(2)]
    qst = [newt(P, "qst%d" % i, [128, 512], BF16) for i in range(3)]
    vst = [newt(P, "vst%d" % i, [128, 1024], BF16) for i in range(2)]
    rq = [newt(P, "rq%d" % i, [128, 512], F32) for i in range(2)]
    for i in range(2):
        P.dma("pool", wv[i].ap[:], wv_d[i], wv[i], True)
    wi = 0
    si = 0
    for tb in range(TPC // 512):
        x = c.xT[tb]
        xb = xn[tb % 2]
        norm_block(P, c, x, gain_t, 1, xb, rstd[tb % 2])
        for blk in range(4):
            w = wqk[wi % 2]
            wi += 1
            P.dma("pool", w.ap[:], wqk_d[blk], w, True)
            for cc in range(4):
                ch = blk * 4 + cc
                isq = ch < 8
                pq = P.ps()
                for kc in range(KC):
                    mm(P, V(pq), V(w, w.ap[:, kc, cc * 128:(cc + 1) * 128]), V(xb, xb.ap[:, kc, :]), kc == 0, kc == KC - 1)
                r = rq[si % 2]
                if isq:
                    sumsq_rstd(P, c, [V(pq)], bd, 1.0, eps64, r, 512)
                else:
                    sumsq_rstd(P, c, [V(pq)], bd, 1.0 / 64, c.eps_t, r, 512)
                st = qst[si % 3]
                si += 1
                stt(P, V(st), V(pq), V(qkg, qkg.ap[:, (0 if isq else 1):(1 if isq else 2)]), V(r), ALU.mult, ALU.mult)
                P.dma("sp", qk_o[ch][:, tb * 512:(tb + 1) * 512], st.ap[:], st, False)
        for sub in range(4):
            vs = vst[sub % 2]
            for cb in range(2):
                pv = P.ps()
                for kc in range(KC):
                    mm(P, V(pv), V(xb, xb.ap[:, kc, sub * 128:(sub + 1) * 128]), V(wv[cb], wv[cb].ap[:, kc, :]), kc == 0, kc == KC - 1)
                cp(P, V(vs, vs.ap[:, cb * 512:(cb + 1) * 512]), V(pv), eng="act")
            P.dma("sp", v_o[tb * 4 + sub], vs.ap[:], vs, False)
    return P.finish()


def host_A(inp, li=0):
    x = inp["x"][0]
    xs = lay_resid(x)
    w = inp["da_w_in"][0]
    wq, wk, wv = w[:, :1024], w[:, 1024:2048], w[:, 2048:]
    common = {
        "w_in": lay_w_in(inp["ffn1_w_in"][li]), "w_out": lay_w_out(inp["ffn1_w_out"][li]),
        "gain": lay_gain(np.stack([inp["ffn1_norm"][li], inp["mix_norm"][li]])),
        "wqk": lay_w_cols(np.concatenate([wq, wk], axis=1), 512),
        "wv": lay_w_cols(wv, 512),
        "qkg": np.ascontiguousarray(np.stack([np.tile(inp["da_q_gain"][0], 2), np.tile(inp["da_k_gain"][0], 2)], axis=1)),
        "bd": np.kron(np.eye(2, dtype=np.float32), np.ones((64, 64), np.float32)),
    }
    return [dict(common, x=xs[c]) for c in range(NCORES)]


NQB = SEQ // 512


def build_B(lam_init, nqb=NQB):
    P = Prog()
    c = setup_common(P)
    common_bufs(P, c)
    qa_d = P.dram("qa", [2, 68, SEQ], BF16, "ExternalInput")
    ka_d = P.dram("ka", [2, 68, SEQ], BF16, "ExternalInput")
    v_d = P.dram("v", [128, SEQ // 128, 128], BF16, "ExternalInput")
    lam_d = P.dram("lam", [128, 4, 64], F32, "ExternalInput")
    sg_d = P.dram("subg", [128, 1], F32, "ExternalInput")
    tri_d = P.dram("tri", [128, 128], F32, "ExternalInput")
    o_o = P.dram("o_out", [128, SEQ], BF16, "ExternalOutput")
    sp_t = [TT(P._enter(P.nc.psum_tensor("psS%d" % i, [128, 2, 512], F32)), "psS%d" % i) for i in range(2)]
    acc = [TT(P._enter(P.nc.psum_tensor("psO%d" % i, [128, 512], F32)), "psO%d" % i) for i in range(2)]
    accl = [TT(P._enter(P.nc.psum_tensor("psL%d" % i, [128, 512], F32)), "psL%d" % i) for i in range(2)]
    ones_b = newt(P, "ones_b", [128, 128], BF16)
    cp(P, V(ones_b), V(c.ones_f))
    lam = load_small(P, c, "lam", lam_d[:, :, :], [128, 4, 64])
    subg = load_small(P, c, "subg", sg_d[:, :], [128, 1])
    trif = load_small(P, c, "trif", tri_d[:, :], [128, 128])
    tri2 = newt(P, "tri2", [128, 2, 128], BF16)
    cp(P, V(tri2, tri2.ap[:, 0, :]), V(trif))
    cp(P, V(tri2, tri2.ap[:, 1, :]), V(trif))
    pr = newt(P, "lpr", [128, 64], F32)
    s1 = newt(P, "ls1", [128, 1], F32)
    s2 = newt(P, "ls2", [128, 1], F32)
    nlam = newt(P, "nlam", [128, 1], F32)
    tt(P, V(pr), V(lam, lam.ap[:, 0, :]), V(lam, lam.ap[:, 1, :]), ALU.mult)
    P.op("dve", lambda e: e.reduce_sum(out=s1.ap[:], in_=pr.ap[:], axis=mybir.AxisListType.X), reads=[pr], writes=[s1])
    tt(P, V(pr), V(lam, lam.ap[:, 2, :]), V(lam, lam.ap[:, 3, :]), ALU.mult)
    P.op("dve", lambda e: e.reduce_sum(out=s2.ap[:], in_=pr.ap[:], axis=mybir.AxisListType.X), reads=[pr], writes=[s2])
    act(P, V(s1), V(s1), AF.Exp)
    act(P, V(s2), V(s2), AF.Exp)
    tt(P, V(nlam), V(s1), V(s2), ALU.subtract)
    ts(P, V(nlam), V(nlam), lam_init, -1.0, ALU.add, ALU.mult)
    one_m = 1.0 - lam_init
    epsb = const_tile(P, c, "epsb", EPS / (one_m * one_m))
    ka = [newt(P, "ka%d" % m, [68, SEQ], BF16) for m in range(2)]
    for m in range(2):
        for part in range(4):
            sl = slice(part * (SEQ // 4), (part + 1) * (SEQ // 4))
            P.dma("sp", ka[m].ap[:, sl], ka_d[m][:, sl], ka[m], True)
    vt = newt(P, "vt", [128, SEQ // 128, 128], BF16)
    for part in range(4):
        sl = slice(part * 32, (part + 1) * 32)
        P.dma("sp", vt.ap[:, sl, :], v_d[:, sl, :], vt, True)
    qb = [[newt(P, "qb%d_%d" % (i, m), [68, 512], BF16) for m in range(2)] for i in range(2)]
    pts = [newt(P, "pt%d" % i, [128, 2, 512], BF16) for i in range(4)]
    accL = {"dve": newt(P, "accLd", [128, 2, 512], F32)}
    rl = [newt(P, "rl%d" % i, [128, 512], F32) for i in range(2)]
    t0 = newt(P, "t0", [128, 512], F32)
    t1 = newt(P, "t1", [128, 512], F32)
    osum = newt(P, "osum", [128, 512], F32)
    sqB = newt(P, "sqB", [128, 512], F32)
    rstd = newt(P, "rstdB", [128, 512], F32)
    ost = [newt(P, "ost%d" % i, [128, 512], BF16) for i in range(2)]
    pti = 0
    spi = 0
    for Qi in range(nqb):
        qq = qb[Qi % 2]
        for m in range(2):
            P.dma("sp", qq[m].ap[:], qa_d[m][:, Qi * 512:(Qi + 1) * 512], qq[m], True)
        nkb = 4 * Qi + 4
        pend = []
        first = {"dve": True, "pe": True}

        def do_pv(item, nkb=nkb):
            kb, c0, pt, on_pe = item
            for m in range(2):
                mm(P, V(acc[m], acc[m].ap[:, c0:512]), V(vt, vt.ap[:, kb, :]), V(pt, pt.ap[:, m, c0:512]), kb == 0, kb == nkb - 1)
            if on_pe:
                for m in range(2):
                    mm(P, V(accl[m]), V(ones_b), V(pt, pt.ap[:, m, :]), first["pe"], False)
                first["pe"] = False
        for kb in range(nkb):
            j = kb - 4 * Qi
            c0 = 128 * j if j > 0 else 0
            S = sp_t[spi % 2]
            spi += 1
            for m in range(2):
                mm(P, V(S, S.ap[:, m, c0:512]), V(ka[m], ka[m].ap[:, kb * 128:(kb + 1) * 128]), V(qq[m], qq[m].ap[:, c0:512]), True, True)
            pt = pts[pti % 4]
            pti += 1
            act(P, V(pt, pt.ap[:, :, c0:512]), V(S, S.ap[:, :, c0:512]), AF.Exp)
            if j >= 0:
                tt(P, V(pt, pt.ap[:, :, 128 * j:128 * j + 128]), V(pt, pt.ap[:, :, 128 * j:128 * j + 128]), V(tri2), ALU.mult)
            on_pe = not (kb % 2 == 0 or j >= 0)
            if not on_pe:
                aL = accL["dve"]
                if first["dve"]:
                    first["dve"] = False
                    cp(P, V(aL), V(pt))
                else:
                    tt(P, V(aL, aL.ap[:, :, c0:512]), V(aL, aL.ap[:, :, c0:512]), V(pt, pt.ap[:, :, c0:512]), ALU.add)
            pend.append((kb, c0, pt, on_pe))
            if len(pend) > 1:
                do_pv(pend.pop(0))
        while pend:
            do_pv(pend.pop(0))
        for m in range(2):
            mm(P, V(accl[m]), V(c.ones_f), V(accL["dve"], accL["dve"].ap[:, m, :]), first["pe"], True)
        recip(P, V(rl[0]), V(accl[0]))
        tt(P, V(t0), V(acc[0]), V(rl[0]), ALU.mult)
        recip(P, V(rl[1]), V(accl[1]))
        tt(P, V(t1), V(acc[1]), V(rl[1]), ALU.mult)
        stt(P, V(osum), V(t1), V(nlam, nlam.ap[:, 0:1]), V(t0), ALU.mult, ALU.add)
        Sq = sp_t[spi % 2]
        spi += 1
        act(P, V(sqB), V(osum), AF.Square)
        mm(P, V(Sq, Sq.ap[:, 0, :]), V(c.ones_f), V(sqB), True, True)
        act(P, V(rstd), V(Sq, Sq.ap[:, 0, :]), AF.Sqrt, bias=V(epsb, epsb.ap[:, 0:1]), scale=1.0 / (128 * one_m * one_m))
        recip(P, V(rstd), V(rstd))
        os_ = ost[Qi % 2]
        stt(P, V(os_), V(osum), V(subg, subg.ap[:, 0:1]), V(rstd), ALU.mult, ALU.mult)
        P.dma("sp", o_o[:, Qi * 512:(Qi + 1) * 512], os_.ap[:], os_, False)
    return P.finish()


def host_B(inp, A_outs):
    import ml_dtypes
    bf = ml_dtypes.bfloat16
    qk = np.concatenate([o["qk_out"] for o in A_outs], axis=2)
    v = np.concatenate([o["v_out"].reshape(TPC, 1024) for o in A_outs], axis=0)
    pos = np.arange(SEQ)
    pb = (pos // 128).astype(np.float32)
    pr = (pos % 128).astype(np.float32)
    ones = np.ones(SEQ, np.float32)
    tri = (np.arange(128)[:, None] <= np.arange(128)[None, :]).astype(np.float32)
    lam = np.ascontiguousarray(np.broadcast_to(inp["da_lambda"][0][None], (128, 4, 64))).astype(np.float32)
    subg = np.ascontiguousarray(inp["da_subln"][0].reshape(128, 1))
    maps = []
    for h in range(NCORES):
        slope = 2.0 ** (-8.0 * (h + 1) / 8)
        qrows = np.stack([ones, ones, -slope * 128 * pb, -slope * pr]).astype(bf)
        krows = np.stack([slope * 128 * pb, slope * pr, ones, ones]).astype(bf)
        qh = qk[h].reshape(2, 64, SEQ)
        kh = qk[8 + h].reshape(2, 64, SEQ)
        qa = np.concatenate([qh, np.broadcast_to(qrows[None], (2, 4, SEQ))], axis=1)
        ka = np.concatenate([kh, np.broadcast_to(krows[None], (2, 4, SEQ))], axis=1)
        vh = v[:, h * 128:(h + 1) * 128].reshape(SEQ // 128, 128, 128).transpose(1, 0, 2)
        maps.append({"qa": np.ascontiguousarray(qa), "ka": np.ascontiguousarray(ka), "v": np.ascontiguousarray(vh),
                     "lam": lam, "subg": subg, "tri": tri})
    return maps


RT_G = [1.0 - 2.0 ** (-5.0 - h) for h in range(4)]
RT_S = 256 ** -0.5


def reload_resid(P, c, x_dram):
    P.dma("sp", c.xT_h[:], x_dram[:, :, :], c.xT_all, True)
    for t in c.xT:
        t.lw = c.xT_all.lw
        t.rd = {}


def gelu_tanh(P, c, out, z, g):
    n = out[1].shape[-1] if False else None
    if "sets" in g:
        g["i"] = g.get("i", 0) + 1
        gg_ = g["sets"][g["i"] % len(g["sets"])]
        t1, t2 = gg_["t1"], gg_["t2"]
    else:
        t1, t2 = g["t1"], g["t2"]
    w = z[1].shape[1]
    act(P, V(t1, t1.ap[:, :w]), z, AF.Square)
    ts(P, V(t1, t1.ap[:, :w]), V(t1, t1.ap[:, :w]), 0.044715, 1.0, ALU.mult, ALU.add)
    tt(P, V(t2, t2.ap[:, :w]), V(t1, t1.ap[:, :w]), z, ALU.mult)
    act(P, V(t2, t2.ap[:, :w]), V(t2, t2.ap[:, :w]), AF.Sigmoid, scale=1.5957691216057308)
    tt(P, out, V(t2, t2.ap[:, :w]), z, ALU.mult)


def build_C():
    P = Prog()
    c = setup_common(P)
    common_bufs(P, c)
    h_d = P.dram("h_in", [128, KC, TPC], F32, "ExternalInput")
    on_d = P.dram("on", [128, 8, TPC], BF16, "ExternalInput")
    wo_d = P.dram("wo", [KC, 128, 8, 128], F32, "ExternalInput")
    f2i = P.dram("f2_in", [NJ, 128, KC, 256], F32, "ExternalInput")
    f2o = P.dram("f2_out", [KC, 128, NJ, 128], F32, "ExternalInput")
    f1i = P.dram("f1_in", [NJ, 128, KC, 256], F32, "ExternalInput")
    f1o = P.dram("f1_out", [KC, 128, NJ, 128], F32, "ExternalInput")
    g_d = P.dram("gain", [128, 3, KC], F32, "ExternalInput")
    wfm_d = P.dram("wfm", [8, 128, KC, 512], F32, "ExternalInput")
    wtm_d = P.dram("wtm", [6, 128, KC, 512], F32, "ExternalInput")
    zc_d = P.dram("zc", [128, 4], F32, "ExternalInput")
    c2_d = P.dram("c2", [128, 16, 4], F32, "ExternalInput")
    h_o = P.dram("h_out", [128, KC, TPC], F32, "ExternalOutput")
    qk_o = P.dram("qkT_out", [16, 128, TPC], BF16, "ExternalOutput")
    gs_o = P.dram("gs_out", [16, 128, TPC], BF16, "ExternalOutput")
    kz_o = P.dram("kz_out", [16, 128, 1024], BF16, "ExternalOutput")
    v_o = P.dram("v_out", [16, 128, 2048], BF16, "ExternalOutput")
    R_o = P.dram("R_out", [128, 8, 512], F32, "ExternalOutput")
    gain_t = load_small(P, c, "gain", g_d[:, :, :], [128, 3, KC])
    zc = load_small(P, c, "zc", zc_d[:, :], [128, 4])
    c2 = load_small(P, c, "c2", c2_d[:, :, :], [128, 16, 4])
    load_resid(P, c, h_d)
    m0 = P.mark()
    on = newt(P, "on", [128, 8, TPC], BF16)
    P.dma("sp", on.ap[:], on_d[:, :, :], on, True)
    linear_resid(P, c, wo_d, 8, lambda tb, k: V(on, on.ap[:, k, tb * 512:(tb + 1) * 512]), 1.0, "woC")
    P.release(m0)
    ffn_scoped(P, c, f2i, f2o, gain_t, 0)
    ffn_scoped(P, c, f1i, f1o, gain_t, 1)
    store_resid(P, c, h_o)
    xn = [newt(P, "xnC%d" % i, [128, KC, 512], BF16) for i in range(2)]
    rstd = [newt(P, "rsC%d" % i, [128, 512], F32) for i in range(2)]
    wb = [newt(P, "wC%d" % i, [128, KC, 512], BF16) for i in range(3)]
    st = [newt(P, "stC%d" % i, [128, 512], BF16) for i in range(3)]
    kzst = [newt(P, "kzst%d" % i, [128, 1024], BF16) for i in range(4)]
    k2 = newt(P, "k2", [128, 4, 1024], BF16)
    vb = newt(P, "vb", [128, 4, 2048], BF16)
    Rl = newt(P, "Rl", [128, 8, 512], F32)
    wi = 0
    si = 0
    for tb in range(TPC // 512):
        x = c.xT[tb]
        xb = xn[tb % 2]
        norm_block(P, c, x, gain_t, 2, xb, rstd[tb % 2])
        for blk in range(8):
            w = wb[wi % 3]
            wi += 1
            P.dma("pool", w.ap[:], wfm_d[blk], w, True)
            for cc in range(4):
                ch = blk * 4 + cc
                pq = P.ps()
                for kc in range(KC):
                    mm(P, V(pq), V(w, w.ap[:, kc, cc * 128:(cc + 1) * 128]), V(xb, xb.ap[:, kc, :]), kc == 0, kc == KC - 1)
                s_ = st[si % 3]
                si += 1
                if ch < 16:
                    cp(P, V(s_), V(pq), eng="act")
                    P.dma("sp", qk_o[ch][:, tb * 512:(tb + 1) * 512], s_.ap[:], s_, False)
                else:
                    act(P, V(s_), V(pq), AF.Silu)
                    P.dma("sp", gs_o[ch - 16][:, tb * 512:(tb + 1) * 512], s_.ap[:], s_, False)
        for cb in range(6):
            w = wb[wi % 3]
            wi += 1
            P.dma("pool", w.ap[:], wtm_d[cb], w, True)
            for sub in range(4):
                n = tb * 4 + sub
                pv = P.ps()
                for kc in range(KC):
                    mm(P, V(pv), V(xb, xb.ap[:, kc, sub * 128:(sub + 1) * 128]), V(w, w.ap[:, kc, :]), kc == 0, kc == KC - 1)
                if cb < 2:
                    kzs = kzst[sub]
                    for hh in range(2):
                        h = cb * 2 + hh
                        ts(P, V(kzs, kzs.ap[:, cb * 512 + hh * 256:cb * 512 + (hh + 1) * 256]), V(pv, pv.ap[:, hh * 256:(hh + 1) * 256]),
                           V(zc, zc.ap[:, h:h + 1]), None, ALU.mult)
                        ts(P, V(k2, k2.ap[:, sub, cb * 512 + hh * 256:cb * 512 + (hh + 1) * 256]), V(pv, pv.ap[:, hh * 256:(hh + 1) * 256]),
                           V(c2, c2.ap[:, n, h:h + 1]), None, ALU.mult)
                    if cb == 1:
                        P.dma("sp", kz_o[n], kzs.ap[:], kzs, False)
                else:
                    cp(P, V(vb, vb.ap[:, sub, (cb - 2) * 512:(cb - 1) * 512]), V(pv), eng="act")
        for sub in range(4):
            P.dma("sp", v_o[tb * 4 + sub], vb.ap[:, sub, :], vb, False)
        for h in range(4):
            for dc in range(2):
                pr_ = P.ps()
                for sub in range(4):
                    mm(P, V(pr_), V(k2, k2.ap[:, sub, h * 256 + dc * 128:h * 256 + (dc + 1) * 128]), V(vb, vb.ap[:, sub, h * 512:(h + 1) * 512]), sub == 0, sub == 3)
                if tb == 0:
                    cp(P, V(Rl, Rl.ap[:, h * 2 + dc, :]), V(pr_))
                else:
                    tt(P, V(Rl, Rl.ap[:, h * 2 + dc, :]), V(Rl, Rl.ap[:, h * 2 + dc, :]), V(pr_), ALU.add)
    P.dma("sp", R_o[:, :, :], Rl.ap[:], Rl, False)
    return P.finish()


def host_C(inp, A_outs, B_outs):
    li = 0
    w = inp["rt_w_in"][0]
    wq, wk, wv, wg = w[:, :1024], w[:, 1024:2048], w[:, 2048:4096], w[:, 4096:]
    p = np.arange(128)
    zc = np.stack([RT_S * np.float64(RT_G[h]) ** (127 - p) for h in range(4)], axis=1).astype(np.float32)
    t = (np.arange(16)[None, :] * 128 + p[:, None])
    c2 = np.stack([RT_S * np.float64(RT_G[h]) ** (2047 - t) for h in range(4)], axis=2).astype(np.float32)
    common = {
        "wo": lay_w_rows(inp["da_w_out"][0]),
        "f2_in": lay_w_in(inp["ffn2_w_in"][0]), "f2_out": lay_w_out(inp["ffn2_w_out"][0]),
        "f1_in": lay_w_in(inp["ffn1_w_in"][1]), "f1_out": lay_w_out(inp["ffn1_w_out"][1]),
        "gain": lay_gain(np.stack([inp["ffn2_norm"][0], inp["ffn1_norm"][1], inp["mix_norm"][1]])),
        "wfm": lay_w_cols(np.concatenate([wq, wk, wg], axis=1), 512),
        "wtm": lay_w_cols(np.concatenate([wk, wv], axis=1), 512),
        "zc": zc, "c2": np.ascontiguousarray(c2),
    }
    maps = []
    for cix in range(NCORES):
        on = np.stack([B_outs[h]["o_out"][:, cix * TPC:(cix + 1) * TPC] for h in range(8)], axis=1)
        maps.append(dict(common, h_in=A_outs[cix]["h_out"], on=np.ascontiguousarray(on)))
    return maps


def build_D():
    P = Prog()
    c = setup_common(P)
    common_bufs(P, c)
    I = lambda n, s, d=F32: P.dram(n, s, d, "ExternalInput")
    O = lambda n, s, d=F32: P.dram(n, s, d, "ExternalOutput")
    h_d = I("h_in", [128, KC, TPC])
    qk_d = I("qkT", [16, 128, TPC], BF16)
    gs_d = I("gs", [16, 128, TPC], BF16)
    kz_d = I("kz", [16, 128, 1024], BF16)
    v_d = I("v", [16, 128, 2048], BF16)
    Rall_d = I("Rall", [8, 128, 8, 512])
    coef_d = I("coef", [128, 8, 4])
    decT_d = I("decT", [128, 4, 128])
    xi2_d = I("xi2", [128, 4, 2, 128])
    gch_d = I("gch", [128, 4])
    rsub_d = I("rsub", [128, 16])
    wo_rt_d = I("wo_rt", [KC, 128, 16, 128])
    fA_i, fA_o = I("fA_in", [NJ, 128, KC, 256]), I("fA_out", [KC, 128, NJ, 128])
    fB_i, fB_o = I("fB_in", [NJ, 128, KC, 256]), I("fB_out", [KC, 128, NJ, 128])
    fC_i, fC_o = I("fC_in", [NJ, 128, KC, 256]), I("fC_out", [KC, 128, NJ, 128])
    fD_i, fD_o = I("fD_in", [NJ, 128, KC, 256]), I("fD_out", [KC, 128, NJ, 128])
    g_d = I("gain", [128, 6, KC])
    wu_d = I("wu", [12, 128, KC, 256])
    wv_d = I("wv", [6, 128, KC, 512])
    vgain_d = I("vgain", [128, 3072])
    wsT_d = I("wsT", [128, 8, 128])
    bsb_d = I("bsb", [128, 8, 128])
    tri_d = I("tri", [128, 128])
    wo_sg_d = I("wo_sg", [KC, 128, 24, 128])
    wl_d = I("wl", [12, 128, KC, 256])
    y_o = O("y_rt", [16, 128, TPC], BF16)
    hscr_o = O("hscr", [128, KC, TPC])
    prod_o = O("prod", [24, 128, TPC], BF16)
    h_o = O("h_out", [128, KC, TPC])
    gg_o = O("gg_out", [12, 128, TPC], BF16)
    xb_o = O("xb_out", [12, 128, TPC])
    x2_o = O("x2_dbg", [128, KC, TPC])
    gain_t = load_small(P, c, "gain", g_d[:, :, :], [128, 6, KC])
    m0 = P.mark()
    coef = load_small(P, c, "coef", coef_d[:, :, :], [128, 8, 4])
    decT = load_small(P, c, "decT", decT_d[:, :, :], [128, 4, 128])
    xi2 = load_small(P, c, "xi2", xi2_d[:, :, :, :], [128, 4, 2, 128])
    gch = load_small(P, c, "gch", gch_d[:, :], [128, 4])
    rsub = load_small(P, c, "rsub", rsub_d[:, :], [128, 16])
    R = [newt(P, "R%d" % h, [128, 2, 512], F32) for h in range(4)]
    Rb = [newt(P, "Rb%d" % h, [128, 2, 512], BF16) for h in range(4)]
    rtmp = [newt(P, "rtmp%d" % i, [128, 2, 512], F32) for i in range(2)]
    for h in range(4):
        for j in range(8):
            t = rtmp[(h * 8 + j) % 2]
            P.dma("sp", t.ap[:], Rall_d[j][:, 2 * h:2 * h + 2, :], t, True)
            if j == 0:
                ts(P, V(R[h]), V(t), V(coef, coef.ap[:, j, h:h + 1]), None, ALU.mult)
            else:
                stt(P, V(R[h]), V(t), V(coef, coef.ap[:, j, h:h + 1]), V(R[h]), ALU.mult, ALU.add)
        cp(P, V(Rb[h]), V(R[h]), eng="act")
    qkb = newt(P, "qkb", [128, 16, 512], BF16)
    gsb = newt(P, "gsb", [128, 16, 512], BF16)
    kzb = newt(P, "kzb", [128, 4, 1024], BF16)
    vbk = newt(P, "vbk", [128, 4, 2048], BF16)
    ob = newt(P, "ob", [128, 16, 512], F32)
    atts = [newt(P, "att%d" % i, [128, 128], BF16) for i in range(2)]
    qxs = [newt(P, "qx%d" % i, [128, 2, 128], BF16) for i in range(2)]
    rstd = newt(P, "rsD", [128, 512], F32)
    yt = [newt(P, "yt%d" % i, [128, 512], F32) for i in range(2)]
    yst = [newt(P, "yst%d" % i, [128, 512], BF16) for i in range(3)]
    it = 0
    yi = 0
    for tb in range(TPC // 512):
        tsl = slice(tb * 512, (tb + 1) * 512)
        P.dma("sp", qkb.ap[:], qk_d[:, :, tsl].rearrange("c p t -> p c t"), qkb, True)
        P.dma("sp", gsb.ap[:], gs_d[:, :, tsl].rearrange("c p t -> p c t"), gsb, True)
        P.dma("sp", kzb.ap[:], kz_d[tb * 4:(tb + 1) * 4].rearrange("n p d -> p n d"), kzb, True)
        P.dma("sp", vbk.ap[:], v_d[tb * 4:(tb + 1) * 4].rearrange("n p d -> p n d"), vbk, True)
        for n in range(4):
            cols = slice(n * 128, (n + 1) * 128)
            for h in range(4):
                pa = P.ps()
                for dc in range(2):
                    mm(P, V(pa, pa.ap[:, 0:128]), V(qkb, qkb.ap[:, 8 + h * 2 + dc, cols]), V(qkb, qkb.ap[:, h * 2 + dc, cols]), dc == 0, dc == 1)
                attm = atts[it % 2]
                qx = qxs[it % 2]
                it += 1
                tt(P, V(attm), V(pa, pa.ap[:, 0:128]), V(decT, decT.ap[:, h, :]), ALU.mult)
                tt(P, V(qx), V(qkb, qkb.ap[:, h * 2:h * 2 + 2, cols]), V(xi2, xi2.ap[:, h, :, :]), ALU.mult)
                po = P.ps()
                for vc in range(4):
                    oc = V(po, po.ap[:, vc * 128:(vc + 1) * 128])
                    mm(P, oc, V(vbk, vbk.ap[:, n, h * 512 + vc * 128:h * 512 + (vc + 1) * 128]), V(attm), True, False)
                    mm(P, oc, V(Rb[h], Rb[h].ap[:, 0, vc * 128:(vc + 1) * 128]), V(qx, qx.ap[:, 0, :]), False, False)
                    mm(P, oc, V(Rb[h], Rb[h].ap[:, 1, vc * 128:(vc + 1) * 128]), V(qx, qx.ap[:, 1, :]), False, True)
                for vc in range(4):
                    cp(P, V(ob, ob.ap[:, h * 4 + vc, cols]), V(po, po.ap[:, vc * 128:(vc + 1) * 128]), eng="act")
                for dc in range(2):
                    pr_ = P.ps()
                    mm(P, V(pr_), V(kzb, kzb.ap[:, n, h * 256 + dc * 128:h * 256 + (dc + 1) * 128]), V(vbk, vbk.ap[:, n, h * 512:(h + 1) * 512]), True, True)
                    stt(P, V(R[h], R[h].ap[:, dc, :]), V(R[h], R[h].ap[:, dc, :]), V(gch, gch.ap[:, h:h + 1]), V(pr_), ALU.mult, ALU.add)
                cp(P, V(Rb[h]), V(R[h]), eng="act")
        for h in range(4):
            sumsq_rstd(P, c, [V(ob, ob.ap[:, h * 4 + vc, :]) for vc in range(4)], c.ones_f, 1.0 / 512, c.eps_t, rstd, 512)
            for vc in range(4):
                ch = h * 4 + vc
                y1 = yt[yi % 2]
                y2 = yst[yi % 3]
                yi += 1
                stt(P, V(y1), V(ob, ob.ap[:, ch, :]), V(rsub, rsub.ap[:, ch:ch + 1]), V(rstd), ALU.mult, ALU.mult)
                tt(P, V(y2), V(y1), V(gsb, gsb.ap[:, ch, :]), ALU.mult)
                P.dma("sp", y_o[ch][:, tsl], y2.ap[:], y2, False)
    P.release(m0)
    mB = P.mark()
    load_resid(P, c, h_d)
    m1 = P.mark()
    yall = newt(P, "yall", [128, 16, TPC], BF16)
    P.dma("sp", yall.ap[:], y_o[:, :, :].rearrange("c p t -> p c t"), yall, True)
    linear_resid(P, c, wo_rt_d, 16, lambda tb, k: V(yall, yall.ap[:, k, tb * 512:(tb + 1) * 512]), 1.0, "woR")
    P.release(m1)
    ffn_scoped(P, c, fA_i, fA_o, gain_t, 0)
    store_resid(P, c, x2_o)
    ffn_scoped(P, c, fB_i, fB_o, gain_t, 1)
    store_resid(P, c, hscr_o)
    P.release(mB)
    mC = P.mark()
    vgain = load_small(P, c, "vgain", vgain_d[:, :], [128, 3072])
    wsT = load_small(P, c, "wsT", wsT_d[:, :, :], [128, 8, 128])
    bsb = load_small(P, c, "bsb", bsb_d[:, :, :], [128, 8, 128])
    trif = load_small(P, c, "trif", tri_d[:, :], [128, 128])
    wsTb = newt(P, "wsTb", [128, 8, 128], BF16)
    bsb4 = newt(P, "bsb4", [128, 8, 512], F32)
    for g in range(8):
        tt(P, V(wsTb, wsTb.ap[:, g, :]), V(wsT, wsT.ap[:, g, :]), V(trif), ALU.mult)
        for s_ in range(4):
            cp(P, V(bsb4, bsb4.ap[:, g, s_ * 128:(s_ + 1) * 128]), V(bsb, bsb.ap[:, g, :]))
    xblk = newt(P, "xblk", [128, KC, 512], F32)
    xn = newt(P, "xnG", [128, KC, 512], BF16)
    rs2 = newt(P, "rsG", [128, 512], F32)
    gt = {"sets": [{"t1": newt(P, "gt1_%d" % i, [128, 512], F32), "t2": newt(P, "gt2_%d" % i, [128, 512], F32)} for i in range(3)]}
    wvb = [newt(P, "wvG%d" % i, [128, KC, 512], BF16) for i in range(2)]
    wub = [newt(P, "wuG%d" % i, [128, KC, 256], BF16) for i in range(2)]
    vg = newt(P, "vg", [128, 4, 3072], BF16)
    vtok = newt(P, "vtok", [128, 4, 3072], BF16)
    ss = newt(P, "ssG", [128, 24], F32)
    ssum = newt(P, "ssumG", [128, 4], F32)
    sqt = newt(P, "sqtG", [128, 512], F32)
    ut = [newt(P, "ut%d" % i, [128, 512], F32) for i in range(2)]
    t3 = [newt(P, "t3_%d" % i, [128, 512], F32) for i in range(2)]
    pst = [newt(P, "pst%d" % i, [128, 512], BF16) for i in range(3)]
    wi = 0
    for tb in range(TPC // 512):
        tsl = slice(tb * 512, (tb + 1) * 512)
        P.dma("sp", xblk.ap[:], hscr_o[:, :, tsl], xblk, True)
        norm_block(P, c, xblk, gain_t, 2, xn, rs2)
        for cb in range(6):
            w = wvb[wi % 2]
            wi += 1
            P.dma("pool", w.ap[:], wv_d[cb], w, True)
            for sub in range(4):
                pv = P.ps()
                for kc in range(KC):
                    mm(P, V(pv), V(xn, xn.ap[:, kc, sub * 128:(sub + 1) * 128]), V(w, w.ap[:, kc, :]), kc == 0, kc == KC - 1)
                vsl = V(vg, vg.ap[:, sub, cb * 512:(cb + 1) * 512])
                gelu_tanh(P, c, vsl, V(pv), gt)
                act(P, V(sqt), vsl, AF.Square)
                col = sub * 6 + cb
                P.op("dve", lambda e, col=col: e.reduce_sum(out=ss.ap[:, col:col + 1], in_=sqt.ap[:], axis=mybir.AxisListType.X), reads=[sqt], writes=[ss])
        for sub in range(4):
            P.op("dve", lambda e, sub=sub: e.reduce_sum(out=ssum.ap[:, sub:sub + 1], in_=ss.ap[:, sub * 6:(sub + 1) * 6], axis=mybir.AxisListType.X), reads=[ss], writes=[ssum])
        act(P, V(ssum), V(ssum), AF.Sqrt, bias=V(c.eps_t, c.eps_t.ap[:, 0:1]), scale=1.0 / 3072)
        recip(P, V(ssum), V(ssum))
        for sub in range(4):
            stt(P, V(vtok, vtok.ap[:, sub, :]), V(vg, vg.ap[:, sub, :]), V(ssum, ssum.ap[:, sub:sub + 1]), V(vgain), ALU.mult, ALU.mult)
        for cc in range(24):
            if cc % 2 == 0:
                wu = wub[(cc // 2) % 2]
                P.dma("pool", wu.ap[:], wu_d[cc // 2], wu, True)
            pu = P.ps()
            for kc in range(KC):
                mm(P, V(pu), V(wu, wu.ap[:, kc, (cc % 2) * 128:(cc % 2 + 1) * 128]), V(xn, xn.ap[:, kc, :]), kc == 0, kc == KC - 1)
            u = ut[cc % 2]
            gelu_tanh(P, c, V(u), V(pu), gt)
            pm = P.ps()
            for sub in range(4):
                mm(P, V(pm, pm.ap[:, sub * 128:(sub + 1) * 128]), V(vtok, vtok.ap[:, sub, cc * 128:(cc + 1) * 128]), V(wsTb, wsTb.ap[:, cc // 3, :]), True, True)
            t3_ = t3[cc % 2]
            tt(P, V(t3_), V(pm), V(bsb4, bsb4.ap[:, cc // 3, :]), ALU.add)
            ps_ = pst[cc % 3]
            tt(P, V(ps_), V(t3_), V(u), ALU.mult)
            P.dma("sp", prod_o[cc][:, tsl], ps_.ap[:], ps_, False)
    P.release(mC)
    load_resid(P, c, hscr_o)
    m2 = P.mark()
    pall = newt(P, "pall", [128, 24, TPC], BF16)
    P.dma("sp", pall.ap[:], prod_o[:, :, :].rearrange("c p t -> p c t"), pall, True)
    linear_resid(P, c, wo_sg_d, 24, lambda tb, k: V(pall, pall.ap[:, k, tb * 512:(tb + 1) * 512]), 1.0, "woG")
    P.release(m2)
    ffn_scoped(P, c, fC_i, fC_o, gain_t, 3)
    ffn_scoped(P, c, fD_i, fD_o, gain_t, 4)
    store_resid(P, c, h_o)
    xn2 = [newt(P, "xnL%d" % i, [128, KC, 512], BF16) for i in range(2)]
    rs3 = [newt(P, "rsL%d" % i, [128, 512], F32) for i in range(2)]
    gt2 = {"sets": [{"t1": newt(P, "gl1_%d" % i, [128, 512], F32), "t2": newt(P, "gl2_%d" % i, [128, 512], F32)} for i in range(3)]}
    wlb = [newt(P, "wlb%d" % i, [128, KC, 256], BF16) for i in range(2)]
    gst = [newt(P, "gst%d" % i, [128, 512], BF16) for i in range(2)]
    xst = [newt(P, "xst%d" % i, [128, 512], F32) for i in range(2)]
    for tb in range(TPC // 512):
        tsl = slice(tb * 512, (tb + 1) * 512)
        xb_ = xn2[tb % 2]
        norm_block(P, c, c.xT[tb], gain_t, 5, xb_, rs3[tb % 2])
        for blk in range(12):
            w = wlb[blk % 2]
            P.dma("pool", w.ap[:], wl_d[blk], w, True)
            for c2_ in range(2):
                ch = blk * 2 + c2_
                pl = P.ps()
                for kc in range(KC):
                    mm(P, V(pl), V(w, w.ap[:, kc, c2_ * 128:(c2_ + 1) * 128]), V(xb_, xb_.ap[:, kc, :]), kc == 0, kc == KC - 1)
                if ch < 12:
                    s_ = gst[ch % 2]
                    gelu_tanh(P, c, V(s_), V(pl), gt2)
                    P.dma("sp", gg_o[ch][:, tsl], s_.ap[:], s_, False)
                else:
                    s_ = xst[ch % 2]
                    cp(P, V(s_), V(pl), eng="act")
                    P.dma("sp", xb_o[ch - 12][:, tsl], s_.ap[:], s_, False)
    return P.finish()


def host_D(inp, C_outs):
    p = np.arange(128)
    G = [np.float64(g) for g in RT_G]
    decT = np.zeros((128, 4, 128), np.float32)
    xi2 = np.zeros((128, 4, 2, 128), np.float32)
    gch = np.zeros((128, 4), np.float32)
    for h in range(4):
        d = p[None, :] - p[:, None]
        decT[:, h, :] = np.where(d >= 0, RT_S * G[h] ** np.maximum(d, 0), 0.0)
        xi2[:, h, :, :] = (G[h] ** (p + 1.0))[None, None, :]
        gch[:, h] = G[h] ** 128
    tri = (p[:, None] <= p[None, :]).astype(np.float32)
    Rall = np.stack([o["R_out"] for o in C_outs])
    w = inp["sg_w_in"][0]
    wl = inp["lr_w_in"][0]
    common = {
        "Rall": Rall, "decT": decT, "xi2": xi2, "gch": gch,
        "rsub": np.ascontiguousarray(inp["rt_subln"][0].reshape(16, 128).T),
        "wo_rt": lay_w_rows(inp["rt_w_out"][0]),
        "fA_in": lay_w_in(inp["ffn2_w_in"][1]), "fA_out": lay_w_out(inp["ffn2_w_out"][1]),
        "fB_in": lay_w_in(inp["ffn1_w_in"][2]), "fB_out": lay_w_out(inp["ffn1_w_out"][2]),
        "fC_in": lay_w_in(inp["ffn2_w_in"][2]), "fC_out": lay_w_out(inp["ffn2_w_out"][2]),
        "fD_in": lay_w_in(inp["ffn1_w_in"][3]), "fD_out": lay_w_out(inp["ffn1_w_out"][3]),
        "gain": lay_gain(np.stack([inp["ffn2_norm"][1], inp["ffn1_norm"][2], inp["mix_norm"][2], inp["ffn2_norm"][2],
                                   inp["ffn1_norm"][3], inp["mix_norm"][3]])),
        "wu": lay_w_cols(w[:, :3072], 256), "wv": lay_w_cols(w[:, 3072:], 512),
        "vgain": np.ascontiguousarray(np.broadcast_to(inp["sg_v_gain"][0][None], (128, 3072))),
        "wsT": np.ascontiguousarray(inp["sg_w_s"][0].transpose(2, 0, 1)),
        "bsb": np.ascontiguousarray(np.broadcast_to(inp["sg_b_s"][0][None], (128, 8, 128))),
        "tri": tri,
        "wo_sg": lay_w_rows(inp["sg_w_out"][0]),
        "wl": lay_w_cols(wl, 256),
    }
    maps = []
    for cix in range(NCORES):
        coef = np.zeros((128, 8, 4), np.float32)
        for j in range(cix):
            for h in range(4):
                coef[:, j, h] = G[h] ** (2048.0 * (cix - 1 - j))
        o = C_outs[cix]
        maps.append(dict(common, h_in=o["h_out"], qkT=o["qkT_out"], gs=o["gs_out"], kz=o["kz_out"], v=o["v_out"], coef=coef))
    return maps


def build_E():
    P = Prog()
    c = setup_common(P)
    common_bufs(P, c)
    I = lambda n, s, d=F32: P.dram(n, s, d, "ExternalInput")
    O = lambda n, s, d=F32: P.dram(n, s, d, "ExternalOutput")
    xb_d = I("xb", [12, 128, TPC])
    halo_d = I("halo", [12, 128, 3])
    cw_d = I("cw", [128, 12, 4])
    cb_d = I("cb", [128, 12])
    wa_d = I("wa", [12, 128, 128])
    wx_d = I("wx", [12, 128, 128])
    ba_d = I("ba", [128, 12])
    bx_d = I("bx", [128, 12])
    lam_d = I("lam", [128, 12])
    hl_o = O("hloc", [12, 128, TPC])
    pc_o = O("pcum", [12, 128, TPC])
    ah_o = O("ah", [128, 12, 2])
    cw = load_small(P, c, "cw", cw_d[:, :, :], [128, 12, 4])
    cb = load_small(P, c, "cb", cb_d[:, :], [128, 12])
    ba = load_small(P, c, "ba", ba_d[:, :], [128, 12])
    bx = load_small(P, c, "bx", bx_d[:, :], [128, 12])
    lam = load_small(P, c, "lam", lam_d[:, :], [128, 12])
    sc = newt(P, "sc", [128, 12], F32)
    act(P, V(sc), V(lam), AF.Exp, scale=-1.0)
    ts(P, V(sc), V(sc), 1.0, None, ALU.add)
    act(P, V(sc), V(sc), AF.Ln)
    ts(P, V(sc), V(sc), -8.0, None, ALU.mult)
    zeros = const_tile(P, c, "zeros", 0.0, shape=(128, TPC))
    ah = newt(P, "ah", [128, 12, 2], F32)
    xpad = [newt(P, "xpad%d" % i, [128, TPC + 3], F32) for i in range(2)]
    xc = newt(P, "xc", [128, TPC], F32)
    xcb = newt(P, "xcb", [128, TPC], BF16)
    a_t = newt(P, "a_t", [128, TPC], F32)
    b_t = newt(P, "b_t", [128, TPC], F32)
    r_t = [newt(P, "r_t%d" % i, [128, 512], F32) for i in range(2)]
    i_t = [newt(P, "i_t%d" % i, [128, 512], F32) for i in range(2)]
    hl = [newt(P, "hl%d" % i, [128, TPC], F32) for i in range(2)]
    pc = [newt(P, "pc%d" % i, [128, TPC], F32) for i in range(2)]
    wab = [newt(P, "wab%d" % i, [128, 128], BF16) for i in range(2)]
    wxb = [newt(P, "wxb%d" % i, [128, 128], BF16) for i in range(2)]
    for n in range(12):
        xp = xpad[n % 2]
        P.dma("sp", xp.ap[:, 3:], xb_d[n], xp, True)
        P.dma("sp", xp.ap[:, 0:3], halo_d[n], xp, True)
        wa = wab[n % 2]
        wx = wxb[n % 2]
        P.dma("pool", wa.ap[:], wa_d[n], wa, True)
        P.dma("pool", wx.ap[:], wx_d[n], wx, True)
        ts(P, V(xc), V(xp, xp.ap[:, 0:TPC]), V(cw, cw.ap[:, n, 0:1]), V(cb, cb.ap[:, n:n + 1]), ALU.mult, ALU.add)
        for j in range(1, 4):
            stt(P, V(xc), V(xp, xp.ap[:, j:j + TPC]), V(cw, cw.ap[:, n, j:j + 1]), V(xc), ALU.mult, ALU.add)
        cp(P, V(xcb), V(xc), eng="act")
        for blk in range(4):
            sl = slice(blk * 512, (blk + 1) * 512)
            pr_ = P.ps()
            mm(P, V(pr_), V(wa), V(xcb, xcb.ap[:, sl]), True, True)
            pi_ = P.ps()
            mm(P, V(pi_), V(wx), V(xcb, xcb.ap[:, sl]), True, True)
            r = r_t[blk % 2]
            ii = i_t[blk % 2]
            act(P, V(r), V(pr_), AF.Sigmoid, bias=V(ba, ba.ap[:, n:n + 1]))
            act(P, V(ii), V(pi_), AF.Sigmoid, bias=V(bx, bx.ap[:, n:n + 1]))
            act(P, V(a_t, a_t.ap[:, sl]), V(r), AF.Exp, scale=V(sc, sc.ap[:, n:n + 1]))
            tt(P, V(r), V(a_t, a_t.ap[:, sl]), V(a_t, a_t.ap[:, sl]), ALU.mult)
            ts(P, V(r), V(r), -1.0, 1.0, ALU.mult, ALU.add)
            ts(P, V(r), V(r), 1e-12, None, ALU.max)
            act(P, V(r), V(r), AF.Sqrt)
            tt(P, V(ii), V(ii), V(xc, xc.ap[:, sl]), ALU.mult)
            tt(P, V(b_t, b_t.ap[:, sl]), V(r), V(ii), ALU.mult)
        h_ = hl[n % 2]
        p_ = pc[n % 2]
        P.op("dve", lambda e, h_=h_: e.tensor_tensor_scan(out=h_.ap[:], data0=a_t.ap[:], data1=b_t.ap[:], initial=0.0, op0=ALU.mult, op1=ALU.add),
             reads=[a_t, b_t], writes=[h_])
        P.op("dve", lambda e, p_=p_: e.tensor_tensor_scan(out=p_.ap[:], data0=a_t.ap[:], data1=zeros.ap[:], initial=1.0, op0=ALU.mult, op1=ALU.add),
             reads=[a_t, zeros], writes=[p_])
        cp(P, V(ah, ah.ap[:, n, 0:1]), V(p_, p_.ap[:, TPC - 1:TPC]))
        cp(P, V(ah, ah.ap[:, n, 1:2]), V(h_, h_.ap[:, TPC - 1:TPC]))
        P.dma("sp", hl_o[n], h_.ap[:], h_, False)
        P.dma("sp", pc_o[n], p_.ap[:], p_, False)
    P.dma("sp", ah_o[:, :, :], ah.ap[:], ah, False)
    return P.finish()


def lay_vec12(v):
    return np.ascontiguousarray(v.reshape(12, 128).T)


def host_E(inp, D_outs):
    common = {
        "cw": np.ascontiguousarray(inp["lr_conv_w"][0].reshape(4, 12, 128).transpose(2, 1, 0)),
        "cb": lay_vec12(inp["lr_conv_b"][0]),
        "wa": np.ascontiguousarray(inp["lr_w_a"][0]), "wx": np.ascontiguousarray(inp["lr_w_x"][0]),
        "ba": lay_vec12(inp["lr_b_a"][0]), "bx": lay_vec12(inp["lr_b_x"][0]), "lam": lay_vec12(inp["lr_lambda"][0]),
    }
    maps = []
    for cix in range(NCORES):
        if cix == 0:
            halo = np.zeros((12, 128, 3), np.float32)
        else:
            halo = np.ascontiguousarray(D_outs[cix - 1]["xb_out"][:, :, TPC - 3:])
        maps.append(dict(common, xb=D_outs[cix]["xb_out"], halo=halo))
    return maps


def build_F():
    P = Prog()
    c = setup_common(P)
    common_bufs(P, c)
    I = lambda n, s, d=F32: P.dram(n, s, d, "ExternalInput")
    O = lambda n, s, d=F32: P.dram(n, s, d, "ExternalOutput")
    h_d = I("h_in", [128, KC, TPC])
    hl_d = I("hloc", [12, 128, TPC])
    pc_d = I("pcum", [12, 128, TPC])
    gg_d = I("gg", [12, 128, TPC], BF16)
    ahall_d = I("ahall", [128, 8, 12, 2])
    msk_d = I("msk", [128, 8, 2])
    wo_d = I("wo", [KC, 128, 12, 128])
    f_i, f_o = I("f_in", [NJ, 128, KC, 256]), I("f_out", [KC, 128, NJ, 128])
    g_d = I("gain", [128, 1, KC])
    y_o = O("out", [128, KC, TPC])
    gain_t = load_small(P, c, "gain", g_d[:, :, :], [128, 1, KC])
    ahall = load_small(P, c, "ahall", ahall_d[:, :, :, :], [128, 8, 12, 2])
    msk = load_small(P, c, "msk", msk_d[:, :, :], [128, 8, 2])
    hs = const_tile(P, c, "hs", 0.0, shape=(128, 12))
    A_ = newt(P, "A_", [128, 12], F32)
    H_ = newt(P, "H_", [128, 12], F32)
    for j in range(8):
        ts(P, V(A_), V(ahall, ahall.ap[:, j, :, 0]), V(msk, msk.ap[:, j, 0:1]), V(msk, msk.ap[:, j, 1:2]), ALU.mult, ALU.add)
        ts(P, V(H_), V(ahall, ahall.ap[:, j, :, 1]), V(msk, msk.ap[:, j, 0:1]), None, ALU.mult)
        tt(P, V(hs), V(hs), V(A_), ALU.mult)
        tt(P, V(hs), V(hs), V(H_), ALU.add)
    load_resid(P, c, h_d)
    m0 = P.mark()
    yall = newt(P, "yallF", [128, 12, TPC], BF16)
    hlt = [newt(P, "hlt%d" % i, [128, TPC], F32) for i in range(2)]
    pct = [newt(P, "pct%d" % i, [128, TPC], F32) for i in range(2)]
    ggt = [newt(P, "ggt%d" % i, [128, TPC], BF16) for i in range(2)]
    for n in range(12):
        a, b, g = hlt[n % 2], pct[n % 2], ggt[n % 2]
        P.dma("sp", a.ap[:], hl_d[n], a, True)
        P.dma("sp", b.ap[:], pc_d[n], b, True)
        P.dma("sp", g.ap[:], gg_d[n], g, True)
        stt(P, V(a), V(b), V(hs, hs.ap[:, n:n + 1]), V(a), ALU.mult, ALU.add)
        tt(P, V(yall, yall.ap[:, n, :]), V(a), V(g), ALU.mult)
    linear_resid(P, c, wo_d, 12, lambda tb, k: V(yall, yall.ap[:, k, tb * 512:(tb + 1) * 512]), 1.0, "woF")
    P.release(m0)
    ffn_scoped(P, c, f_i, f_o, gain_t, 0)
    store_resid(P, c, y_o)
    return P.finish()


def host_F(inp, D_outs, E_outs):
    ahall = np.ascontiguousarray(np.stack([o["ah"] for o in E_outs], axis=1))
    common = {
        "ahall": ahall, "wo": lay_w_rows(inp["lr_w_out"][0]),
        "f_in": lay_w_in(inp["ffn2_w_in"][3]), "f_out": lay_w_out(inp["ffn2_w_out"][3]),
        "gain": lay_gain(inp["ffn2_norm"][3:4]),
    }
    maps = []
    for cix in range(NCORES):
        msk = np.zeros((128, 8, 2), np.float32)
        msk[:, :, 1] = 1.0
        msk[:, :cix, 0] = 1.0
        msk[:, :cix, 1] = 0.0
        maps.append(dict(common, h_in=D_outs[cix]["h_out"], hloc=E_outs[cix]["hloc"], pcum=E_outs[cix]["pcum"],
                         gg=D_outs[cix]["gg_out"], msk=msk))
    return maps


_CACHE = {}


def _prog(name, fn):
    if name not in _CACHE:
        _CACHE[name] = fn()
    return _CACHE[name]


def _run(nc, maps):
    res = run_bass_kernel_spmd(nc, maps, core_ids=list(range(NCORES)))
    return [dict(r) for r in res.results]


def kernel(**inp):
    inp = {k: np.asarray(v) for k, v in inp.items()}
    A = _run(build_A(), host_A(inp))
    B = _run(build_B(0.8 - 0.6 * 1.0), host_B(inp, A))
    C = _run(build_C(), host_C(inp, A, B))
    del B
    D_ = _run(build_D(), host_D(inp, C))
    del A, C
    E = _run(build_E(), host_E(inp, D_))
    F = _run(build_F(), host_F(inp, D_, E))
    out = unlay_resid([o["out"] for o in F])
    return out.reshape(1, SEQ, D).astype(np.float32)
```

```python
import numpy as np
import concourse.bass as bass
import concourse.mybir as mybir
from concourse.bass_utils import run_bass_kernel_spmd

F32 = mybir.dt.float32
BF16 = mybir.dt.bfloat16
AF = mybir.ActivationFunctionType
ALU = mybir.AluOpType

NCORES = 8
D = 1024
SEQ = 16384
TPC = SEQ // NCORES
DFF = 2816
NJ = DFF // 128
KC = D // 128
EPS = 1e-6
SAME_ENG_SYNC = True


class TT:
    __slots__ = ("ap", "lw", "rd", "dsem", "dcnt", "name")

    def __init__(self, ap, name=""):
        self.ap = ap
        self.lw = None
        self.rd = {}
        self.dsem = None
        self.dcnt = 0
        self.name = name


class Prog:
    ENG = ("pe", "act", "dve", "pool", "sp")

    def __init__(self):
        self.nc = bass.Bass("TRN2", target_bir_lowering=False)
        self.ops = {e: [] for e in self.ENG}
        self.cnt = {e: 0 for e in self.ENG}
        self.waited = {e: {} for e in self.ENG}
        self.sems = {}
        self.stack = []
        self.dma_tiles = []
        self.dma_final = {}
        self.all_eng_sems = []
        self.nsem = 0
        for e in self.ENG:
            self.sems[e] = self._sem("e_" + e)
        self.psum = []

    def _enter(self, cm):
        v = cm.__enter__()
        self.stack.append(cm)
        return v

    def _sem(self, name):
        self.nsem += 1
        cm = self.nc.semaphore(name + "_%d" % self.nsem)
        v = cm.__enter__()
        try:
            cm._is_sem = True
            self.stack.append(cm)
        except Exception:
            self.semstack = getattr(self, "semstack", [])
            self.semstack.append(cm)
        return v

    def sbuf(self, name, shape, dt):
        self.nsb = getattr(self, "nsb", 0) + 1
        return self._enter(self.nc.sbuf_tensor("sb%d_%s" % (self.nsb, name), list(shape), dt))

    def psum_tiles(self):
        if not self.psum:
            for i in range(8):
                h = self._enter(self.nc.psum_tensor("ps%d" % i, [128, 512], F32))
                self.psum.append(TT(h, "ps%d" % i))
            self.psi = 0
        return self.psum

    def ps(self):
        self.psum_tiles()
        pool = getattr(self, "ps_pool", list(range(8)))
        t = self.psum[pool[self.psi % len(pool)]]
        self.psi += 1
        return t

    def barrier(self):
        evs = [(self.sems[e], self.cnt[e], e) for e in self.ENG if self.cnt[e] > 0 and e != "sp"]
        devs = list(self.dma_final.values())
        for eng in self.ENG:
            wl = []
            wd = self.waited[eng]
            for sm, v, e_ in evs:
                if e_ != eng and wd.get(id(sm), 0) < v:
                    wd[id(sm)] = v
                    wl.append((sm, v))
            for sm, v in devs:
                if wd.get(id(sm), 0) < v:
                    wd[id(sm)] = v
                    wl.append((sm, v))

            def emit(e, wl=wl):
                for s_, v in wl:
                    e.wait_ge(s_, v)
            self.ops[eng].append(emit)

    def mark(self):
        return len(self.stack)

    def release(self, m):
        self.barrier()
        keep = []
        while len(self.stack) > m:
            cm = self.stack.pop()
            if getattr(cm, "_is_sem", False):
                keep.append(cm)
            else:
                cm.__exit__(None, None, None)
        self.stack.extend(reversed(keep))

    def dram(self, name, shape, dt, kind):
        return self.nc.dram_tensor(name, list(shape), dt, kind=kind).ap()

    EPOCH = 1500

    def _deps(self, eng, reads, writes):
        need = {}

        def add(ev):
            if ev is None:
                return
            sm, v, e = ev
            if e == eng and (eng == "pe" or not SAME_ENG_SYNC):
                return
            k = id(sm)
            if k not in need or need[k][1] < v:
                need[k] = (sm, v)
        for t in reads:
            add(t.lw)
        for t in writes:
            add(t.lw)
            for ev in t.rd.values():
                add(ev)
        out = []
        wd = self.waited[eng]
        for k, (sm, v) in need.items():
            if wd.get(k, 0) < v:
                wd[k] = v
                out.append((sm, v))
        return out

    def op(self, eng, fn, reads=(), writes=()):
        wl = self._deps(eng, reads, writes)
        if self.cnt[eng] >= self.EPOCH:
            self.sems[eng] = self._sem("e_" + eng)
            self.cnt[eng] = 0
        self.cnt[eng] += 1
        val = self.cnt[eng]
        sem = self.sems[eng]

        def emit(e):
            for s, v in wl:
                e.wait_ge(s, v)
            fn(e).then_inc(sem, 1)
        self.ops[eng].append(emit)
        for t in reads:
            t.rd[eng] = (sem, val, eng)
        for t in writes:
            t.lw = (sem, val, eng)
            t.rd = {}

    def dma(self, q, out_ap, in_ap, tile, is_load):
        if tile.dsem is None or tile.dcnt >= 16 * 100:
            tile.dsem = self._sem("d")
            tile.dcnt = 0
            self.dma_tiles.append((tile, tile.dsem))
        wl = self._deps(q, () if is_load else (tile,), (tile,) if is_load else ())
        tile.dcnt += 16
        val = tile.dcnt
        sem = tile.dsem
        self.dma_final[id(sem)] = (sem, val)

        def emit(e):
            for s, v in wl:
                e.wait_ge(s, v)
            e.dma_start(out=out_ap, in_=in_ap).then_inc(sem, 16)
        self.ops[q].append(emit)
        if is_load:
            tile.lw = (sem, val, None)
            tile.rd = {}
        else:
            tile.rd[id(sem)] = (sem, val, None)

    def finish(self):
        finals = list(self.dma_final.values())

        def emit(e):
            for s, v in finals:
                e.wait_ge(s, v)
        self.ops["sp"].append(emit)
        nc = self.nc
        ops = self.ops
        with nc.Block() as block:
            @block.tensor
            def _(e):
                for f in ops["pe"]:
                    f(e)

            @block.scalar
            def _(e):
                for f in ops["act"]:
                    f(e)

            @block.vector
            def _(e):
                for f in ops["dve"]:
                    f(e)

            @block.gpsimd
            def _(e):
                for f in ops["pool"]:
                    f(e)

            @block.sync
            def _(e):
                for f in ops["sp"]:
                    f(e)
        while self.stack:
            self.stack.pop().__exit__(None, None, None)
        return nc


class Ctx:
    pass


def setup_common(P):
    c = Ctx()
    c.ones_f = TT(P.sbuf("ones_f", [128, 128], F32), "ones_f")
    P.op("dve", lambda e: e.memset(c.ones_f.ap[:], 1.0), writes=[c.ones_f])
    return c


def load_resid(P, c, x_dram):
    c.xT_h = P.sbuf("xT", [128, KC, TPC], F32)
    c.xT_all = TT(c.xT_h, "xT")
    P.dma("sp", c.xT_h[:], x_dram[:, :, :], c.xT_all, True)
    c.xT = []
    for tb in range(TPC // 512):
        t = TT(c.xT_h[:, :, tb * 512:(tb + 1) * 512], "xT%d" % tb)
        t.lw = c.xT_all.lw
        c.xT.append(t)


def store_resid(P, c, y_dram):
    tall = c.xT_all
    for t in c.xT:
        pass
    for tb, t in enumerate(c.xT):
        P.dma("sp", y_dram[:, :, tb * 512:(tb + 1) * 512], t.ap, t, False)


def rms_norm(P, c, src_blk, gain_ap_fn, dst, n_feat_chunks, tb_cols, tmp_sq, rstd, src_ap_fn, inv_n):
    raise NotImplementedError


def ffn(P, c, pre, w_in_d, w_out_d, gain_t, gidx):
    if not hasattr(c, "ffn_bufs"):
        b = Ctx()
        b.xn = [TT(P.sbuf("xn%d" % i, [128, KC, 512], BF16), "xn%d" % i) for i in range(2)]
        b.gT = [TT(P.sbuf("gT%d" % i, [128, NJ, 512], BF16), "gT%d" % i) for i in range(2)]
        b.win = [TT(P.sbuf("win%d" % i, [128, KC, 256], BF16), "win%d" % i) for i in range(3)]
        b.wout = [TT(P.sbuf("wout%d" % i, [128, NJ, 128], BF16), "wout%d" % i) for i in range(2)]
        b.sq = [TT(P.sbuf("sq%d" % i, [128, 512], F32), "sq%d" % i) for i in range(2)]
        b.rstd = [TT(P.sbuf("rstd%d" % i, [128, 512], F32), "rstd%d" % i) for i in range(2)]
        b.sa = [TT(P.sbuf("sa%d" % i, [128, 512], F32), "sa%d" % i) for i in range(2)]
        b.i = 0
        c.ffn_bufs = b
    b = c.ffn_bufs
    nblk = TPC // 512
    for half in range(nblk // 2):
        tbs = [2 * half, 2 * half + 1]
        for ii, tb in enumerate(tbs):
            x = c.xT[tb]
            xn = b.xn[ii]
            rstd = b.rstd[ii]
            pss = P.ps()
            for kc in range(KC):
                sq = b.sq[kc % 2]
                P.op("act", lambda e, sq=sq, x=x, kc=kc: e.activation(out=sq.ap[:], in_=x.ap[:, kc, :], func=AF.Square),
                     reads=[x], writes=[sq])
                P.op("pe", lambda e, sq=sq, kc=kc, pss=pss: e.matmul(pss.ap[:], c.ones_f.ap[:], sq.ap[:], start=(kc == 0), stop=(kc == KC - 1)),
                     reads=[sq, c.ones_f], writes=[pss])
            P.op("act", lambda e, rstd=rstd, pss=pss: e.activation(out=rstd.ap[:], in_=pss.ap[:], func=AF.Sqrt, scale=1.0 / D, bias=c.eps_t.ap[:, 0:1]),
                 reads=[pss, c.eps_t], writes=[rstd])
            P.op("dve", lambda e, rstd=rstd: e.reciprocal(out=rstd.ap[:], in_=rstd.ap[:]), reads=[rstd], writes=[rstd])
            for kc in range(KC):
                P.op("dve", lambda e, xn=xn, x=x, kc=kc, rstd=rstd: e.scalar_tensor_tensor(
                    out=xn.ap[:, kc, :], in0=x.ap[:, kc, :], scalar=gain_t.ap[:, gidx, kc:kc + 1], in1=rstd.ap[:],
                    op0=ALU.mult, op1=ALU.mult), reads=[x, rstd, gain_t], writes=[xn])
        for j in range(NJ):
            w = b.win[b.i % 3]
            b.i += 1
            P.dma("pool", w.ap[:], w_in_d[j], w, True)
            for ii, tb in enumerate(tbs):
                xn = b.xn[ii]
                gT = b.gT[ii]
                pa = P.ps()
                pb = P.ps()
                for kc in range(KC):
                    P.op("pe", lambda e, pa=pa, w=w, xn=xn, kc=kc: e.matmul(pa.ap[:], w.ap[:, kc, 0:128], xn.ap[:, kc, :], start=(kc == 0), stop=(kc == KC - 1)),
                         reads=[w, xn], writes=[pa])
                for kc in range(KC):
                    P.op("pe", lambda e, pb=pb, w=w, xn=xn, kc=kc: e.matmul(pb.ap[:], w.ap[:, kc, 128:256], xn.ap[:, kc, :], start=(kc == 0), stop=(kc == KC - 1)),
                         reads=[w, xn], writes=[pb])
                sa = b.sa[(j * 2 + ii) % 2]
                P.op("act", lambda e, sa=sa, pa=pa: e.activation(out=sa.ap[:], in_=pa.ap[:], func=AF.Silu), reads=[pa], writes=[sa])
                P.op("dve", lambda e, gT=gT, sa=sa, pb=pb, j=j: e.tensor_tensor(out=gT.ap[:, j, :], in0=sa.ap[:], in1=pb.ap[:], op=ALU.mult),
                     reads=[sa, pb], writes=[gT])
        for n in range(KC):
            w = b.wout[n % 2]
            P.dma("pool", w.ap[:], w_out_d[n], w, True)
            for ii, tb in enumerate(tbs):
                x = c.xT[tb]
                gT = b.gT[ii]
                po = P.ps()
                for j in range(NJ):
                    P.op("pe", lambda e, po=po, w=w, gT=gT, j=j: e.matmul(po.ap[:], w.ap[:, j, :], gT.ap[:, j, :], start=(j == 0), stop=(j == NJ - 1)),
                         reads=[w, gT], writes=[po])
                P.op("dve", lambda e, x=x, po=po, n=n: e.scalar_tensor_tensor(
                    out=x.ap[:, n, :], in0=po.ap[:], scalar=0.5, in1=x.ap[:, n, :], op0=ALU.mult, op1=ALU.add),
                    reads=[po, x], writes=[x])


def load_small(P, c, name, dram_ap, shape):
    t = TT(P.sbuf(name, shape, F32), name)
    P.dma("sp", t.ap[:], dram_ap, t, True)
    return t


def lay_resid(x_tok):
    out = []
    for c in range(NCORES):
        xs = x_tok[c * TPC:(c + 1) * TPC]
        out.append(np.ascontiguousarray(xs.T.reshape(KC, 128, TPC).transpose(1, 0, 2)))
    return out


def unlay_resid(shards):
    outs = []
    for s in shards:
        outs.append(s.transpose(1, 0, 2).reshape(D, TPC).T)
    return np.ascontiguousarray(np.concatenate(outs, axis=0))


def lay_w_in(w):
    a = w[:, :DFF].reshape(KC, 128, NJ, 128)
    b = w[:, DFF:].reshape(KC, 128, NJ, 128)
    ab = np.concatenate([a, b], axis=3)
    return np.ascontiguousarray(ab.transpose(2, 1, 0, 3))


def lay_w_out(w):
    return np.ascontiguousarray(w.reshape(NJ, 128, KC, 128).transpose(2, 1, 0, 3))


def lay_gain(g):
    n = g.shape[0]
    return np.ascontiguousarray(g.reshape(n, KC, 128).transpose(2, 0, 1))


def V(t, ap=None):
    return (t, t.ap[:] if ap is None else ap)


def _tl(*ops):
    return [o[0] for o in ops if o is not None and isinstance(o, tuple)]


def act(P, out, in_, func, bias=None, scale=None, accum=None):
    kw = {}
    rd = [in_[0]]
    if bias is not None:
        if isinstance(bias, tuple):
            kw["bias"] = bias[1]
            rd.append(bias[0])
        else:
            kw["bias"] = bias
    if scale is not None:
        if isinstance(scale, tuple):
            kw["scale"] = scale[1]
            rd.append(scale[0])
        else:
            kw["scale"] = scale
    wr = [out[0]]
    if accum is not None:
        kw["accum_out"] = accum[1]
        wr.append(accum[0])
    P.op("act", lambda e: e.activation(out=out[1], in_=in_[1], func=func, **kw), reads=rd, writes=wr)


def mm(P, out, lhsT, rhs, start, stop):
    P.op("pe", lambda e: e.matmul(out[1], lhsT[1], rhs[1], start=start, stop=stop), reads=[lhsT[0], rhs[0]], writes=[out[0]])


def tt(P, out, a, b, op, eng="dve"):
    P.op(eng, lambda e: e.tensor_tensor(out=out[1], in0=a[1], in1=b[1], op=op), reads=[a[0], b[0]], writes=[out[0]])


def stt(P, out, a, scalar, b, op0, op1, eng="dve"):
    rd = [a[0], b[0]]
    sc = scalar
    if isinstance(scalar, tuple):
        rd.append(scalar[0])
        sc = scalar[1]
    P.op(eng, lambda e: e.scalar_tensor_tensor(out=out[1], in0=a[1], scalar=sc, in1=b[1], op0=op0, op1=op1), reads=rd, writes=[out[0]])


def ts(P, out, a, s1, s2, op0, op1=None, eng="dve", accum=None):
    rd = [a[0]]
    v1 = s1
    if isinstance(s1, tuple):
        rd.append(s1[0])
        v1 = s1[1]
    v2 = s2
    if isinstance(s2, tuple):
        rd.append(s2[0])
        v2 = s2[1]
    kw = {}
    if op1 is not None:
        kw["op1"] = op1
    wr = [out[0]]
    if accum is not None:
        kw["accum_out"] = accum[1]
        wr.append(accum[0])
    P.op(eng, lambda e: e.tensor_scalar(out=out[1], in0=a[1], scalar1=v1, scalar2=v2, op0=op0, **kw), reads=rd, writes=wr)


def cp(P, out, a, eng="dve"):
    if eng == "act":
        P.op("act", lambda e: e.activation(out=out[1], in_=a[1], func=AF.Copy), reads=[a[0]], writes=[out[0]])
    else:
        P.op(eng, lambda e: e.tensor_copy(out=out[1], in_=a[1]), reads=[a[0]], writes=[out[0]])


def recip(P, out, a):
    P.op("dve", lambda e: e.reciprocal(out=out[1], in_=a[1]), reads=[a[0]], writes=[out[0]])


def const_tile(P, c, name, val, shape=(128, 1), dt=F32):
    t = TT(P.sbuf(name, list(shape), dt), name)
    P.op("dve", lambda e: e.memset(t.ap[:], val), writes=[t])
    return t


def newt(P, name, shape, dt):
    return TT(P.sbuf(name, list(shape), dt), name)


def sumsq_rstd(P, c, srcs, lhsT, scale, bias_t, rstd, ncols):
    pss = P.ps()
    n = len(srcs)
    for i, s in enumerate(srcs):
        sq = c.sq[c.sqi % 2]
        c.sqi += 1
        act(P, V(sq, sq.ap[:, :ncols]), s, AF.Square)
        mm(P, V(pss, pss.ap[:, :ncols]), V(lhsT), V(sq, sq.ap[:, :ncols]), i == 0, i == n - 1)
    act(P, V(rstd, rstd.ap[:, :ncols]), V(pss, pss.ap[:, :ncols]), AF.Sqrt, bias=V(bias_t, bias_t.ap[:, 0:1]), scale=scale)
    recip(P, V(rstd, rstd.ap[:, :ncols]), V(rstd, rstd.ap[:, :ncols]))


def norm_block(P, c, x, gain_t, gidx, xn, rstd):
    sumsq_rstd(P, c, [V(x, x.ap[:, kc, :]) for kc in range(KC)], c.ones_f, 1.0 / D, c.eps_t, rstd, 512)
    for kc in range(KC):
        stt(P, V(xn, xn.ap[:, kc, :]), V(x, x.ap[:, kc, :]), V(gain_t, gain_t.ap[:, gidx, kc:kc + 1]), V(rstd), ALU.mult, ALU.mult)


def common_bufs(P, c):
    c.sq = [newt(P, "sqc%d" % i, [128, 512], F32) for i in range(2)]
    c.sqi = 0
    c.eps_t = const_tile(P, c, "eps", EPS)


def ffn_scoped(P, c, w_in_d, w_out_d, gain_t, gidx):
    m = P.mark()
    ffn(P, c, None, w_in_d, w_out_d, gain_t, gidx)
    del c.ffn_bufs
    P.release(m)


def linear_resid(P, c, w_d, nk, src_fn, scale, wname):
    wt = [newt(P, "%s%d" % (wname, i), [128, nk, 128], BF16) for i in range(2)]
    for n in range(KC):
        w = wt[n % 2]
        P.dma("pool", w.ap[:], w_d[n], w, True)
        for tb in range(TPC // 512):
            x = c.xT[tb]
            po = P.ps()
            for k in range(nk):
                mm(P, V(po), V(w, w.ap[:, k, :]), src_fn(tb, k), k == 0, k == nk - 1)
            stt(P, V(x, x.ap[:, n, :]), V(po), scale, V(x, x.ap[:, n, :]), ALU.mult, ALU.add)


def lay_w_cols(w, cb):
    K_, N_ = w.shape
    return np.ascontiguousarray(w.reshape(K_ // 128, 128, N_ // cb, cb).transpose(2, 1, 0, 3))


def lay_w_rows(w):
    K_ = w.shape[0]
    return np.ascontiguousarray(w.reshape(K_ // 128, 128, KC, 128).transpose(2, 1, 0, 3))


def build_A():
    P = Prog()
    c = setup_common(P)
    common_bufs(P, c)
    x_d = P.dram("x", [128, KC, TPC], F32, "ExternalInput")
    win_d = P.dram("w_in", [NJ, 128, KC, 256], F32, "ExternalInput")
    wout_d = P.dram("w_out", [KC, 128, NJ, 128], F32, "ExternalInput")
    g_d = P.dram("gain", [128, 2, KC], F32, "ExternalInput")
    wqk_d = P.dram("wqk", [4, 128, KC, 512], F32, "ExternalInput")
    wv_d = P.dram("wv", [2, 128, KC, 512], F32, "ExternalInput")
    qkg_d = P.dram("qkg", [128, 2], F32, "ExternalInput")
    bd_d = P.dram("bd", [128, 128], F32, "ExternalInput")
    h_o = P.dram("h_out", [128, KC, TPC], F32, "ExternalOutput")
    qk_o = P.dram("qk_out", [16, 128, TPC], BF16, "ExternalOutput")
    v_o = P.dram("v_out", [TPC // 128, 128, 1024], BF16, "ExternalOutput")
    gain_t = load_small(P, c, "gain", g_d[:, :, :], [128, 2, KC])
    qkg = load_small(P, c, "qkg", qkg_d[:, :], [128, 2])
    bd = load_small(P, c, "bd", bd_d[:, :], [128, 128])
    eps64 = const_tile(P, c, "eps64", 64 * EPS)
    load_resid(P, c, x_d)
    ffn_scoped(P, c, win_d, wout_d, gain_t, 0)
    store_resid(P, c, h_o)
    xn = [newt(P, "xnA%d" % i, [128, KC, 512], BF16) for i in range(2)]
    rstd = [newt(P, "rsA%d" % i, [128, 512], F32) for i in range(2)]
    wv = [newt(P, "wv%d" % i, [128, KC, 512], BF16) for i in range(2)]
    wqk = [newt(P, "wqk%d" % i, [128, KC, 512], BF16) for i in range(2)]
    qst = [newt(P, "qst%d" % i, [128, 512], BF16) for i in range(3)]
    vst = [newt(P, "vst%d" % i, [128, 1024], BF16) for i in range(2)]
    rq = [newt(P, "rq%d" % i, [128, 512], F32) for i in range(2)]
    for i in range(2):
        P.dma("pool", wv[i].ap[:], wv_d[i], wv[i], True)
    wi = 0
    si = 0
    for tb in range(TPC // 512):
        x = c.xT[tb]
        xb = xn[tb % 2]
        norm_block(P, c, x, gain_t, 1, xb, rstd[tb % 2])
        for blk in range(4):
            w = wqk[wi % 2]
            wi += 1
            P.dma("pool", w.ap[:], wqk_d[blk], w, True)
            for cc in range(4):
                ch = blk * 4 + cc
                isq = ch < 8
                pq = P.ps()
                for kc in range(KC):
                    mm(P, V(pq), V(w, w.ap[:, kc, cc * 128:(cc + 1) * 128]), V(xb, xb.ap[:, kc, :]), kc == 0, kc == KC - 1)
                r = rq[si % 2]
                if isq:
                    sumsq_rstd(P, c, [V(pq)], bd, 1.0, eps64, r, 512)
                else:
                    sumsq_rstd(P, c, [V(pq)], bd, 1.0 / 64, c.eps_t, r, 512)
                st = qst[si % 3]
                si += 1
                stt(P, V(st), V(pq), V(qkg, qkg.ap[:, (0 if isq else 1):(1 if isq else 2)]), V(r), ALU.mult, ALU.mult)
                P.dma("sp", qk_o[ch][:, tb * 512:(tb + 1) * 512], st.ap[:], st, False)
        for sub in range(4):
            vs = vst[sub % 2]
            for cb in range(2):
                pv = P.ps()
                for kc in range(KC):
                    mm(P, V(pv), V(xb, xb.ap[:, kc, sub * 128:(sub + 1) * 128]), V(wv[cb], wv[cb].ap[:, kc, :]), kc == 0, kc == KC - 1)
                cp(P, V(vs, vs.ap[:, cb * 512:(cb + 1) * 512]), V(pv), eng="act")
            P.dma("sp", v_o[tb * 4 + sub], vs.ap[:], vs, False)
    return P.finish()


def host_A(inp, li=0):
    x = inp["x"][0]
    xs = lay_resid(x)
    w = inp["da_w_in"][0]
    wq, wk, wv = w[:, :1024], w[:, 1024:2048], w[:, 2048:]
    common = {
        "w_in": lay_w_in(inp["ffn1_w_in"][li]), "w_out": lay_w_out(inp["ffn1_w_out"][li]),
        "gain": lay_gain(np.stack([inp["ffn1_norm"][li], inp["mix_norm"][li]])),
        "wqk": lay_w_cols(np.concatenate([wq, wk], axis=1), 512),
        "wv": lay_w_cols(wv, 512),
        "qkg": np.ascontiguousarray(np.stack([np.tile(inp["da_q_gain"][0], 2), np.tile(inp["da_k_gain"][0], 2)], axis=1)),
        "bd": np.kron(np.eye(2, dtype=np.float32), np.ones((64, 64), np.float32)),
    }
    return [dict(common, x=xs[c]) for c in range(NCORES)]


NQB = SEQ // 512


def build_B(lam_init, nqb=NQB):
    P = Prog()
    c = setup_common(P)
    common_bufs(P, c)
    qa_d = P.dram("qa", [2, 68, SEQ], BF16, "ExternalInput")
    ka_d = P.dram("ka", [2, 68, SEQ], BF16, "ExternalInput")
    v_d = P.dram("v", [128, SEQ // 128, 128], BF16, "ExternalInput")
    lam_d = P.dram("lam", [128, 4, 64], F32, "ExternalInput")
    sg_d = P.dram("subg", [128, 1], F32, "ExternalInput")
    tri_d = P.dram("tri", [128, 128], F32, "ExternalInput")
    o_o = P.dram("o_out", [128, SEQ], BF16, "ExternalOutput")
    P.psum_tiles()
    P.ps_pool = [4, 5, 6, 7]
    lam = load_small(P, c, "lam", lam_d[:, :, :], [128, 4, 64])
    subg = load_small(P, c, "subg", sg_d[:, :], [128, 1])
    trif = load_small(P, c, "trif", tri_d[:, :], [128, 128])
    tri = newt(P, "tri", [128, 128], BF16)
    cp(P, V(tri), V(trif))
    ones_b = newt(P, "ones_b", [128, 128], BF16)
    cp(P, V(ones_b), V(c.ones_f))
    pr = newt(P, "lpr", [128, 64], F32)
    s1 = newt(P, "ls1", [128, 1], F32)
    s2 = newt(P, "ls2", [128, 1], F32)
    nlam = newt(P, "nlam", [128, 1], F32)
    tt(P, V(pr), V(lam, lam.ap[:, 0, :]), V(lam, lam.ap[:, 1, :]), ALU.mult)
    P.op("dve", lambda e: e.reduce_sum(out=s1.ap[:], in_=pr.ap[:], axis=mybir.AxisListType.X), reads=[pr], writes=[s1])
    tt(P, V(pr), V(lam, lam.ap[:, 2, :]), V(lam, lam.ap[:, 3, :]), ALU.mult)
    P.op("dve", lambda e: e.reduce_sum(out=s2.ap[:], in_=pr.ap[:], axis=mybir.AxisListType.X), reads=[pr], writes=[s2])
    act(P, V(s1), V(s1), AF.Exp)
    act(P, V(s2), V(s2), AF.Exp)
    tt(P, V(nlam), V(s1), V(s2), ALU.subtract)
    ts(P, V(nlam), V(nlam), lam_init, -1.0, ALU.add, ALU.mult)
    one_m = 1.0 - lam_init
    epsb = const_tile(P, c, "epsb", EPS / (one_m * one_m))
    ka = [newt(P, "ka%d" % m, [68, SEQ], BF16) for m in range(2)]
    for m in range(2):
        for part in range(4):
            sl = slice(part * (SEQ // 4), (part + 1) * (SEQ // 4))
            P.dma("sp", ka[m].ap[:, sl], ka_d[m][:, sl], ka[m], True)
    vt = newt(P, "vt", [128, SEQ // 128, 128], BF16)
    for part in range(4):
        sl = slice(part * 32, (part + 1) * 32)
        P.dma("sp", vt.ap[:, sl, :], v_d[:, sl, :], vt, True)
    qb = [[newt(P, "qb%d_%d" % (i, m), [68, 512], BF16) for m in range(2)] for i in range(2)]
    pts = [newt(P, "pt%d" % i, [128, 512], BF16) for i in range(4)]
    rl = [newt(P, "rl%d" % i, [128, 512], F32) for i in range(2)]
    t0 = newt(P, "t0", [128, 512], F32)
    t1 = newt(P, "t1", [128, 512], F32)
    osum = newt(P, "osum", [128, 512], F32)
    rstd = newt(P, "rstdB", [128, 512], F32)
    ost = [newt(P, "ost%d" % i, [128, 512], BF16) for i in range(2)]
    acc = P.psum
    accLd = [newt(P, "accLd%d" % m, [128, 512], F32) for m in range(2)]
    pti = 0
    for Qi in range(nqb):
        qq = qb[Qi % 2]
        for m in range(2):
            P.dma("sp", qq[m].ap[:], qa_d[m][:, Qi * 512:(Qi + 1) * 512], qq[m], True)
        for m in range(2):
            ao, al = acc[2 * m], acc[2 * m + 1]
            nkb = 4 * Qi + 4
            pend = []
            first = {"pe": True, "dve": True}
            aL = accLd[m]

            def do_pv(item, ao=ao, al=al, nkb=nkb, first=first):
                kb, c0, pt, on_pe = item
                mm(P, V(ao, ao.ap[:, c0:512]), V(vt, vt.ap[:, kb, :]), V(pt, pt.ap[:, c0:512]), kb == 0, kb == nkb - 1)
                if on_pe:
                    mm(P, V(al), V(ones_b), V(pt), first["pe"], False)
                    first["pe"] = False
            for kb in range(nkb):
                j = kb - 4 * Qi
                c0 = 128 * j if j > 0 else 0
                S = P.ps()
                mm(P, V(S, S.ap[:, c0:512]), V(ka[m], ka[m].ap[:, kb * 128:(kb + 1) * 128]), V(qq[m], qq[m].ap[:, c0:512]), True, True)
                pt = pts[pti % 4]
                pti += 1
                act(P, V(pt, pt.ap[:, c0:512]), V(S, S.ap[:, c0:512]), AF.Exp)
                if j >= 0:
                    tt(P, V(pt, pt.ap[:, 128 * j:128 * j + 128]), V(pt, pt.ap[:, 128 * j:128 * j + 128]), V(tri), ALU.mult)
                on_pe = not (kb % 2 == 0 or j >= 0)
                if not on_pe:
                    if first["dve"]:
                        first["dve"] = False
                        cp(P, V(aL), V(pt))
                    else:
                        tt(P, V(aL, aL.ap[:, c0:512]), V(aL, aL.ap[:, c0:512]), V(pt, pt.ap[:, c0:512]), ALU.add)
                pend.append((kb, c0, pt, on_pe))
                if len(pend) > 2:
                    do_pv(pend.pop(0))
            while pend:
                do_pv(pend.pop(0))
            mm(P, V(al), V(c.ones_f), V(aL), first["pe"], True)
        recip(P, V(rl[0]), V(acc[1]))
        tt(P, V(t0), V(acc[0]), V(rl[0]), ALU.mult)
        recip(P, V(rl[1]), V(acc[3]))
        tt(P, V(t1), V(acc[2]), V(rl[1]), ALU.mult)
        stt(P, V(osum), V(t1), V(nlam, nlam.ap[:, 0:1]), V(t0), ALU.mult, ALU.add)
        sumsq_rstd(P, c, [V(osum)], c.ones_f, 1.0 / (128 * one_m * one_m), epsb, rstd, 512)
        os_ = ost[Qi % 2]
        stt(P, V(os_), V(osum), V(subg, subg.ap[:, 0:1]), V(rstd), ALU.mult, ALU.mult)
        P.dma("sp", o_o[:, Qi * 512:(Qi + 1) * 512], os_.ap[:], os_, False)
    return P.finish()


def host_B(inp, A_outs):
    import ml_dtypes
    bf = ml_dtypes.bfloat16
    qk = np.concatenate([o["qk_out"] for o in A_outs], axis=2)
    v = np.concatenate([o["v_out"].reshape(TPC, 1024) for o in A_outs], axis=0)
    pos = np.arange(SEQ)
    pb = (pos // 128).astype(np.float32)
    pr = (pos % 128).astype(np.float32)
    ones = np.ones(SEQ, np.float32)
    tri = (np.arange(128)[:, None] <= np.arange(128)[None, :]).astype(np.float32)
    lam = np.ascontiguousarray(np.broadcast_to(inp["da_lambda"][0][None], (128, 4, 64))).astype(np.float32)
    subg = np.ascontiguousarray(inp["da_subln"][0].reshape(128, 1))
    maps = []
    for h in range(NCORES):
        slope = 2.0 ** (-8.0 * (h + 1) / 8)
        qrows = np.stack([ones, ones, -slope * 128 * pb, -slope * pr]).astype(bf)
        krows = np.stack([slope * 128 * pb, slope * pr, ones, ones]).astype(bf)
        qh = qk[h].reshape(2, 64, SEQ)
        kh = qk[8 + h].reshape(2, 64, SEQ)
        qa = np.concatenate([qh, np.broadcast_to(qrows[None], (2, 4, SEQ))], axis=1)
        ka = np.concatenate([kh, np.broadcast_to(krows[None], (2, 4, SEQ))], axis=1)
        vh = v[:, h * 128:(h + 1) * 128].reshape(SEQ // 128, 128, 128).transpose(1, 0, 2)
        maps.append({"qa": np.ascontiguousarray(qa), "ka": np.ascontiguousarray(ka), "v": np.ascontiguousarray(vh),
                     "lam": lam, "subg": subg, "tri": tri})
    return maps


RT_G = [1.0 - 2.0 ** (-5.0 - h) for h in range(4)]
RT_S = 256 ** -0.5


def reload_resid(P, c, x_dram):
    P.dma("sp", c.xT_h[:], x_dram[:, :, :], c.xT_all, True)
    for t in c.xT:
        t.lw = c.xT_all.lw
        t.rd = {}


def gelu_tanh(P, c, out, z, g):
    n = out[1].shape[-1] if False else None
    if "sets" in g:
        g["i"] = g.get("i", 0) + 1
        gg_ = g["sets"][g["i"] % len(g["sets"])]
        t1, t2 = gg_["t1"], gg_["t2"]
    else:
        t1, t2 = g["t1"], g["t2"]
    w = z[1].shape[1]
    act(P, V(t1, t1.ap[:, :w]), z, AF.Square)
    ts(P, V(t1, t1.ap[:, :w]), V(t1, t1.ap[:, :w]), 0.044715, 1.0, ALU.mult, ALU.add)
    tt(P, V(t2, t2.ap[:, :w]), V(t1, t1.ap[:, :w]), z, ALU.mult)
    act(P, V(t2, t2.ap[:, :w]), V(t2, t2.ap[:, :w]), AF.Sigmoid, scale=1.5957691216057308)
    tt(P, out, V(t2, t2.ap[:, :w]), z, ALU.mult)


def build_C():
    P = Prog()
    c = setup_common(P)
    common_bufs(P, c)
    h_d = P.dram("h_in", [128, KC, TPC], F32, "ExternalInput")
    on_d = P.dram("on", [128, 8, TPC], BF16, "ExternalInput")
    wo_d = P.dram("wo", [KC, 128, 8, 128], F32, "ExternalInput")
    f2i = P.dram("f2_in", [NJ, 128, KC, 256], F32, "ExternalInput")
    f2o = P.dram("f2_out", [KC, 128, NJ, 128], F32, "ExternalInput")
    f1i = P.dram("f1_in", [NJ, 128, KC, 256], F32, "ExternalInput")
    f1o = P.dram("f1_out", [KC, 128, NJ, 128], F32, "ExternalInput")
    g_d = P.dram("gain", [128, 3, KC], F32, "ExternalInput")
    wfm_d = P.dram("wfm", [8, 128, KC, 512], F32, "ExternalInput")
    wtm_d = P.dram("wtm", [6, 128, KC, 512], F32, "ExternalInput")
    zc_d = P.dram("zc", [128, 4], F32, "ExternalInput")
    c2_d = P.dram("c2", [128, 16, 4], F32, "ExternalInput")
    h_o = P.dram("h_out", [128, KC, TPC], F32, "ExternalOutput")
    qk_o = P.dram("qkT_out", [16, 128, TPC], BF16, "ExternalOutput")
    gs_o = P.dram("gs_out", [16, 128, TPC], BF16, "ExternalOutput")
    kz_o = P.dram("kz_out", [16, 128, 1024], BF16, "ExternalOutput")
    v_o = P.dram("v_out", [16, 128, 2048], BF16, "ExternalOutput")
    R_o = P.dram("R_out", [128, 8, 512], F32, "ExternalOutput")
    gain_t = load_small(P, c, "gain", g_d[:, :, :], [128, 3, KC])
    zc = load_small(P, c, "zc", zc_d[:, :], [128, 4])
    c2 = load_small(P, c, "c2", c2_d[:, :, :], [128, 16, 4])
    load_resid(P, c, h_d)
    m0 = P.mark()
    on = newt(P, "on", [128, 8, TPC], BF16)
    P.dma("sp", on.ap[:], on_d[:, :, :], on, True)
    linear_resid(P, c, wo_d, 8, lambda tb, k: V(on, on.ap[:, k, tb * 512:(tb + 1) * 512]), 1.0, "woC")
    P.release(m0)
    ffn_scoped(P, c, f2i, f2o, gain_t, 0)
    ffn_scoped(P, c, f1i, f1o, gain_t, 1)
    store_resid(P, c, h_o)
    xn = [newt(P, "xnC%d" % i, [128, KC, 512], BF16) for i in range(2)]
    rstd = [newt(P, "rsC%d" % i, [128, 512], F32) for i in range(2)]
    wb = [newt(P, "wC%d" % i, [128, KC, 512], BF16) for i in range(3)]
    st = [newt(P, "stC%d" % i, [128, 512], BF16) for i in range(3)]
    kzst = [newt(P, "kzst%d" % i, [128, 1024], BF16) for i in range(4)]
    k2 = newt(P, "k2", [128, 4, 1024], BF16)
    vb = newt(P, "vb", [128, 4, 2048], BF16)
    Rl = newt(P, "Rl", [128, 8, 512], F32)
    wi = 0
    si = 0
    for tb in range(TPC // 512):
        x = c.xT[tb]
        xb = xn[tb % 2]
        norm_block(P, c, x, gain_t, 2, xb, rstd[tb % 2])
        for blk in range(8):
            w = wb[wi % 3]
            wi += 1
            P.dma("pool", w.ap[:], wfm_d[blk], w, True)
            for cc in range(4):
                ch = blk * 4 + cc
                pq = P.ps()
                for kc in range(KC):
                    mm(P, V(pq), V(w, w.ap[:, kc, cc * 128:(cc + 1) * 128]), V(xb, xb.ap[:, kc, :]), kc == 0, kc == KC - 1)
                s_ = st[si % 3]
                si += 1
                if ch < 16:
                    cp(P, V(s_), V(pq), eng="act")
                    P.dma("sp", qk_o[ch][:, tb * 512:(tb + 1) * 512], s_.ap[:], s_, False)
                else:
                    act(P, V(s_), V(pq), AF.Silu)
                    P.dma("sp", gs_o[ch - 16][:, tb * 512:(tb + 1) * 512], s_.ap[:], s_, False)
        for cb in range(6):
            w = wb[wi % 3]
            wi += 1
            P.dma("pool", w.ap[:], wtm_d[cb], w, True)
            for sub in range(4):
                n = tb * 4 + sub
                pv = P.ps()
                for kc in range(KC):
                    mm(P, V(pv), V(xb, xb.ap[:, kc, sub * 128:(sub + 1) * 128]), V(w, w.ap[:, kc, :]), kc == 0, kc == KC - 1)
                if cb < 2:
                    kzs = kzst[sub]
                    for hh in range(2):
                        h = cb * 2 + hh
                        ts(P, V(kzs, kzs.ap[:, cb * 512 + hh * 256:cb * 512 + (hh + 1) * 256]), V(pv, pv.ap[:, hh * 256:(hh + 1) * 256]),
                           V(zc, zc.ap[:, h:h + 1]), None, ALU.mult)
                        ts(P, V(k2, k2.ap[:, sub, cb * 512 + hh * 256:cb * 512 + (hh + 1) * 256]), V(pv, pv.ap[:, hh * 256:(hh + 1) * 256]),
                           V(c2, c2.ap[:, n, h:h + 1]), None, ALU.mult)
                    if cb == 1:
                        P.dma("sp", kz_o[n], kzs.ap[:], kzs, False)
                else:
                    cp(P, V(vb, vb.ap[:, sub, (cb - 2) * 512:(cb - 1) * 512]), V(pv), eng="act")
        for sub in range(4):
            P.dma("sp", v_o[tb * 4 + sub], vb.ap[:, sub, :], vb, False)
        for h in range(4):
            for dc in range(2):
                pr_ = P.ps()
                for sub in range(4):
                    mm(P, V(pr_), V(k2, k2.ap[:, sub, h * 256 + dc * 128:h * 256 + (dc + 1) * 128]), V(vb, vb.ap[:, sub, h * 512:(h + 1) * 512]), sub == 0, sub == 3)
                if tb == 0:
                    cp(P, V(Rl, Rl.ap[:, h * 2 + dc, :]), V(pr_))
                else:
                    tt(P, V(Rl, Rl.ap[:, h * 2 + dc, :]), V(Rl, Rl.ap[:, h * 2 + dc, :]), V(pr_), ALU.add)
    P.dma("sp", R_o[:, :, :], Rl.ap[:], Rl, False)
    return P.finish()


def host_C(inp, A_outs, B_outs):
    li = 0
    w = inp["rt_w_in"][0]
    wq, wk, wv, wg = w[:, :1024], w[:, 1024:2048], w[:, 2048:4096], w[:, 4096:]
    p = np.arange(128)
    zc = np.stack([RT_S * np.float64(RT_G[h]) ** (127 - p) for h in range(4)], axis=1).astype(np.float32)
    t = (np.arange(16)[None, :] * 128 + p[:, None])
    c2 = np.stack([RT_S * np.float64(RT_G[h]) ** (2047 - t) for h in range(4)], axis=2).astype(np.float32)
    common = {
        "wo": lay_w_rows(inp["da_w_out"][0]),
        "f2_in": lay_w_in(inp["ffn2_w_in"][0]), "f2_out": lay_w_out(inp["ffn2_w_out"][0]),
        "f1_in": lay_w_in(inp["ffn1_w_in"][1]), "f1_out": lay_w_out(inp["ffn1_w_out"][1]),
        "gain": lay_gain(np.stack([inp["ffn2_norm"][0], inp["ffn1_norm"][1], inp["mix_norm"][1]])),
        "wfm": lay_w_cols(np.concatenate([wq, wk, wg], axis=1), 512),
        "wtm": lay_w_cols(np.concatenate([wk, wv], axis=1), 512),
        "zc": zc, "c2": np.ascontiguousarray(c2),
    }
    maps = []
    for cix in range(NCORES):
        on = np.stack([B_outs[h]["o_out"][:, cix * TPC:(cix + 1) * TPC] for h in range(8)], axis=1)
        maps.append(dict(common, h_in=A_outs[cix]["h_out"], on=np.ascontiguousarray(on)))
    return maps


def build_D():
    P = Prog()
    c = setup_common(P)
    common_bufs(P, c)
    I = lambda n, s, d=F32: P.dram(n, s, d, "ExternalInput")
    O = lambda n, s, d=F32: P.dram(n, s, d, "ExternalOutput")
    h_d = I("h_in", [128, KC, TPC])
    qk_d = I("qkT", [16, 128, TPC], BF16)
    gs_d = I("gs", [16, 128, TPC], BF16)
    kz_d = I("kz", [16, 128, 1024], BF16)
    v_d = I("v", [16, 128, 2048], BF16)
    Rall_d = I("Rall", [8, 128, 8, 512])
    coef_d = I("coef", [128, 8, 4])
    decT_d = I("decT", [128, 4, 128])
    xi2_d = I("xi2", [128, 4, 2, 128])
    gch_d = I("gch", [128, 4])
    rsub_d = I("rsub", [128, 16])
    wo_rt_d = I("wo_rt", [KC, 128, 16, 128])
    fA_i, fA_o = I("fA_in", [NJ, 128, KC, 256]), I("fA_out", [KC, 128, NJ, 128])
    fB_i, fB_o = I("fB_in", [NJ, 128, KC, 256]), I("fB_out", [KC, 128, NJ, 128])
    fC_i, fC_o = I("fC_in", [NJ, 128, KC, 256]), I("fC_out", [KC, 128, NJ, 128])
    fD_i, fD_o = I("fD_in", [NJ, 128, KC, 256]), I("fD_out", [KC, 128, NJ, 128])
    g_d = I("gain", [128, 6, KC])
    wu_d = I("wu", [12, 128, KC, 256])
    wv_d = I("wv", [6, 128, KC, 512])
    vgain_d = I("vgain", [128, 3072])
    wsT_d = I("wsT", [128, 8, 128])
    bsb_d = I("bsb", [128, 8, 128])
    tri_d = I("tri", [128, 128])
    wo_sg_d = I("wo_sg", [KC, 128, 24, 128])
    wl_d = I("wl", [12, 128, KC, 256])
    y_o = O("y_rt", [16, 128, TPC], BF16)
    hscr_o = O("hscr", [128, KC, TPC])
    prod_o = O("prod", [24, 128, TPC], BF16)
    h_o = O("h_out", [128, KC, TPC])
    gg_o = O("gg_out", [12, 128, TPC], BF16)
    xb_o = O("xb_out", [12, 128, TPC])
    x2_o = O("x2_dbg", [128, KC, TPC])
    gain_t = load_small(P, c, "gain", g_d[:, :, :], [128, 6, KC])
    m0 = P.mark()
    coef = load_small(P, c, "coef", coef_d[:, :, :], [128, 8, 4])
    decT = load_small(P, c, "decT", decT_d[:, :, :], [128, 4, 128])
    xi2 = load_small(P, c, "xi2", xi2_d[:, :, :, :], [128, 4, 2, 128])
    gch = load_small(P, c, "gch", gch_d[:, :], [128, 4])
    rsub = load_small(P, c, "rsub", rsub_d[:, :], [128, 16])
    R = [newt(P, "R%d" % h, [128, 2, 512], F32) for h in range(4)]
    Rb = [newt(P, "Rb%d" % h, [128, 2, 512], BF16) for h in range(4)]
    rtmp = [newt(P, "rtmp%d" % i, [128, 2, 512], F32) for i in range(2)]
    for h in range(4):
        for j in range(8):
            t = rtmp[(h * 8 + j) % 2]
            P.dma("sp", t.ap[:], Rall_d[j][:, 2 * h:2 * h + 2, :], t, True)
            if j == 0:
                ts(P, V(R[h]), V(t), V(coef, coef.ap[:, j, h:h + 1]), None, ALU.mult)
            else:
                stt(P, V(R[h]), V(t), V(coef, coef.ap[:, j, h:h + 1]), V(R[h]), ALU.mult, ALU.add)
        cp(P, V(Rb[h]), V(R[h]), eng="act")
    qkb = newt(P, "qkb", [128, 16, 512], BF16)
    gsb = newt(P, "gsb", [128, 16, 512], BF16)
    kzb = newt(P, "kzb", [128, 4, 1024], BF16)
    vbk = newt(P, "vbk", [128, 4, 2048], BF16)
    ob = newt(P, "ob", [128, 16, 512], F32)
    atts = [newt(P, "att%d" % i, [128, 128], BF16) for i in range(2)]
    qxs = [newt(P, "qx%d" % i, [128, 2, 128], BF16) for i in range(2)]
    rstd = newt(P, "rsD", [128, 512], F32)
    yt = [newt(P, "yt%d" % i, [128, 512], F32) for i in range(2)]
    yst = [newt(P, "yst%d" % i, [128, 512], BF16) for i in range(3)]
    it = 0
    yi = 0
    for tb in range(TPC // 512):
        tsl = slice(tb * 512, (tb + 1) * 512)
        P.dma("sp", qkb.ap[:], qk_d[:, :, tsl].rearrange("c p t -> p c t"), qkb, True)
        P.dma("sp", gsb.ap[:], gs_d[:, :, tsl].rearrange("c p t -> p c t"), gsb, True)
        P.dma("sp", kzb.ap[:], kz_d[tb * 4:(tb + 1) * 4].rearrange("n p d -> p n d"), kzb, True)
        P.dma("sp", vbk.ap[:], v_d[tb * 4:(tb + 1) * 4].rearrange("n p d -> p n d"), vbk, True)
        for n in range(4):
            cols = slice(n * 128, (n + 1) * 128)
            for h in range(4):
                pa = P.ps()
                for dc in range(2):
                    mm(P, V(pa, pa.ap[:, 0:128]), V(qkb, qkb.ap[:, 8 + h * 2 + dc, cols]), V(qkb, qkb.ap[:, h * 2 + dc, cols]), dc == 0, dc == 1)
                attm = atts[it % 2]
                qx = qxs[it % 2]
                it += 1
                tt(P, V(attm), V(pa, pa.ap[:, 0:128]), V(decT, decT.ap[:, h, :]), ALU.mult)
                tt(P, V(qx), V(qkb, qkb.ap[:, h * 2:h * 2 + 2, cols]), V(xi2, xi2.ap[:, h, :, :]), ALU.mult)
                po = P.ps()
                for vc in range(4):
                    oc = V(po, po.ap[:, vc * 128:(vc + 1) * 128])
                    mm(P, oc, V(vbk, vbk.ap[:, n, h * 512 + vc * 128:h * 512 + (vc + 1) * 128]), V(attm), True, False)
                    mm(P, oc, V(Rb[h], Rb[h].ap[:, 0, vc * 128:(vc + 1) * 128]), V(qx, qx.ap[:, 0, :]), False, False)
                    mm(P, oc, V(Rb[h], Rb[h].ap[:, 1, vc * 128:(vc + 1) * 128]), V(qx, qx.ap[:, 1, :]), False, True)
                for vc in range(4):
                    cp(P, V(ob, ob.ap[:, h * 4 + vc, cols]), V(po, po.ap[:, vc * 128:(vc + 1) * 128]), eng="act")
                for dc in range(2):
                    pr_ = P.ps()
                    mm(P, V(pr_), V(kzb, kzb.ap[:, n, h * 256 + dc * 128:h * 256 + (dc + 1) * 128]), V(vbk, vbk.ap[:, n, h * 512:(h + 1) * 512]), True, True)
                    stt(P, V(R[h], R[h].ap[:, dc, :]), V(R[h], R[h].ap[:, dc, :]), V(gch, gch.ap[:, h:h + 1]), V(pr_), ALU.mult, ALU.add)
                cp(P, V(Rb[h]), V(R[h]), eng="act")
        for h in range(4):
            sumsq_rstd(P, c, [V(ob, ob.ap[:, h * 4 + vc, :]) for vc in range(4)], c.ones_f, 1.0 / 512, c.eps_t, rstd, 512)
            for vc in range(4):
                ch = h * 4 + vc
                y1 = yt[yi % 2]
                y2 = yst[yi % 3]
                yi += 1
                stt(P, V(y1), V(ob, ob.ap[:, ch, :]), V(rsub, rsub.ap[:, ch:ch + 1]), V(rstd), ALU.mult, ALU.mult)
                tt(P, V(y2), V(y1), V(gsb, gsb.ap[:, ch, :]), ALU.mult)
                P.dma("sp", y_o[ch][:, tsl], y2.ap[:], y2, False)
    P.release(m0)
    mB = P.mark()
    load_resid(P, c, h_d)
    m1 = P.mark()
    yall = newt(P, "yall", [128, 16, TPC], BF16)
    P.dma("sp", yall.ap[:], y_o[:, :, :].rearrange("c p t -> p c t"), yall, True)
    linear_resid(P, c, wo_rt_d, 16, lambda tb, k: V(yall, yall.ap[:, k, tb * 512:(tb + 1) * 512]), 1.0, "woR")
    P.release(m1)
    ffn_scoped(P, c, fA_i, fA_o, gain_t, 0)
    store_resid(P, c, x2_o)
    ffn_scoped(P, c, fB_i, fB_o, gain_t, 1)
    store_resid(P, c, hscr_o)
    P.release(mB)
    mC = P.mark()
    vgain = load_small(P, c, "vgain", vgain_d[:, :], [128, 3072])
    wsT = load_small(P, c, "wsT", wsT_d[:, :, :], [128, 8, 128])
    bsb = load_small(P, c, "bsb", bsb_d[:, :, :], [128, 8, 128])
    trif = load_small(P, c, "trif", tri_d[:, :], [128, 128])
    wsTb = newt(P, "wsTb", [128, 8, 128], BF16)
    bsb4 = newt(P, "bsb4", [128, 8, 512], F32)
    for g in range(8):
        tt(P, V(wsTb, wsTb.ap[:, g, :]), V(wsT, wsT.ap[:, g, :]), V(trif), ALU.mult)
        for s_ in range(4):
            cp(P, V(bsb4, bsb4.ap[:, g, s_ * 128:(s_ + 1) * 128]), V(bsb, bsb.ap[:, g, :]))
    xblk = newt(P, "xblk", [128, KC, 512], F32)
    xn = newt(P, "xnG", [128, KC, 512], BF16)
    rs2 = newt(P, "rsG", [128, 512], F32)
    gt = {"sets": [{"t1": newt(P, "gt1_%d" % i, [128, 512], F32), "t2": newt(P, "gt2_%d" % i, [128, 512], F32)} for i in range(3)]}
    wvb = [newt(P, "wvG%d" % i, [128, KC, 512], BF16) for i in range(2)]
    wub = [newt(P, "wuG%d" % i, [128, KC, 256], BF16) for i in range(2)]
    vg = newt(P, "vg", [128, 4, 3072], BF16)
    vtok = newt(P, "vtok", [128, 4, 3072], BF16)
    ss = newt(P, "ssG", [128, 24], F32)
    ssum = newt(P, "ssumG", [128, 4], F32)
    sqt = newt(P, "sqtG", [128, 512], F32)
    ut = [newt(P, "ut%d" % i, [128, 512], F32) for i in range(2)]
    t3 = [newt(P, "t3_%d" % i, [128, 512], F32) for i in range(2)]
    pst = [newt(P, "pst%d" % i, [128, 512], BF16) for i in range(3)]
    wi = 0
    for tb in range(TPC // 512):
        tsl = slice(tb * 512, (tb + 1) * 512)
        P.dma("sp", xblk.ap[:], hscr_o[:, :, tsl], xblk, True)
        norm_block(P, c, xblk, gain_t, 2, xn, rs2)
        for cb in range(6):
            w = wvb[wi % 2]
            wi += 1
            P.dma("pool", w.ap[:], wv_d[cb], w, True)
            for sub in range(4):
                pv = P.ps()
                for kc in range(KC):
                    mm(P, V(pv), V(xn, xn.ap[:, kc, sub * 128:(sub + 1) * 128]), V(w, w.ap[:, kc, :]), kc == 0, kc == KC - 1)
                vsl = V(vg, vg.ap[:, sub, cb * 512:(cb + 1) * 512])
                gelu_tanh(P, c, vsl, V(pv), gt)
                act(P, V(sqt), vsl, AF.Square)
                col = sub * 6 + cb
                P.op("dve", lambda e, col=col: e.reduce_sum(out=ss.ap[:, col:col + 1], in_=sqt.ap[:], axis=mybir.AxisListType.X), reads=[sqt], writes=[ss])
        for sub in range(4):
            P.op("dve", lambda e, sub=sub: e.reduce_sum(out=ssum.ap[:, sub:sub + 1], in_=ss.ap[:, sub * 6:(sub + 1) * 6], axis=mybir.AxisListType.X), reads=[ss], writes=[ssum])
        act(P, V(ssum), V(ssum), AF.Sqrt, bias=V(c.eps_t, c.eps_t.ap[:, 0:1]), scale=1.0 / 3072)
        recip(P, V(ssum), V(ssum))
        for sub in range(4):
            stt(P, V(vtok, vtok.ap[:, sub, :]), V(vg, vg.ap[:, sub, :]), V(ssum, ssum.ap[:, sub:sub + 1]), V(vgain), ALU.mult, ALU.mult)
        for cc in range(24):
            if cc % 2 == 0:
                wu = wub[(cc // 2) % 2]
                P.dma("pool", wu.ap[:], wu_d[cc // 2], wu, True)
            pu = P.ps()
            for kc in range(KC):
                mm(P, V(pu), V(wu, wu.ap[:, kc, (cc % 2) * 128:(cc % 2 + 1) * 128]), V(xn, xn.ap[:, kc, :]), kc == 0, kc == KC - 1)
            u = ut[cc % 2]
            gelu_tanh(P, c, V(u), V(pu), gt)
            pm = P.ps()
            for sub in range(4):
                mm(P, V(pm, pm.ap[:, sub * 128:(sub + 1) * 128]), V(vtok, vtok.ap[:, sub, cc * 128:(cc + 1) * 128]), V(wsTb, wsTb.ap[:, cc // 3, :]), True, True)
            t3_ = t3[cc % 2]
            tt(P, V(t3_), V(pm), V(bsb4, bsb4.ap[:, cc // 3, :]), ALU.add)
            ps_ = pst[cc % 3]
            tt(P, V(ps_), V(t3_), V(u), ALU.mult)
            P.dma("sp", prod_o[cc][:, tsl], ps_.ap[:], ps_, False)
    P.release(mC)
    load_resid(P, c, hscr_o)
    m2 = P.mark()
    pall = newt(P, "pall", [128, 24, TPC], BF16)
    P.dma("sp", pall.ap[:], prod_o[:, :, :].rearrange("c p t -> p c t"), pall, True)
    linear_resid(P, c, wo_sg_d, 24, lambda tb, k: V(pall, pall.ap[:, k, tb * 512:(tb + 1) * 512]), 1.0, "woG")
    P.release(m2)
    ffn_scoped(P, c, fC_i, fC_o, gain_t, 3)
    ffn_scoped(P, c, fD_i, fD_o, gain_t, 4)
    store_resid(P, c, h_o)
    xn2 = [newt(P, "xnL%d" % i, [128, KC, 512], BF16) for i in range(2)]
    rs3 = [newt(P, "rsL%d" % i, [128, 512], F32) for i in range(2)]
    gt2 = {"sets": [{"t1": newt(P, "gl1_%d" % i, [128, 512], F32), "t2": newt(P, "gl2_%d" % i, [128, 512], F32)} for i in range(3)]}
    wlb = [newt(P, "wlb%d" % i, [128, KC, 256], BF16) for i in range(2)]
    gst = [newt(P, "gst%d" % i, [128, 512], BF16) for i in range(2)]
    xst = [newt(P, "xst%d" % i, [128, 512], F32) for i in range(2)]
    for tb in range(TPC // 512):
        tsl = slice(tb * 512, (tb + 1) * 512)
        xb_ = xn2[tb % 2]
        norm_block(P, c, c.xT[tb], gain_t, 5, xb_, rs3[tb % 2])
        for blk in range(12):
            w = wlb[blk % 2]
            P.dma("pool", w.ap[:], wl_d[blk], w, True)
            for c2_ in range(2):
                ch = blk * 2 + c2_
                pl = P.ps()
                for kc in range(KC):
                    mm(P, V(pl), V(w, w.ap[:, kc, c2_ * 128:(c2_ + 1) * 128]), V(xb_, xb_.ap[:, kc, :]), kc == 0, kc == KC - 1)
                if ch < 12:
                    s_ = gst[ch % 2]
                    gelu_tanh(P, c, V(s_), V(pl), gt2)
                    P.dma("sp", gg_o[ch][:, tsl], s_.ap[:], s_, False)
                else:
                    s_ = xst[ch % 2]
                    cp(P, V(s_), V(pl), eng="act")
                    P.dma("sp", xb_o[ch - 12][:, tsl], s_.ap[:], s_, False)
    return P.finish()


def host_D(inp, C_outs):
    p = np.arange(128)
    G = [np.float64(g) for g in RT_G]
    decT = np.zeros((128, 4, 128), np.float32)
    xi2 = np.zeros((128, 4, 2, 128), np.float32)
    gch = np.zeros((128, 4), np.float32)
    for h in range(4):
        d = p[None, :] - p[:, None]
        decT[:, h, :] = np.where(d >= 0, RT_S * G[h] ** np.maximum(d, 0), 0.0)
        xi2[:, h, :, :] = (G[h] ** (p + 1.0))[None, None, :]
        gch[:, h] = G[h] ** 128
    tri = (p[:, None] <= p[None, :]).astype(np.float32)
    Rall = np.stack([o["R_out"] for o in C_outs])
    w = inp["sg_w_in"][0]
    wl = inp["lr_w_in"][0]
    common = {
        "Rall": Rall, "decT": decT, "xi2": xi2, "gch": gch,
        "rsub": np.ascontiguousarray(inp["rt_subln"][0].reshape(16, 128).T),
        "wo_rt": lay_w_rows(inp["rt_w_out"][0]),
        "fA_in": lay_w_in(inp["ffn2_w_in"][1]), "fA_out": lay_w_out(inp["ffn2_w_out"][1]),
        "fB_in": lay_w_in(inp["ffn1_w_in"][2]), "fB_out": lay_w_out(inp["ffn1_w_out"][2]),
        "fC_in": lay_w_in(inp["ffn2_w_in"][2]), "fC_out": lay_w_out(inp["ffn2_w_out"][2]),
        "fD_in": lay_w_in(inp["ffn1_w_in"][3]), "fD_out": lay_w_out(inp["ffn1_w_out"][3]),
        "gain": lay_gain(np.stack([inp["ffn2_norm"][1], inp["ffn1_norm"][2], inp["mix_norm"][2], inp["ffn2_norm"][2],
                                   inp["ffn1_norm"][3], inp["mix_norm"][3]])),
        "wu": lay_w_cols(w[:, :3072], 256), "wv": lay_w_cols(w[:, 3072:], 512),
        "vgain": np.ascontiguousarray(np.broadcast_to(inp["sg_v_gain"][0][None], (128, 3072))),
        "wsT": np.ascontiguousarray(inp["sg_w_s"][0].transpose(2, 0, 1)),
        "bsb": np.ascontiguousarray(np.broadcast_to(inp["sg_b_s"][0][None], (128, 8, 128))),
        "tri": tri,
        "wo_sg": lay_w_rows(inp["sg_w_out"][0]),
        "wl": lay_w_cols(wl, 256),
    }
    maps = []
    for cix in range(NCORES):
        coef = np.zeros((128, 8, 4), np.float32)
        for j in range(cix):
            for h in range(4):
                coef[:, j, h] = G[h] ** (2048.0 * (cix - 1 - j))
        o = C_outs[cix]
        maps.append(dict(common, h_in=o["h_out"], qkT=o["qkT_out"], gs=o["gs_out"], kz=o["kz_out"], v=o["v_out"], coef=coef))
    return maps


def build_E():
    P = Prog()
    c = setup_common(P)
    common_bufs(P, c)
    I = lambda n, s, d=F32: P.dram(n, s, d, "ExternalInput")
    O = lambda n, s, d=F32: P.dram(n, s, d, "ExternalOutput")
    xb_d = I("xb", [12, 128, TPC])
    halo_d = I("halo", [12, 128, 3])
    cw_d = I("cw", [128, 12, 4])
    cb_d = I("cb", [128, 12])
    wa_d = I("wa", [12, 128, 128])
    wx_d = I("wx", [12, 128, 128])
    ba_d = I("ba", [128, 12])
    bx_d = I("bx", [128, 12])
    lam_d = I("lam", [128, 12])
    hl_o = O("hloc", [12, 128, TPC])
    pc_o = O("pcum", [12, 128, TPC])
    ah_o = O("ah", [128, 12, 2])
    cw = load_small(P, c, "cw", cw_d[:, :, :], [128, 12, 4])
    cb = load_small(P, c, "cb", cb_d[:, :], [128, 12])
    ba = load_small(P, c, "ba", ba_d[:, :], [128, 12])
    bx = load_small(P, c, "bx", bx_d[:, :], [128, 12])
    lam = load_small(P, c, "lam", lam_d[:, :], [128, 12])
    sc = newt(P, "sc", [128, 12], F32)
    act(P, V(sc), V(lam), AF.Exp, scale=-1.0)
    ts(P, V(sc), V(sc), 1.0, None, ALU.add)
    act(P, V(sc), V(sc), AF.Ln)
    ts(P, V(sc), V(sc), -8.0, None, ALU.mult)
    zeros = const_tile(P, c, "zeros", 0.0, shape=(128, TPC))
    ah = newt(P, "ah", [128, 12, 2], F32)
    xpad = [newt(P, "xpad%d" % i, [128, TPC + 3], F32) for i in range(2)]
    xc = newt(P, "xc", [128, TPC], F32)
    xcb = newt(P, "xcb", [128, TPC], BF16)
    a_t = newt(P, "a_t", [128, TPC], F32)
    b_t = newt(P, "b_t", [128, TPC], F32)
    r_t = [newt(P, "r_t%d" % i, [128, 512], F32) for i in range(2)]
    i_t = [newt(P, "i_t%d" % i, [128, 512], F32) for i in range(2)]
    hl = [newt(P, "hl%d" % i, [128, TPC], F32) for i in range(2)]
    pc = [newt(P, "pc%d" % i, [128, TPC], F32) for i in range(2)]
    wab = [newt(P, "wab%d" % i, [128, 128], BF16) for i in range(2)]
    wxb = [newt(P, "wxb%d" % i, [128, 128], BF16) for i in range(2)]
    for n in range(12):
        xp = xpad[n % 2]
        P.dma("sp", xp.ap[:, 3:], xb_d[n], xp, True)
        P.dma("sp", xp.ap[:, 0:3], halo_d[n], xp, True)
        wa = wab[n % 2]
        wx = wxb[n % 2]
        P.dma("pool", wa.ap[:], wa_d[n], wa, True)
        P.dma("pool", wx.ap[:], wx_d[n], wx, True)
        ts(P, V(xc), V(xp, xp.ap[:, 0:TPC]), V(cw, cw.ap[:, n, 0:1]), V(cb, cb.ap[:, n:n + 1]), ALU.mult, ALU.add)
        for j in range(1, 4):
            stt(P, V(xc), V(xp, xp.ap[:, j:j + TPC]), V(cw, cw.ap[:, n, j:j + 1]), V(xc), ALU.mult, ALU.add)
        cp(P, V(xcb), V(xc), eng="act")
        for blk in range(4):
            sl = slice(blk * 512, (blk + 1) * 512)
            pr_ = P.ps()
            mm(P, V(pr_), V(wa), V(xcb, xcb.ap[:, sl]), True, True)
            pi_ = P.ps()
            mm(P, V(pi_), V(wx), V(xcb, xcb.ap[:, sl]), True, True)
            r = r_t[blk % 2]
            ii = i_t[blk % 2]
            act(P, V(r), V(pr_), AF.Sigmoid, bias=V(ba, ba.ap[:, n:n + 1]))
            act(P, V(ii), V(pi_), AF.Sigmoid, bias=V(bx, bx.ap[:, n:n + 1]))
            act(P, V(a_t, a_t.ap[:, sl]), V(r), AF.Exp, scale=V(sc, sc.ap[:, n:n + 1]))
            tt(P, V(r), V(a_t, a_t.ap[:, sl]), V(a_t, a_t.ap[:, sl]), ALU.mult)
            ts(P, V(r), V(r), -1.0, 1.0, ALU.mult, ALU.add)
            ts(P, V(r), V(r), 1e-12, None, ALU.max)
            act(P, V(r), V(r), AF.Sqrt)
            tt(P, V(ii), V(ii), V(xc, xc.ap[:, sl]), ALU.mult)
            tt(P, V(b_t, b_t.ap[:, sl]), V(r), V(ii), ALU.mult)
        h_ = hl[n % 2]
        p_ = pc[n % 2]
        P.op("dve", lambda e, h_=h_: e.tensor_tensor_scan(out=h_.ap[:], data0=a_t.ap[:], data1=b_t.ap[:], initial=0.0, op0=ALU.mult, op1=ALU.add),
             reads=[a_t, b_t], writes=[h_])
        P.op("dve", lambda e, p_=p_: e.tensor_tensor_scan(out=p_.ap[:], data0=a_t.ap[:], data1=zeros.ap[:], initial=1.0, op0=ALU.mult, op1=ALU.add),
             reads=[a_t, zeros], writes=[p_])
        cp(P, V(ah, ah.ap[:, n, 0:1]), V(p_, p_.ap[:, TPC - 1:TPC]))
        cp(P, V(ah, ah.ap[:, n, 1:2]), V(h_, h_.ap[:, TPC - 1:TPC]))
        P.dma("sp", hl_o[n], h_.ap[:], h_, False)
        P.dma("sp", pc_o[n], p_.ap[:], p_, False)
    P.dma("sp", ah_o[:, :, :], ah.ap[:], ah, False)
    return P.finish()


def lay_vec12(v):
    return np.ascontiguousarray(v.reshape(12, 128).T)


def host_E(inp, D_outs):
    common = {
        "cw": np.ascontiguousarray(inp["lr_conv_w"][0].reshape(4, 12, 128).transpose(2, 1, 0)),
        "cb": lay_vec12(inp["lr_conv_b"][0]),
        "wa": np.ascontiguousarray(inp["lr_w_a"][0]), "wx": np.ascontiguousarray(inp["lr_w_x"][0]),
        "ba": lay_vec12(inp["lr_b_a"][0]), "bx": lay_vec12(inp["lr_b_x"][0]), "lam": lay_vec12(inp["lr_lambda"][0]),
    }
    maps = []
    for cix in range(NCORES):
        if cix == 0:
            halo = np.zeros((12, 128, 3), np.float32)
        else:
            halo = np.ascontiguousarray(D_outs[cix - 1]["xb_out"][:, :, TPC - 3:])
        maps.append(dict(common, xb=D_outs[cix]["xb_out"], halo=halo))
    return maps


def build_F():
    P = Prog()
    c = setup_common(P)
    common_bufs(P, c)
    I = lambda n, s, d=F32: P.dram(n, s, d, "ExternalInput")
    O = lambda n, s, d=F32: P.dram(n, s, d, "ExternalOutput")
    h_d = I("h_in", [128, KC, TPC])
    hl_d = I("hloc", [12, 128, TPC])
    pc_d = I("pcum", [12, 128, TPC])
    gg_d = I("gg", [12, 128, TPC], BF16)
    ahall_d = I("ahall", [128, 8, 12, 2])
    msk_d = I("msk", [128, 8, 2])
    wo_d = I("wo", [KC, 128, 12, 128])
    f_i, f_o = I("f_in", [NJ, 128, KC, 256]), I("f_out", [KC, 128, NJ, 128])
    g_d = I("gain", [128, 1, KC])
    y_o = O("out", [128, KC, TPC])
    gain_t = load_small(P, c, "gain", g_d[:, :, :], [128, 1, KC])
    ahall = load_small(P, c, "ahall", ahall_d[:, :, :, :], [128, 8, 12, 2])
    msk = load_small(P, c, "msk", msk_d[:, :, :], [128, 8, 2])
    hs = const_tile(P, c, "hs", 0.0, shape=(128, 12))
    A_ = newt(P, "A_", [128, 12], F32)
    H_ = newt(P, "H_", [128, 12], F32)
    for j in range(8):
        ts(P, V(A_), V(ahall, ahall.ap[:, j, :, 0]), V(msk, msk.ap[:, j, 0:1]), V(msk, msk.ap[:, j, 1:2]), ALU.mult, ALU.add)
        ts(P, V(H_), V(ahall, ahall.ap[:, j, :, 1]), V(msk, msk.ap[:, j, 0:1]), None, ALU.mult)
        tt(P, V(hs), V(hs), V(A_), ALU.mult)
        tt(P, V(hs), V(hs), V(H_), ALU.add)
    load_resid(P, c, h_d)
    m0 = P.mark()
    yall = newt(P, "yallF", [128, 12, TPC], BF16)
    hlt = [newt(P, "hlt%d" % i, [128, TPC], F32) for i in range(2)]
    pct = [newt(P, "pct%d" % i, [128, TPC], F32) for i in range(2)]
    ggt = [newt(P, "ggt%d" % i, [128, TPC], BF16) for i in range(2)]
    for n in range(12):
        a, b, g = hlt[n % 2], pct[n % 2], ggt[n % 2]
        P.dma("sp", a.ap[:], hl_d[n], a, True)
        P.dma("sp", b.ap[:], pc_d[n], b, True)
        P.dma("sp", g.ap[:], gg_d[n], g, True)
        stt(P, V(a), V(b), V(hs, hs.ap[:, n:n + 1]), V(a), ALU.mult, ALU.add)
        tt(P, V(yall, yall.ap[:, n, :]), V(a), V(g), ALU.mult)
    linear_resid(P, c, wo_d, 12, lambda tb, k: V(yall, yall.ap[:, k, tb * 512:(tb + 1) * 512]), 1.0, "woF")
    P.release(m0)
    ffn_scoped(P, c, f_i, f_o, gain_t, 0)
    store_resid(P, c, y_o)
    return P.finish()


def host_F(inp, D_outs, E_outs):
    ahall = np.ascontiguousarray(np.stack([o["ah"] for o in E_outs], axis=1))
    common = {
        "ahall": ahall, "wo": lay_w_rows(inp["lr_w_out"][0]),
        "f_in": lay_w_in(inp["ffn2_w_in"][3]), "f_out": lay_w_out(inp["ffn2_w_out"][3]),
        "gain": lay_gain(inp["ffn2_norm"][3:4]),
    }
    maps = []
    for cix in range(NCORES):
        msk = np.zeros((128, 8, 2), np.float32)
        msk[:, :, 1] = 1.0
        msk[:, :cix, 0] = 1.0
        msk[:, :cix, 1] = 0.0
        maps.append(dict(common, h_in=D_outs[cix]["h_out"], hloc=E_outs[cix]["hloc"], pcum=E_outs[cix]["pcum"],
                         gg=D_outs[cix]["gg_out"], msk=msk))
    return maps


_CACHE = {}


def _prog(name, fn):
    if name not in _CACHE:
        _CACHE[name] = fn()
    return _CACHE[name]


def _run(nc, maps):
    res = run_bass_kernel_spmd(nc, maps, core_ids=list(range(NCORES)))
    return [dict(r) for r in res.results]


def kernel(**inp):
    inp = {k: np.asarray(v) for k, v in inp.items()}
    A = _run(build_A(), host_A(inp))
    B = _run(build_B(0.8 - 0.6 * 1.0), host_B(inp, A))
    C = _run(build_C(), host_C(inp, A, B))
    del B
    D_ = _run(build_D(), host_D(inp, C))
    del A, C
    E = _run(build_E(), host_E(inp, D_))
    F = _run(build_F(), host_F(inp, D_, E))
    out = unlay_resid([o["out"] for o in F])
    return out.reshape(1, SEQ, D).astype(np.float32)
```
